# Optimizing a Trainium2 kernel written in Bass

```python
import math
import jax
import jax.numpy as jnp
from jax import lax
import numpy as np

D_MODEL = 2048
BATCH = 1
SEQ = 16384
DEPTH = 4

MEM_LEN = 256
EPS = 1e-6
DN_HEADS = 8
DN_DK = 128
DN_DV = 128
DN_CONV = 4
DN_CHUNK = 64
SWA_HEADS = 8
SWA_KV_HEADS = 2
SWA_DH = 64
WINDOW = 128
SB_HEADS = 4
SB_DH = 128
SB_BLOCK = 128
X_HEADS = 4
X_DH = 128
D_FF = 4096
FFN_CONV = 3
N_BRANCH = 3

DN_QK_WIDTH = DN_HEADS * DN_DK
DN_WIDTH = DN_HEADS * DN_DV
SWA_WIDTH = SWA_HEADS * SWA_DH
SWA_KV_WIDTH = SWA_KV_HEADS * SWA_DH
SB_WIDTH = SB_HEADS * SB_DH
X_WIDTH = X_HEADS * X_DH
IN_SIZES = (2 * DN_QK_WIDTH + DN_WIDTH, DN_WIDTH, DN_HEADS, DN_HEADS, SWA_WIDTH, 2 * SWA_KV_WIDTH, 3 * SB_WIDTH, N_BRANCH * D_MODEL)
IN_COLS = sum(IN_SIZES)

kernel_name = 'hybrid_gdn_swa_stickbreak_decoder'


def rmsnorm(x, g):
    xf = x.astype(jnp.float32)
    y = xf * lax.rsqrt(jnp.mean(xf * xf, axis=-1, keepdims=True) + EPS)
    return (y * g.astype(jnp.float32)).astype(x.dtype)


def l2norm(x):
    xf = x.astype(jnp.float32)
    return xf * lax.rsqrt(jnp.sum(xf * xf, axis=-1, keepdims=True) + EPS)


def causal_dwconv(x, w):
    width = w.shape[0]
    seq = x.shape[1]
    xp = jnp.pad(x, ((0, 0), (width - 1, 0), (0, 0)))
    out = xp[:, 0:seq] * w[0]
    for i in range(1, width):
        out = out + xp[:, i:i + seq] * w[i]
    return out


def split_cols(t, sizes):
    offsets = [int(o) for o in np.cumsum(sizes)[:-1]]
    return jnp.split(t, offsets, axis=-1)


def alibi_slopes(n):
    return jnp.exp2(-8.0 * (jnp.arange(n, dtype=jnp.float32) + 1.0) / n)


def gated_delta_rule(q, k, v, g, beta):
    bsz, seq, heads, dk = q.shape
    dv = v.shape[-1]
    c = DN_CHUNK
    n = seq // c
    f32 = jnp.float32

    def chunks(t):
        return t.astype(f32).reshape(bsz, n, c, heads, -1).transpose(0, 1, 3, 2, 4)

    qc = chunks(q) * (dk ** -0.5)
    kc = chunks(k)
    vc = chunks(v)
    gc = jnp.cumsum(g.astype(f32).reshape(bsz, n, c, heads).transpose(0, 1, 3, 2), axis=-1)
    bc = beta.astype(f32).reshape(bsz, n, c, heads).transpose(0, 1, 3, 2)[..., None]
    causal = jnp.tril(jnp.ones((c, c), dtype=bool))
    strict = jnp.tril(jnp.ones((c, c), dtype=bool), -1)
    diff = gc[..., :, None] - gc[..., None, :]
    decay = jnp.where(causal, jnp.exp(jnp.where(causal, diff, 0.0)), 0.0)
    kb = kc * bc
    kk = jnp.einsum('bnhid,bnhjd->bnhij', kb, kc) * decay
    t_mat = jnp.eye(c, dtype=f32) + jnp.where(strict, kk, 0.0)
    u = lax.linalg.triangular_solve(t_mat, vc * bc, left_side=True, lower=True, unit_diagonal=True)
    w = lax.linalg.triangular_solve(t_mat, kb * jnp.exp(gc)[..., None], left_side=True, lower=True, unit_diagonal=True)
    qk = jnp.einsum('bnhid,bnhjd->bnhij', qc, kc) * decay
    q_dec = qc * jnp.exp(gc)[..., None]
    k_dec = kc * jnp.exp(gc[..., -1:] - gc)[..., None]
    g_last = jnp.exp(gc[..., -1])

    def step(state, xs):
        qk_n, qd_n, kd_n, u_n, w_n, gl_n = xs
        v_new = u_n - jnp.einsum('bhcd,bhde->bhce', w_n, state)
        o = jnp.einsum('bhcd,bhde->bhce', qd_n, state) + jnp.einsum('bhij,bhje->bhie', qk_n, v_new)
        state = state * gl_n[..., None, None] + jnp.einsum('bhcd,bhce->bhde', kd_n, v_new)
        return state, o

    xs = tuple(jnp.moveaxis(t, 1, 0) for t in (qk, q_dec, k_dec, u, w, g_last))
    state0 = jnp.zeros((bsz, heads, dk, dv), f32)
    _, o = lax.scan(step, state0, xs)
    return o.transpose(1, 0, 3, 2, 4).reshape(bsz, seq, heads, dv)


def sliding_window_gqa(q, k, v, sinks):
    bsz, seq = q.shape[:2]
    n = seq // WINDOW
    grp = SWA_HEADS // SWA_KV_HEADS
    qb = q.reshape(bsz, n, WINDOW, SWA_KV_HEADS, grp, SWA_DH)

    def band(t):
        tb = t.reshape(bsz, n, WINDOW, SWA_KV_HEADS, SWA_DH)
        prev = jnp.pad(tb, ((0, 0), (1, 0), (0, 0), (0, 0), (0, 0)))[:, :-1]
        return jnp.concatenate([prev, tb], axis=2)

    kb = band(k)
    vb = band(v)
    s = jnp.einsum('bnqhgd,bnkhd->bnhgqk', qb, kb).astype(jnp.float32) * (SWA_DH ** -0.5)
    qi = jnp.arange(WINDOW)[:, None]
    kj = jnp.arange(2 * WINDOW)[None, :]
    dist = qi + WINDOW - kj
    blk = jnp.arange(n)[:, None, None]
    valid = (dist >= 0) & (dist < WINDOW) & (blk * WINDOW + kj - WINDOW >= 0)
    slopes = alibi_slopes(SWA_HEADS).reshape(SWA_KV_HEADS, grp)[:, :, None, None]
    s = s - slopes * dist.astype(jnp.float32)
    s = jnp.where(valid[None, :, None, None], s, -jnp.inf)
    sink = sinks.astype(jnp.float32).reshape(SWA_KV_HEADS, grp)[:, :, None, None]
    m = jnp.maximum(jnp.max(s, axis=-1, keepdims=True), sink)
    p = jnp.exp(s - m)
    p = p / (jnp.sum(p, axis=-1, keepdims=True) + jnp.exp(sink - m))
    o = jnp.einsum('bnhgqk,bnkhd->bnqhgd', p.astype(vb.dtype), vb)
    return o.reshape(bsz, seq, SWA_WIDTH)


def stick_breaking_attention(q, k, v):
    bsz, seq, heads, dh = q.shape
    n = seq // SB_BLOCK
    f32 = jnp.float32
    tri_incl = jnp.tril(jnp.ones((SB_BLOCK, SB_BLOCK), f32))
    tri_blocks = jnp.tril(jnp.ones((n, n), f32), -1)
    scale = dh ** -0.5
    outs = []
    for i in range(n):
        nk = i + 1
        length = nk * SB_BLOCK
        q_blk = q[:, i * SB_BLOCK:(i + 1) * SB_BLOCK]
        k_blk = k[:, :length]
        v_blk = v[:, :length]
        z = jnp.einsum('bqhd,bshd->bhqs', q_blk, k_blk).astype(f32) * scale
        qpos = i * SB_BLOCK + jnp.arange(SB_BLOCK)
        valid = jnp.arange(length)[None, :] < qpos[:, None]
        zm = jnp.where(valid, z, -jnp.inf)
        l = jax.nn.log_sigmoid(-zm).reshape(bsz, heads, SB_BLOCK, nk, SB_BLOCK)
        rev_in = jnp.einsum('bhqnj,js->bhqns', l, tri_incl)
        after = jnp.einsum('bhqm,mn->bhqn', jnp.sum(l, axis=-1), tri_blocks[:nk, :nk])
        rev = (rev_in + after[..., None]).reshape(bsz, heads, SB_BLOCK, length)
        a = jnp.exp(zm + rev)
        outs.append(jnp.einsum('bhqs,bshd->bqhd', a.astype(v.dtype), v_blk))
    o = jnp.concatenate(outs, axis=1)
    return o.reshape(bsz, seq, heads * dh)


def hybrid_mixer(xn, w_in, dn_conv, dn_a_log, dn_dt_bias, dn_norm, swa_sinks, w_br_dn, w_br_swa, w_br_sb, w_o):
    bsz, seq, _ = xn.shape
    proj = xn @ w_in
    dn_qkv, dn_z, dn_a, dn_b, swa_q, swa_kv, sb_qkv, gates = split_cols(proj, IN_SIZES)
    dn_qkv = jax.nn.silu(causal_dwconv(dn_qkv, dn_conv))
    q, k, v = split_cols(dn_qkv, (DN_QK_WIDTH, DN_QK_WIDTH, DN_WIDTH))
    q = l2norm(q.reshape(bsz, seq, DN_HEADS, DN_DK))
    k = l2norm(k.reshape(bsz, seq, DN_HEADS, DN_DK))
    v = v.reshape(bsz, seq, DN_HEADS, DN_DV)
    g = -jnp.exp(dn_a_log.astype(jnp.float32)) * jax.nn.softplus(dn_a.astype(jnp.float32) + dn_dt_bias.astype(jnp.float32))
    beta = jax.nn.sigmoid(dn_b.astype(jnp.float32))
    o_dn = gated_delta_rule(q, k, v, g, beta)
    o_dn = rmsnorm(o_dn, dn_norm) * jax.nn.silu(dn_z.reshape(bsz, seq, DN_HEADS, DN_DV).astype(jnp.float32))
    y_dn = o_dn.reshape(bsz, seq, DN_WIDTH).astype(xn.dtype)
    swa_k, swa_v = split_cols(swa_kv, (SWA_KV_WIDTH, SWA_KV_WIDTH))
    y_swa = sliding_window_gqa(swa_q.reshape(bsz, seq, SWA_HEADS, SWA_DH), swa_k.reshape(bsz, seq, SWA_KV_HEADS, SWA_DH), swa_v.reshape(bsz, seq, SWA_KV_HEADS, SWA_DH), swa_sinks)
    sb_q, sb_k, sb_v = split_cols(sb_qkv, (SB_WIDTH, SB_WIDTH, SB_WIDTH))
    y_sb = stick_breaking_attention(sb_q.reshape(bsz, seq, SB_HEADS, SB_DH), sb_k.reshape(bsz, seq, SB_HEADS, SB_DH), sb_v.reshape(bsz, seq, SB_HEADS, SB_DH))
    gate = jax.nn.sigmoid(gates).reshape(bsz, seq, N_BRANCH, D_MODEL)
    merged = gate[:, :, 0] * (y_dn @ w_br_dn) + gate[:, :, 1] * (y_swa @ w_br_swa) + gate[:, :, 2] * (y_sb @ w_br_sb)
    return merged @ w_o


def memory_cross_attention(hn, mem_n, w_xq, w_xkv, w_xo):
    bsz, seq, _ = hn.shape
    q = (hn @ w_xq).reshape(bsz, seq, X_HEADS, X_DH)
    kv = (mem_n @ w_xkv).reshape(bsz, mem_n.shape[1], 2, X_HEADS, X_DH)
    s = jnp.einsum('bqhd,bmhd->bhqm', q, kv[:, :, 0]).astype(jnp.float32) * (X_DH ** -0.5)
    p = jax.nn.softmax(s, axis=-1)
    o = jnp.einsum('bhqm,bmhd->bqhd', p.astype(kv.dtype), kv[:, :, 1])
    return o.reshape(bsz, seq, X_WIDTH) @ w_xo


def conv_ffn(hn, w_up, ffn_conv, w_down):
    up = causal_dwconv(hn @ w_up, ffn_conv)
    gate, val = jnp.split(up, 2, axis=-1)
    return (jax.nn.silu(gate) * val) @ w_down


def setup_inputs(seed: int = 0) -> dict:
    key = jax.random.key(seed)
    ks = jax.random.split(key, 24)
    f32 = jnp.float32

    def dense(k, shape, fan_in):
        return jax.random.normal(k, shape, f32) * (fan_in ** -0.5)

    def gain(k, shape):
        return 1.0 + 0.02 * jax.random.normal(k, shape, f32)

    dt = jnp.exp(jax.random.uniform(ks[6], (DEPTH, DN_HEADS), f32, minval=math.log(1e-3), maxval=math.log(0.1)))
    return {
        'x': jax.random.normal(ks[0], (BATCH, SEQ, D_MODEL), f32),
        'mem': jax.random.normal(ks[1], (BATCH, MEM_LEN, D_MODEL), f32),
        'norm_mix': gain(ks[2], (DEPTH, D_MODEL)),
        'w_in': dense(ks[3], (DEPTH, D_MODEL, IN_COLS), D_MODEL),
        'dn_conv': dense(ks[4], (DEPTH, DN_CONV, 2 * DN_QK_WIDTH + DN_WIDTH), DN_CONV),
        'dn_a_log': jnp.log(jax.random.uniform(ks[5], (DEPTH, DN_HEADS), f32, minval=1.0, maxval=16.0)),
        'dn_dt_bias': dt + jnp.log(-jnp.expm1(-dt)),
        'dn_norm': gain(ks[7], (DEPTH, DN_DV)),
        'swa_sinks': 0.5 * jax.random.normal(ks[8], (DEPTH, SWA_HEADS), f32),
        'w_br_dn': dense(ks[9], (DEPTH, DN_WIDTH, D_MODEL), DN_WIDTH),
        'w_br_swa': dense(ks[10], (DEPTH, SWA_WIDTH, D_MODEL), SWA_WIDTH),
        'w_br_sb': dense(ks[11], (DEPTH, SB_WIDTH, D_MODEL), SB_WIDTH),
        'w_o': dense(ks[12], (DEPTH, D_MODEL, D_MODEL), D_MODEL),
        'norm_xattn': gain(ks[13], (DEPTH, D_MODEL)),
        'norm_mem': gain(ks[14], (DEPTH, D_MODEL)),
        'w_xq': dense(ks[15], (DEPTH, D_MODEL, X_WIDTH), D_MODEL),
        'w_xkv': dense(ks[16], (DEPTH, D_MODEL, 2 * X_WIDTH), D_MODEL),
        'w_xo': dense(ks[17], (DEPTH, X_WIDTH, D_MODEL), X_WIDTH),
        'norm_ffn': gain(ks[18], (DEPTH, D_MODEL)),
        'w_up': dense(ks[19], (DEPTH, D_MODEL, 2 * D_FF), D_MODEL),
        'ffn_conv': dense(ks[20], (DEPTH, FFN_CONV, 2 * D_FF), FFN_CONV),
        'w_down': dense(ks[21], (DEPTH, D_FF, D_MODEL), D_FF),
        'norm_final': gain(ks[22], (D_MODEL,)),
    }


def reference(x, mem, norm_mix, w_in, dn_conv, dn_a_log, dn_dt_bias, dn_norm, swa_sinks, w_br_dn, w_br_swa, w_br_sb, w_o, norm_xattn, norm_mem, w_xq, w_xkv, w_xo, norm_ffn, w_up, ffn_conv, w_down, norm_final):
    h = x
    for l in range(DEPTH):
        h = h + hybrid_mixer(rmsnorm(h, norm_mix[l]), w_in[l], dn_conv[l], dn_a_log[l], dn_dt_bias[l], dn_norm[l], swa_sinks[l], w_br_dn[l], w_br_swa[l], w_br_sb[l], w_o[l])
        h = h + memory_cross_attention(rmsnorm(h, norm_xattn[l]), rmsnorm(mem, norm_mem[l]), w_xq[l], w_xkv[l], w_xo[l])
        h = h + conv_ffn(rmsnorm(h, norm_ffn[l]), w_up[l], ffn_conv[l], w_down[l])
    return rmsnorm(h, norm_final)
```

```python
import numpy as np
import concourse.bass as bass
import concourse.mybir as mybir
from contextlib import ExitStack

F32 = mybir.dt.float32
BF16 = mybir.dt.bfloat16
AF = mybir.ActivationFunctionType
ALU = mybir.AluOpType
AX = mybir.AxisListType

N_DMA_SEMS = 32


class Buf:
    __slots__ = ("name", "w", "r", "excl")

    def __init__(self, name="", excl=False):
        self.name = name
        self.excl = excl
        self.w = None
        self.r = {}


class _Rec:
    def __init__(self, p, eng):
        self._p = p
        self._eng = eng

    def __getattr__(self, name):
        p, eng = self._p, self._eng

        def f(*args, reads=(), writes=(), **kw):
            return p.op(eng, lambda e: getattr(e, name)(*args, **kw), reads=reads, writes=writes)
        return f


class _Phase:
    def __init__(self, p):
        self.p = p

    def __enter__(self):
        self.saved = self.p.es
        self.stack = ExitStack()
        self.p.es = self.stack
        return self

    def __exit__(self, *a):
        self.p.barrier()
        self.p.es = self.saved
        self.stack.close()
        return False


class Prog:
    ENG = ("pe", "dve", "act", "pool", "sp")

    def __init__(self, nc, es):
        self.nc = nc
        self.es = es
        self.eng = {"pe": nc.tensor, "dve": nc.vector, "act": nc.scalar, "pool": nc.gpsimd, "sp": nc.sync}
        self.q = {e: [] for e in self.ENG}
        self.cnt = {e: 0 for e in self.ENG}
        self.sem = {e: es.enter_context(nc.semaphore("sem_" + e)) for e in self.ENG}
        self.dsem = [es.enter_context(nc.semaphore("dsem%d" % i)) for i in range(N_DMA_SEMS)]
        self.dcnt = [0] * N_DMA_SEMS
        self.dnext = 0
        self.dnext2 = [0, 0]
        self.seen = {e: {} for e in self.ENG}
        self.out_tokens = []
        self.prefix = ""
        for _e in self.ENG:
            setattr(self, _e, _Rec(self, _e))

    def sb(self, name, shape, dt):
        return self.es.enter_context(self.nc.sbuf_tensor("S_" + self.prefix + name, list(shape), dt))

    def ps(self, name, shape, dt=F32):
        return self.es.enter_context(self.nc.psum_tensor(name, list(shape), dt))

    def _semh(self, key):
        return self.sem[key] if isinstance(key, str) else self.dsem[key]

    def _waits(self, e, reads, writes):
        deps = {}
        def add(tok):
            if tok is None:
                return
            k, v = tok
            if k == e and e == "pe":
                return
            if deps.get(k, 0) < v:
                deps[k] = v
        for b in reads:
            add(b.w)
            if b.excl:
                for t in b.r.items():
                    if t[0] != e:
                        add(t)
        for b in writes:
            add(b.w)
            for t in b.r.items():
                add(t)
        out = []
        for k, v in deps.items():
            if self.seen[e].get(k, 0) >= v:
                continue
            self.seen[e][k] = v
            out.append((k, v))
        return out

    def op(self, e, fn, reads=(), writes=()):
        waits = self._waits(e, reads, writes)
        self.cnt[e] += 1
        tok = (e, self.cnt[e])
        self.q[e].append((waits, fn, (e, 1)))
        self._mark(tok, reads, writes)
        return tok

    def _mark(self, tok, reads, writes):
        for b in reads:
            if b.r.get(tok[0], 0) < tok[1]:
                b.r[tok[0]] = tok[1]
        for b in writes:
            b.w = tok
            b.r = {}

    def dma(self, e, out, in_, reads=(), writes=(), is_output=False, **kw):
        half = N_DMA_SEMS // 2
        base = 0 if e == "sp" else half
        i = base + self.dnext2[e != "sp"]
        self.dnext2[e != "sp"] = (self.dnext2[e != "sp"] + 1) % half
        waits = self._waits(e, reads, writes)
        if self.dcnt[i] > 0 and self.seen[e].get(i, 0) < self.dcnt[i]:
            waits.append((i, self.dcnt[i]))
            self.seen[e][i] = self.dcnt[i]
        self.dcnt[i] += 16
        tok = (i, self.dcnt[i])
        self.q[e].append((waits, lambda eng: eng.dma_start(out=out, in_=in_, **kw), (i, 16)))
        self._mark(tok, reads, writes)
        if is_output:
            self.out_tokens.append(tok)
        return tok

    def barrier(self):
        for e in self.ENG:
            waits = []
            for k in self.ENG:
                if self.cnt[k] > 0 and not (k == e and e in ("pe", "sp")) and self.seen[e].get(k, 0) < self.cnt[k]:
                    waits.append((k, self.cnt[k]))
                    self.seen[e][k] = self.cnt[k]
            for i in range(N_DMA_SEMS):
                if self.dcnt[i] > 0 and self.seen[e].get(i, 0) < self.dcnt[i]:
                    waits.append((i, self.dcnt[i]))
                    self.seen[e][i] = self.dcnt[i]
            if waits:
                self.q[e].append((waits, None, None))

    def phase(self):
        return _Phase(self)

    def finish(self):
        fin = []
        for (k, v) in self.out_tokens:
            fin.append((k, v))
        for e in self.ENG:
            if e != "sp" and self.cnt[e] > 0:
                fin.append((e, self.cnt[e]))
        for i in range(N_DMA_SEMS):
            if self.dcnt[i] > 0:
                fin.append((i, self.dcnt[i]))
        self.q["sp"].append((fin, None, None))
        nc = self.nc
        with nc.Block() as block:
            def mk(ename):
                def run(eng):
                    for waits, fn, inc in self.q[ename]:
                        for (k, v) in waits:
                            eng.wait_ge(self._semh(k), v)
                        if fn is not None:
                            ins = fn(eng)
                            ins.then_inc(self._semh(inc[0]), inc[1])
                return run
            if self.q["pe"]:
                block.tensor(mk("pe"))
            if self.q["dve"]:
                block.vector(mk("dve"))
            if self.q["act"]:
                block.scalar(mk("act"))
            if self.q["pool"]:
                block.gpsimd(mk("pool"))
            block.sync(mk("sp"))


from concourse.bass_utils import run_bass_kernel_spmd

D = 2048
DC = 16
EPS = 1e-6
NCORES = 8


class Ctx:
    def __init__(self, p, n_wbuf=2, wbuf_elems=8192):
        self.p = p
        nc = p.nc
        self.ps_t = [p.ps("psb%d" % i, [128, 512]) for i in range(8)]
        self.ps_b = [Buf("psb%d" % i, excl=True) for i in range(8)]
        self.ps_i = 0
        self.psl_i = 0
        self.wb_t = [p.sb("wbuf%d" % i, [128, wbuf_elems], BF16) for i in range(n_wbuf)]
        self.wb_b = [Buf("wbuf%d" % i) for i in range(n_wbuf)]
        self.wb_i = 0
        self.wbuf_elems = wbuf_elems
        self.ones_bf = p.sb("ones_bf", [128, 128], BF16)
        self.ones_f = p.sb("ones_f", [128, 128], F32)
        self.Bconst = Buf("const")
        p.pool.memset(self.ones_bf[:], 1.0, writes=[self.Bconst])
        p.pool.memset(self.ones_f[:], 1.0, writes=[self.Bconst])
        self.eps_col = p.sb("eps_col", [128, 1], F32)
        self.one_col = p.sb("one_col", [128, 1], F32)
        p.pool.memset(self.eps_col[:], EPS, writes=[self.Bconst])
        p.pool.memset(self.one_col[:], 1.0, writes=[self.Bconst])

    def psum(self):
        i = self.ps_i
        self.ps_i = (i + 1) % 6
        return self.ps_t[i], self.ps_b[i]

    def psum_long(self):
        i = 6 + self.psl_i
        self.psl_i = 1 - self.psl_i
        return self.ps_t[i], self.ps_b[i]

    def wbuf(self):
        i = self.wb_i
        self.wb_i = (i + 1) % len(self.wb_t)
        return self.wb_t[i], self.wb_b[i]


def mm(p, out, lhsT, rhs, start, stop, reads, writes):
    return p.pe.matmul(out, lhsT, rhs, start=start, stop=stop, reads=reads, writes=writes)


def linear_T(c, w_ap, K, M, rhs_fn, tiles, epilogue, mg=512):
    p = c.p
    KC = K // 128
    while KC * mg > c.wbuf_elems:
        mg //= 2
    for m0 in range(0, M, mg):
        mw = min(mg, M - m0)
        wt, wb = c.wbuf()
        view = wt[:, 0:KC * mw].rearrange("p (k m) -> p k m", m=mw)
        p.dma("pool", view, w_ap[:, m0:m0 + mw].rearrange("(k p) m -> p k m", p=128), writes=[wb])
        for mi in range(mw // 128):
            for ti, (t0, tw) in enumerate(tiles):
                ps, pb = c.psum()
                for kc in range(KC):
                    r, rb = rhs_fn(kc, ti)
                    mm(p, ps[:, 0:tw], view[:, kc, mi * 128:(mi + 1) * 128], r, kc == 0, kc == KC - 1,
                       [wb, rb, c.Bconst], [pb])
                epilogue(m0 // 128 + mi, ti, (t0, tw), ps, pb)


def rms_stats(c, src_fn, nchunks, tiles, T, dim, name):
    p = c.p
    rstd = p.sb(name + "_rstd", [128, T], F32)
    Brstd = Buf(name + "_rstd")
    sq = [p.sb(name + "_sq%d" % i, [128, 512], BF16) for i in range(2)]
    Bsq = [Buf() for i in range(2)]
    k = 0
    for ti, (t0, tw) in enumerate(tiles):
        ps, pb = c.psum()
        for ci in range(nchunks):
            s, sbuf_ = src_fn(ci, ti)
            q, qb = sq[k % 2], Bsq[k % 2]
            k += 1
            p.act.activation(out=q[:, 0:tw], in_=s, func=AF.Square,
                 reads=[sbuf_], writes=[qb])
            mm(p, ps[:, 0:tw], c.ones_bf[:], q[:, 0:tw], ci == 0, ci == nchunks - 1, [qb, c.Bconst], [pb])
        p.act.activation(out=rstd[:, t0:t0 + tw], in_=ps[:, 0:tw], func=AF.Sqrt,
                                                                  scale=1.0 / dim, bias=c.eps_col[:, 0:1],
             reads=[pb, c.Bconst], writes=[Brstd])
        p.dve.reciprocal(out=rstd[:, t0:t0 + tw], in_=rstd[:, t0:t0 + tw],
             reads=[Brstd], writes=[Brstd])
    return rstd, Brstd


class BufGrid:
    def __init__(self):
        self.d = {}

    def __getitem__(self, k):
        if k not in self.d:
            self.d[k] = Buf(str(k))
        return self.d[k]


def build_LB(T, TW, final, NPASS=1):
    nc = bass.Bass("TRN2", target_bir_lowering=False)
    tiles = [(t0, min(TW, T - t0)) for t0 in range(0, T, TW)]
    NT = len(tiles)
    TWP = max(TW, 256)

    def din(name, shape):
        return nc.dram_tensor(name, list(shape), F32, kind="ExternalInput").ap()

    hT_all = din("hT", [NPASS, D, T]); yT_all = din("yT", [NPASS, D, T]); memT = din("memT", [D, 256])
    w_gate = din("w_gate", [D, 3 * D]); w_br = din("w_br", [D, D]); w_o = din("w_o", [D, D])
    w_xq = din("w_xq", [D, 512]); w_xkv = din("w_xkv", [D, 1024]); w_xo = din("w_xo", [512, D])
    w_up = din("w_up", [D, 8192]); w_down = din("w_down", [4096, D])
    gains = din("gains", [128, 5 * 16])
    fconv = din("fconv", [128, 3 * 64])
    flag_all = din("flag", [NPASS, 128, 1])
    hout_all = nc.dram_tensor("hout", [NPASS, D, T - 2], F32, kind="ExternalOutput").ap()
    hA = nc.dram_tensor("hA", [D, T], F32, kind="Internal").ap()
    hB = nc.dram_tensor("hB", [D, T], F32, kind="Internal").ap()
    hC = nc.dram_tensor("hC", [D, T], F32, kind="Internal").ap()

    with ExitStack() as es:
        p = Prog(nc, es)
        c = Ctx(p, n_wbuf=3, wbuf_elems=4096)
        dB = {"hT": BufGrid(), "hA": BufGrid(), "hB": BufGrid(), "hC": BufGrid()}

        def one_pass(pi):
            hT = hT_all[pi]; yT = yT_all[pi]; flag = flag_all[pi]; hout = hout_all[pi]
            xn = p.sb("xn", [128, 16, T], BF16); xnB = BufGrid()
            big = p.sb("big", [128, 32, T], BF16); bigB = BufGrid()
            qo = p.sb("qo", [128, 8, T], BF16); qoB = BufGrid()
            gains_sb = p.sb("gains_sb", [128, 80], F32); fconv_sb = p.sb("fconv_sb", [128, 192], F32)
            flag_sb = p.sb("flag_sb", [128, 1], F32)
            Bpar = Buf("params")
            p.dma("sp", gains_sb[:], gains, writes=[Bpar])
            p.dma("sp", fconv_sb[:], fconv, writes=[Bpar])
            p.dma("sp", flag_sb[:], flag, writes=[Bpar])
            ssq = p.sb("ssq", [128, T], F32); Bssq = BufGrid()
            rstd = p.sb("rstd", [128, T], F32); Brstd = BufGrid()
            NST = 3
            st_h = [p.sb("st_h%d" % i, [128, TWP], F32) for i in range(NST)]; Bst_h = [Buf() for _ in range(NST)]
            st_n = [p.sb("st_n%d" % i, [128, TWP], F32) for i in range(NST)]; Bst_n = [Buf() for _ in range(NST)]
            st_q = [p.sb("st_q%d" % i, [128, TWP], F32) for i in range(2)]; Bst_q = [Buf() for _ in range(2)]
            cnt = {"h": 0, "n": 0, "q": 0}

            def stg(kind):
                arr, bufs, n = {"h": (st_h, Bst_h, NST), "n": (st_n, Bst_n, NST), "q": (st_q, Bst_q, 2)}[kind]
                i = cnt[kind] % n
                cnt[kind] += 1
                return arr[i], bufs[i]


            def compute_rstd():
                for ti, (t0, tw) in enumerate(tiles):
                    ps, pb = c.psum()
                    mm(p, ps[:, 0:tw], c.ones_f[:], ssq[:, t0:t0 + tw], True, True, [Bssq[ti], c.Bconst], [pb])
                    p.act.activation(out=rstd[:, t0:t0 + tw], in_=ps[:, 0:tw],
                                                                              func=AF.Sqrt, scale=1.0 / D, bias=c.eps_col[:, 0:1],
                         reads=[pb, c.Bconst], writes=[Brstd[ti]])
                    p.dve.reciprocal(out=rstd[:, t0:t0 + tw], in_=rstd[:, t0:t0 + tw],
                         reads=[Brstd[ti]], writes=[Brstd[ti]])

            def accum_ssq(src, srcB, ci, ti, t0, tw):
                if ci == 0:
                    p.act.activation(out=ssq[:, t0:t0 + tw], in_=src, func=AF.Square,
                         reads=[srcB], writes=[Bssq[ti]])
                else:
                    q, qb = stg("q")
                    p.act.activation(out=q[:, 0:tw], in_=src, func=AF.Square, reads=[srcB], writes=[qb])
                    p.pool.tensor_tensor(out=ssq[:, t0:t0 + tw], in0=ssq[:, t0:t0 + tw], in1=q[:, 0:tw], op=ALU.add,
                         reads=[qb, Bssq[ti]], writes=[Bssq[ti]])

            def stats_from_dram(src, srcname):
                for ti, (t0, tw) in enumerate(tiles):
                    for ci in range(16):
                        s, sb_ = stg("h")
                        p.dma("sp", s[:, 0:tw], src[ci * 128:(ci + 1) * 128, t0:t0 + tw], reads=[dB[srcname][(ci, ti)]], writes=[sb_])
                        accum_ssq(s[:, 0:tw], sb_, ci, ti, t0, tw)
                compute_rstd()

            def normalize_from_dram(src, srcname, gj):
                for ti, (t0, tw) in enumerate(tiles):
                    for ci in range(16):
                        s, sb_ = stg("h")
                        p.dma("sp", s[:, 0:tw], src[ci * 128:(ci + 1) * 128, t0:t0 + tw], reads=[dB[srcname][(ci, ti)]], writes=[sb_])
                        p.dve.scalar_tensor_tensor(
                            out=xn[:, ci, t0:t0 + tw], in0=s[:, 0:tw], scalar=gains_sb[:, gj * 16 + ci:gj * 16 + ci + 1],
                            in1=rstd[:, t0:t0 + tw], op0=ALU.mult, op1=ALU.mult,
                            reads=[sb_, Bpar, Brstd[ti]], writes=[xnB[(ci, ti)]])

            def xn_rhs(kc, ti):
                t0, tw = tiles[ti]
                return xn[:, kc, t0:t0 + tw], xnB[(kc, ti)]

            def residual_epilogue(prev, prevname, nxt, nxtname, out_dram=None):
                def ep(m, ti, tt, ps, pb):
                    t0, tw = tt
                    s, sb_ = stg("h")
                    p.dma("sp", s[:, 0:tw], prev[m * 128:(m + 1) * 128, t0:t0 + tw], reads=[dB[prevname][(m, ti)]], writes=[sb_])
                    n, nb = stg("n")
                    p.dve.tensor_tensor(out=n[:, 0:tw], in0=s[:, 0:tw], in1=ps[:, 0:tw], op=ALU.add,
                         reads=[sb_, pb], writes=[nb])
                    if nxt is not None:
                        p.dma("sp", nxt[m * 128:(m + 1) * 128, t0:t0 + tw], n[:, 0:tw], reads=[nb], writes=[dB[nxtname][(m, ti)]])
                        accum_ssq(n[:, 0:tw], nb, m, ti, t0, tw)
                    if out_dram is not None:
                        a0 = max(t0, 2)
                        p.dma("sp", out_dram[m * 128:(m + 1) * 128, a0 - 2:t0 + tw - 2], n[:, a0 - t0:tw], reads=[nb], is_output=True)
                return ep


            stats_from_dram(hT, "hT")
            normalize_from_dram(hT, "hT", 0)
            for ci in range(16):
                p.dma("pool", big[:, 16 + ci, :], yT[ci * 128:(ci + 1) * 128, :], writes=[bigB[16 + ci]])

            macc = p.sb("macc", [128, 2, T], F32); maccB = BufGrid()
            sg = [p.sb("sg%d" % i, [128, TWP], F32) for i in range(2)]; Bsg = [Buf() for _ in range(2)]
            br_rows = [(0, 8), (8, 4), (12, 4)]
            sgi = 0
            for m0 in range(0, D, 256):
                for br in range(3):
                    wg, wgb = c.wbuf()
                    vg = wg[:, 0:16 * 256].rearrange("p (k m) -> p k m", m=256)
                    p.dma("pool", vg, w_gate[:, br * D + m0:br * D + m0 + 256].rearrange("(k p) m -> p k m", p=128), writes=[wgb])
                    k0, nk = br_rows[br]
                    wz, wzb = c.wbuf()
                    vz = wz[:, 0:nk * 256].rearrange("p (k m) -> p k m", m=256)
                    p.dma("pool", vz, w_br[k0 * 128:(k0 + nk) * 128, m0:m0 + 256].rearrange("(k p) m -> p k m", p=128), writes=[wzb])
                    for mi in range(2):
                        for ti, (t0, tw) in enumerate(tiles):
                            pg, pgb = c.psum()
                            for kc in range(16):
                                mm(p, pg[:, 0:tw], vg[:, kc, mi * 128:(mi + 1) * 128], xn[:, kc, t0:t0 + tw], kc == 0, kc == 15,
                                   [wgb, xnB[(kc, ti)]], [pgb])
                            pz, pzb = c.psum()
                            for kc in range(nk):
                                mm(p, pz[:, 0:tw], vz[:, kc, mi * 128:(mi + 1) * 128], big[:, 16 + k0 + kc, t0:t0 + tw], kc == 0, kc == nk - 1,
                                   [wzb, bigB[16 + k0 + kc]], [pzb])
                            s_, sb_ = sg[sgi % 2], Bsg[sgi % 2]
                            sgi += 1
                            p.act.activation(out=s_[:, 0:tw], in_=pg[:, 0:tw], func=AF.Sigmoid,
                                 reads=[pgb], writes=[sb_])
                            mB = maccB[(mi, ti)]
                            if br == 0:
                                p.dve.tensor_tensor(
                                    out=macc[:, mi, t0:t0 + tw], in0=s_[:, 0:tw], in1=pz[:, 0:tw], op=ALU.mult,
                                    reads=[sb_, pzb], writes=[mB])
                            else:
                                p.dve.tensor_tensor(
                                    out=s_[:, 0:tw], in0=s_[:, 0:tw], in1=pz[:, 0:tw], op=ALU.mult,
                                    reads=[sb_, pzb], writes=[sb_])
                                if br == 1:
                                    p.pool.tensor_tensor(
                                        out=macc[:, mi, t0:t0 + tw], in0=macc[:, mi, t0:t0 + tw], in1=s_[:, 0:tw], op=ALU.add,
                                        reads=[sb_, mB], writes=[mB])
                                else:
                                    mc = m0 // 128 + mi
                                    p.pool.tensor_tensor(
                                        out=big[:, mc, t0:t0 + tw], in0=macc[:, mi, t0:t0 + tw], in1=s_[:, 0:tw], op=ALU.add,
                                        reads=[sb_, mB], writes=[bigB[(mc, ti)]])

            linear_T(c, w_o, D, D, lambda kc, ti: (big[:, kc, tiles[ti][0]:tiles[ti][0] + tiles[ti][1]], bigB[(kc, ti)]),
                     tiles, residual_epilogue(hT, "hT", hA, "hA"))
            compute_rstd()
            normalize_from_dram(hA, "hA", 1)

            memn = p.sb("memn", [128, 16, 256], BF16); memnB = BufGrid()
            mst = [p.sb("mst%d" % i, [128, 256], F32) for i in range(2)]; Bmst = [Buf() for _ in range(2)]
            mssq = p.sb("mssq", [128, 256], F32); Bmssq = Buf()
            mrstd = p.sb("mrstd", [128, 256], F32); Bmrstd = Buf()
            for ci in range(16):
                s, sb_ = mst[ci % 2], Bmst[ci % 2]
                p.dma("sp", s[:], memT[ci * 128:(ci + 1) * 128, :], writes=[sb_])
                if ci == 0:
                    p.act.activation(out=mssq[:], in_=s[:], func=AF.Square, reads=[sb_], writes=[Bmssq])
                else:
                    q, qb = stg("q")
                    p.act.activation(out=q[:, 0:256], in_=s[:], func=AF.Square, reads=[sb_], writes=[qb])
                    p.pool.tensor_tensor(out=mssq[:], in0=mssq[:], in1=q[:, 0:256], op=ALU.add,
                         reads=[qb, Bmssq], writes=[Bmssq])
            ps, pb = c.psum()
            mm(p, ps[:, 0:256], c.ones_f[:], mssq[:], True, True, [Bmssq, c.Bconst], [pb])
            p.act.activation(out=mrstd[:], in_=ps[:, 0:256], func=AF.Sqrt, scale=1.0 / D, bias=c.eps_col[:, 0:1],
                 reads=[pb, c.Bconst], writes=[Bmrstd])
            p.dve.reciprocal(out=mrstd[:], in_=mrstd[:], reads=[Bmrstd], writes=[Bmrstd])
            for ci in range(16):
                s, sb_ = mst[ci % 2], Bmst[ci % 2]
                p.dma("sp", s[:], memT[ci * 128:(ci + 1) * 128, :], writes=[sb_])
                p.dve.scalar_tensor_tensor(
                    out=memn[:, ci, :], in0=s[:], scalar=gains_sb[:, 2 * 16 + ci:2 * 16 + ci + 1], in1=mrstd[:],
                    op0=ALU.mult, op1=ALU.mult, reads=[sb_, Bpar, Bmrstd], writes=[memnB[ci]])
            KxT = p.sb("KxT", [128, 4, 256], BF16); KxTB = BufGrid()
            Vx = p.sb("Vx", [128, 2, 512], BF16); VxB = BufGrid()

            def ep_k(m, ti, tt, ps, pb):
                p.act.activation(out=KxT[:, m, :], in_=ps[:, 0:256], func=AF.Copy, reads=[pb], writes=[KxTB[m]])
            linear_T(c, w_xkv[:, 0:512], D, 512, lambda kc, ti: (memn[:, kc, :], memnB[kc]), [(0, 256)], ep_k)
            for half in range(2):
                wt, wb = c.wbuf()
                vw = wt[:, 0:16 * 256].rearrange("p (k m) -> p k m", m=256)
                p.dma("pool", vw, w_xkv[:, 512 + half * 256:512 + (half + 1) * 256].rearrange("(k p) m -> p k m", p=128), writes=[wb])
                for mc in range(2):
                    ps, pb = c.psum()
                    for kc in range(16):
                        mm(p, ps[:, 0:256], memn[:, kc, mc * 128:(mc + 1) * 128], vw[:, kc, :], kc == 0, kc == 15, [wb, memnB[kc]], [pb])
                    p.act.activation(out=Vx[:, mc, half * 256:(half + 1) * 256], in_=ps[:, 0:256], func=AF.Copy,
                         reads=[pb], writes=[VxB[(mc, half)]])

            def ep_q(m, ti, tt, ps, pb):
                t0, tw = tt
                p.act.activation(out=qo[:, m, t0:t0 + tw], in_=ps[:, 0:tw], func=AF.Copy, scale=128.0 ** -0.5,
                     reads=[pb], writes=[qoB[(m, ti)]])
            linear_T(c, w_xq, D, 512, xn_rhs, tiles, ep_q)
            pt = [p.sb("pt%d" % i, [128, TWP], BF16) for i in range(4)]; Bpt = [Buf() for _ in range(4)]
            rden = [p.sb("rden%d" % i, [128, TWP], F32) for i in range(2)]; Brden = [Buf() for _ in range(2)]
            k = 0
            for hd in range(4):
                for ti, (t0, tw) in enumerate(tiles):
                    pts = []
                    for mc in range(2):
                        ps, pb = c.psum()
                        mm(p, ps[:, 0:tw], KxT[:, hd, mc * 128:(mc + 1) * 128], qo[:, hd, t0:t0 + tw], True, True,
                           [KxTB[hd], qoB[(hd, ti)]], [pb])
                        e_, eb = pt[k % 4], Bpt[k % 4]
                        k += 1
                        p.act.activation(out=e_[:, 0:tw], in_=ps[:, 0:tw], func=AF.Exp,
                             reads=[pb], writes=[eb])
                        pts.append((e_, eb))
                    po, pob = c.psum()
                    pd, pdb = c.psum()
                    for mc in range(2):
                        mm(p, po[:, 0:tw], Vx[:, mc, hd * 128:(hd + 1) * 128], pts[mc][0][:, 0:tw], mc == 0, mc == 1,
                           [VxB[(mc, hd // 2)], pts[mc][1]], [pob])
                    for mc in range(2):
                        mm(p, pd[:, 0:tw], c.ones_bf[:], pts[mc][0][:, 0:tw], mc == 0, mc == 1, [c.Bconst, pts[mc][1]], [pdb])
                    r_, rb = rden[(k // 2) % 2], Brden[(k // 2) % 2]
                    p.dve.reciprocal(out=r_[:, 0:tw], in_=pd[:, 0:tw], reads=[pdb], writes=[rb])
                    p.dve.tensor_tensor(
                        out=qo[:, 4 + hd, t0:t0 + tw], in0=po[:, 0:tw], in1=r_[:, 0:tw], op=ALU.mult,
                        reads=[pob, rb], writes=[qoB[(4 + hd, ti)]])

            linear_T(c, w_xo, 512, D, lambda kc, ti: (qo[:, 4 + kc, tiles[ti][0]:tiles[ti][0] + tiles[ti][1]], qoB[(4 + kc, ti)]),
                     tiles, residual_epilogue(hA, "hA", hB, "hB"))
            compute_rstd()
            normalize_from_dram(hB, "hB", 3)

            ug = p.sb("ug", [128, T + 2], F32); uv = p.sb("uv", [128, T + 2], F32)
            Bug = BufGrid(); Buv = BufGrid()
            cg = p.sb("cg", [128, T], F32); cv = p.sb("cv", [128, T], F32); Bcg = Buf(); Bcv = Buf()
            p.pool.memset(ug[:, 0:2], 0.0, writes=[Bug["pad"]])
            p.pool.memset(uv[:, 0:2], 0.0, writes=[Buv["pad"]])
            for f in range(32):
                for which, (u, Bu, col0) in enumerate(((ug, Bug, f * 128), (uv, Buv, 4096 + f * 128))):
                    wt, wb = c.wbuf()
                    vw = wt[:, 0:16 * 128].rearrange("p (k m) -> p k m", m=128)
                    p.dma("pool", vw, w_up[:, col0:col0 + 128].rearrange("(k p) m -> p k m", p=128), writes=[wb])
                    for ti, (t0, tw) in enumerate(tiles):
                        ps, pb = c.psum()
                        for kc in range(16):
                            mm(p, ps[:, 0:tw], vw[:, kc, :], xn[:, kc, t0:t0 + tw], kc == 0, kc == 15, [wb, xnB[(kc, ti)]], [pb])
                        p.act.activation(out=u[:, 2 + t0:2 + t0 + tw], in_=ps[:, 0:tw], func=AF.Copy,
                             reads=[pb], writes=[Bu[ti]])
                    p.dve.tensor_scalar(out=u[:, 2:4], in0=u[:, 2:4], scalar1=flag_sb[:, 0:1], scalar2=None, op0=ALU.mult,
                         reads=[Bu[0], Bpar], writes=[Bu[0]])
                    cc, Bcc = (cg, Bcg) if which == 0 else (cv, Bcv)
                    ch = (col0 // 128)
                    allu = [Bu[ti] for ti in range(NT)] + [Bu["pad"], Bpar]
                    p.dve.tensor_scalar(out=cc[:, :], in0=u[:, 0:T], scalar1=fconv_sb[:, 0 * 64 + ch:0 * 64 + ch + 1],
                                                                          scalar2=None, op0=ALU.mult, reads=allu, writes=[Bcc])
                    p.dve.scalar_tensor_tensor(out=cc[:, :], in0=u[:, 1:T + 1], scalar=fconv_sb[:, 1 * 64 + ch:1 * 64 + ch + 1],
                                                                                 in1=cc[:, :], op0=ALU.mult, op1=ALU.add, reads=allu + [Bcc], writes=[Bcc])
                    p.dve.scalar_tensor_tensor(out=cc[:, :], in0=u[:, 2:T + 2], scalar=fconv_sb[:, 2 * 64 + ch:2 * 64 + ch + 1],
                                                                                 in1=cc[:, :], op0=ALU.mult, op1=ALU.add, reads=allu + [Bcc], writes=[Bcc])
                p.act.activation(out=cg[:, :], in_=cg[:, :], func=AF.Silu, reads=[Bcg], writes=[Bcg])
                p.pool.tensor_tensor(out=big[:, f, :], in0=cg[:, :], in1=cv[:, :], op=ALU.mult,
                     reads=[Bcg, Bcv], writes=[bigB[("A", f)]])

            if not final:
                linear_T(c, w_down, 4096, D, lambda kc, ti: (big[:, kc, tiles[ti][0]:tiles[ti][0] + tiles[ti][1]], bigB[("A", kc)]),
                         tiles, residual_epilogue(hB, "hB", None, None, out_dram=hout))
            else:
                linear_T(c, w_down, 4096, D, lambda kc, ti: (big[:, kc, tiles[ti][0]:tiles[ti][0] + tiles[ti][1]], bigB[("A", kc)]),
                         tiles, residual_epilogue(hB, "hB", hC, "hC"))
                compute_rstd()
                for ti, (t0, tw) in enumerate(tiles):
                    for ci in range(16):
                        s, sb_ = stg("h")
                        p.dma("sp", s[:, 0:tw], hC[ci * 128:(ci + 1) * 128, t0:t0 + tw], reads=[dB["hC"][(ci, ti)]], writes=[sb_])
                        n, nb = stg("n")
                        p.dve.scalar_tensor_tensor(
                            out=n[:, 0:tw], in0=s[:, 0:tw], scalar=gains_sb[:, 4 * 16 + ci:4 * 16 + ci + 1],
                            in1=rstd[:, t0:t0 + tw], op0=ALU.mult, op1=ALU.mult,
                            reads=[sb_, Bpar, Brstd[ti]], writes=[nb])
                        a0 = max(t0, 2)
                        p.dma("sp", hout[ci * 128:(ci + 1) * 128, a0 - 2:t0 + tw - 2], n[:, a0 - t0:tw], reads=[nb], is_output=True)
        for pi in range(NPASS):
            p.prefix = "p%d_" % pi
            with p.phase():
                one_pass(pi)
        p.finish()
    return nc


def _pcol(v, n):
    return np.ascontiguousarray(np.asarray(v, np.float32).reshape(n, 128).T)


def lb_weights(inp, l):
    f32 = np.float32
    gains = np.concatenate([_pcol(inp["norm_mix"][l], 16), _pcol(inp["norm_xattn"][l], 16), _pcol(inp["norm_mem"][l], 16),
                            _pcol(inp["norm_ffn"][l], 16), _pcol(inp["norm_final"], 16)], axis=1)
    fc = np.asarray(inp["ffn_conv"][l], f32)
    fconv = np.concatenate([_pcol(fc[i], 64) for i in range(3)], axis=1)
    return {
        "memT": np.ascontiguousarray(np.asarray(inp["mem"][0], f32).T),
        "w_gate": np.ascontiguousarray(inp["w_in"][l][:, 6416:]),
        "w_br": np.ascontiguousarray(np.concatenate([inp["w_br_dn"][l], inp["w_br_swa"][l], inp["w_br_sb"][l]], axis=0)),
        "w_o": np.ascontiguousarray(inp["w_o"][l]),
        "w_xq": np.ascontiguousarray(inp["w_xq"][l]), "w_xkv": np.ascontiguousarray(inp["w_xkv"][l]),
        "w_xo": np.ascontiguousarray(inp["w_xo"][l]),
        "w_up": np.ascontiguousarray(inp["w_up"][l]), "w_down": np.ascontiguousarray(inp["w_down"][l]),
        "gains": np.ascontiguousarray(gains), "fconv": np.ascontiguousarray(fconv),
    }


NEG = -30000.0
DN_STAGE = 9
DN_SUB = 99


def la_proj(c, S, hT, wcm_d, wtm_d, gmix_sb, Bpar, cm, tm, dB):
    p = c.p
    KC = 16
    wcm = p.sb("wcm_sb", [128, 16, 896], BF16); Bwcm = Buf()
    wtm = p.sb("wtm_sb", [128, 16, 194], BF16); Bwtm = Buf()
    for k0 in range(0, 16, 4):
        p.dma("pool", wcm[:, k0:k0 + 4, :], wcm_d[k0 * 128:(k0 + 4) * 128, :].rearrange("(k p) m -> p k m", p=128), writes=[Bwcm])
    for k0 in range(0, 16, 8):
        p.dma("pool", wtm[:, k0:k0 + 8, :], wtm_d[k0 * 128:(k0 + 8) * 128, :].rearrange("(k p) m -> p k m", p=128), writes=[Bwtm])
    hs = [p.sb("la_hs%d" % i, [128, 16, 512], F32) for i in range(2)]; Bhs = [Buf() for _ in range(2)]
    sq = p.sb("la_sq", [128, 16, 512], BF16); Bsq = Buf()
    xn = [p.sb("la_xn%d" % i, [128, 16, 512], BF16) for i in range(2)]; Bxn = [Buf() for _ in range(2)]
    rstd = p.sb("la_rstd", [128, 512], F32); Brstd = Buf()
    ev = [p.sb("la_ev%d" % i, [128, 512], F32) for i in range(4)]; Bev = [Buf() for _ in range(4)]
    evi = 0
    groups = [(g * 128, 128) for g in range(6)] + [(768, 64), (832, 64)]
    for t in range(S // 512):
        h_, hb = hs[t % 2], Bhs[t % 2]
        for k0 in range(0, 16, 4):
            p.dma("sp", h_[:, k0:k0 + 4, :], hT[k0 * 128:(k0 + 4) * 128, t * 512:(t + 1) * 512].rearrange("(k p) m -> p k m", p=128), writes=[hb])
        p.act.activation(out=sq[:], in_=h_[:], func=AF.Square, reads=[hb], writes=[Bsq])
        ps, pb = c.psum()
        for kc in range(16):
            mm(p, ps[:, :], c.ones_bf[:], sq[:, kc, :], kc == 0, kc == 15, [Bsq, c.Bconst], [pb])
        p.act.activation(out=rstd[:], in_=ps[:, :], func=AF.Sqrt, scale=1.0 / D, bias=c.eps_col[:, 0:1],
             reads=[pb, c.Bconst], writes=[Brstd])
        p.dve.reciprocal(out=rstd[:], in_=rstd[:], reads=[Brstd], writes=[Brstd])
        x_, xb = xn[t % 2], Bxn[t % 2]
        for kc in range(16):
            eng = "dve" if kc % 2 == 0 else "pool"
            if eng == "dve":
                p.dve.scalar_tensor_tensor(
                    out=x_[:, kc, :], in0=h_[:, kc, :], scalar=gmix_sb[:, kc:kc + 1], in1=rstd[:], op0=ALU.mult, op1=ALU.mult,
                    reads=[hb, Bpar, Brstd], writes=[xb])
            else:
                p.dve.scalar_tensor_tensor(
                    out=x_[:, kc, :], in0=h_[:, kc, :], scalar=gmix_sb[:, kc:kc + 1], in1=rstd[:], op0=ALU.mult, op1=ALU.mult,
                    reads=[hb, Bpar, Brstd], writes=[xb])
        for g, (c0, mw) in enumerate(groups):
            ps, pb = c.psum()
            for kc in range(16):
                mm(p, ps[0:mw, :], wcm[:, kc, c0:c0 + mw], x_[:, kc, :], kc == 0, kc == 15, [Bwcm, xb], [pb])
            e_, eb = ev[evi % 4], Bev[evi % 4]
            evi += 1
            p.act.activation(out=e_[0:mw, :], in_=ps[0:mw, :], func=AF.Copy, reads=[pb], writes=[eb])
            r0 = g * 128 if g < 6 else (768 if g == 6 else 832)
            p.dma("sp", cm[r0:r0 + mw, t * 512:(t + 1) * 512], e_[0:mw, :], reads=[eb], writes=[dB[("cm", g, t)]])
        for s4 in range(4):
            ps, pb = c.psum()
            for kc in range(16):
                mm(p, ps[:, 0:194], x_[:, kc, s4 * 128:(s4 + 1) * 128], wtm[:, kc, :], kc == 0, kc == 15, [Bwtm, xb], [pb])
            e_, eb = ev[evi % 4], Bev[evi % 4]
            evi += 1
            p.dve.tensor_copy(out=e_[:, 0:194], in_=ps[:, 0:194], reads=[pb], writes=[eb])
            p.dma("sp", tm[t * 512 + s4 * 128:t * 512 + (s4 + 1) * 128, :], e_[:, 0:194], reads=[eb], writes=[dB[("tm", t)]])


def la_swa(c, S, cm, tm, dB, swa_bias_d, scal_sb, Bpar, y_swa):
    p = c.p
    NB = S // 128
    NTL = S // 512
    QT = p.sb("swa_QT", [64, S], BF16); KT = p.sb("swa_KT", [64, S], BF16)
    V = p.sb("swa_V", [128, NB, 64], BF16)
    BQ = BufGrid(); BK = BufGrid(); BV = BufGrid()
    for t in range(NTL):
        p.dma("pool", QT[:, t * 512:(t + 1) * 512], cm[768:832, t * 512:(t + 1) * 512], reads=[dB[("cm", 6, t)]], writes=[BQ[t]])
        p.dma("pool", KT[:, t * 512:(t + 1) * 512], cm[832:896, t * 512:(t + 1) * 512], reads=[dB[("cm", 7, t)]], writes=[BK[t]])
        p.dma("pool", V[:, t * 4:(t + 1) * 4, :], tm[t * 512:(t + 1) * 512, 128:192].rearrange("(b p) d -> p b d", p=128),
              reads=[dB[("tm", t)]], writes=[BV[t]])
    bias = p.sb("swa_bias", [128, 256], F32); Bbias = Buf()
    p.dma("sp", bias[:], swa_bias_d, writes=[Bbias])
    es = p.sb("swa_es", [128, 1], F32); Bes = Buf()
    p.act.activation(out=es[:], in_=scal_sb[:, 0:1], func=AF.Exp, reads=[Bpar], writes=[Bes])
    ones64 = c.ones_bf[:, 0:64]
    tmp = [p.sb("swa_tmp%d" % i, [128, 256], F32) for i in range(2)]; Btmp = [Buf() for _ in range(2)]
    PT = [p.sb("swa_PT%d" % i, [128, 256], BF16) for i in range(3)]; BPT = [Buf() for _ in range(3)]
    den = [p.sb("swa_den%d" % i, [64, 512], F32) for i in range(2)]; Bden = [Buf() for _ in range(2)]
    yo = [p.sb("swa_yo%d" % i, [64, 512], F32) for i in range(2)]; Byo = [Buf() for _ in range(2)]
    prev = None
    po = pd = None
    for b in range(NB):
        qw = 256 if b < NB - 1 else 128
        t_q = [BQ[(b * 128) // 512], BQ[min((b * 128 + 128) // 512, NTL - 1)]]
        ps, pb = c.psum()
        mm(p, ps[:, 0:qw], KT[:, b * 128:(b + 1) * 128], QT[:, b * 128:b * 128 + qw], True, True, [BK[b // 4]] + t_q, [pb])
        tm_, tb = tmp[b % 2], Btmp[b % 2]
        p.dve.scalar_tensor_tensor(out=tm_[:, 0:qw], in0=ps[:, 0:qw], scalar=0.125, in1=bias[:, 0:qw],
                                                                           op0=ALU.mult, op1=ALU.add, reads=[pb, Bbias], writes=[tb])
        pt_, ptb = PT[b % 3], BPT[b % 3]
        p.act.activation(out=pt_[:, 0:qw], in_=tm_[:, 0:qw], func=AF.Exp, reads=[tb], writes=[ptb])
        j = b % 4
        if j == 0:
            po, pob = c.psum()
            pd, pdb = c.psum()
        if prev is not None:
            ppt, pptb, pb_ = prev
            mm(p, po[0:64, j * 128:(j + 1) * 128], V[:, b - 1, :], ppt[:, 128:256], True, False, [BV[(b - 1) // 4], pptb], [pob])
            mm(p, po[0:64, j * 128:(j + 1) * 128], V[:, b, :], pt_[:, 0:128], False, True, [BV[b // 4], ptb], [pob])
            mm(p, pd[0:64, j * 128:(j + 1) * 128], ones64, ppt[:, 128:256], True, False, [c.Bconst, pptb], [pdb])
            mm(p, pd[0:64, j * 128:(j + 1) * 128], ones64, pt_[:, 0:128], False, True, [c.Bconst, ptb], [pdb])
        else:
            mm(p, po[0:64, j * 128:(j + 1) * 128], V[:, b, :], pt_[:, 0:128], True, True, [BV[b // 4], ptb], [pob])
            mm(p, pd[0:64, j * 128:(j + 1) * 128], ones64, pt_[:, 0:128], True, True, [c.Bconst, ptb], [pdb])
        prev = (pt_, ptb, b)
        if j == 3:
            t = b // 4
            d_, db_ = den[t % 2], Bden[t % 2]
            y_, yb = yo[t % 2], Byo[t % 2]
            p.dve.tensor_scalar(out=d_[:], in0=pd[0:64, :], scalar1=es[0:64, 0:1], scalar2=None, op0=ALU.add,
                 reads=[pdb, Bes], writes=[db_])
            p.dve.reciprocal(out=d_[:], in_=d_[:], reads=[db_], writes=[db_])
            p.dve.tensor_tensor(out=y_[:], in0=po[0:64, :], in1=d_[:], op=ALU.mult,
                 reads=[pob, db_], writes=[yb])
            p.dma("sp", y_swa[:, t * 512:(t + 1) * 512], y_[:], reads=[yb], is_output=True)


def la_sb(c, S, cm, tm, dB, sbmask_d, scal_sb, Bpar, consts, y_sb):
    p = c.p
    NB = S // 128
    NTL = S // 512
    NSLOT = NTL // 2
    negtri, ident_bf, Bc2 = consts
    QT = p.sb("sb_QT", [128, S], BF16); KT = p.sb("sb_KT", [128, S], BF16)
    V = p.sb("sb_V", [128, NB, 128], BF16)
    BQ = BufGrid(); BK = BufGrid(); BV = BufGrid()
    for t in range(NTL):
        p.dma("pool", QT[:, t * 512:(t + 1) * 512], cm[512:640, t * 512:(t + 1) * 512], reads=[dB[("cm", 4, t)]], writes=[BQ[t]])
        p.dma("pool", KT[:, t * 512:(t + 1) * 512], cm[640:768, t * 512:(t + 1) * 512], reads=[dB[("cm", 5, t)]], writes=[BK[t]])
        p.dma("pool", V[:, t * 4:(t + 1) * 4, :], tm[t * 512:(t + 1) * 512, 0:128].rearrange("(b p) d -> p b d", p=128),
              reads=[dB[("tm", t)]], writes=[BV[t]])
    mask = p.sb("sb_mask", [128, 8, 512], BF16); Bmask = Buf()
    for r in range(8):
        p.dma("pool", mask[:, r, :], sbmask_d[r], writes=[Bmask])
    qs = [p.sb("sb_qs%d" % i, [128, 512], BF16) for i in range(2)]; Bqs = [Buf() for _ in range(2)]
    ebuf = [p.sb("sb_e%d" % i, [128, 512], F32) for i in range(2)]; Be = [Buf() for _ in range(2)]
    spb = [p.sb("sb_sp%d" % i, [128, 512], BF16) for i in range(3)]; Bsp = [Buf() for _ in range(3)]
    at = [p.sb("sb_at%d" % i, [128, 512], BF16) for i in range(2)]; Bat = [Buf() for _ in range(2)]
    lcum = [p.sb("sb_lc%d" % i, [128, 512], BF16) for i in range(2)]; Blc = [Buf() for _ in range(2)]
    yo = [p.sb("sb_yo%d" % i, [128, 512], F32) for i in range(2)]; Byo = [Buf() for _ in range(2)]
    negones = p.sb("sb_negones", [128, 128], BF16)
    p.pool.memset(negones[:], -1.0, writes=[Bc2])
    scale = 128.0 ** -0.5
    k = 0
    for j in range(NSLOT):
        q_, qb = qs[j % 2], Bqs[j % 2]
        p.dve.tensor_scalar(out=q_[:], in0=QT[:, (2 * j) * 512:(2 * j + 1) * 512], scalar1=scal_sb[:, 3:4],
                                                        scalar2=None, op0=ALU.mult, reads=[BQ[2 * j], Bpar], writes=[qb])
        p.dve.scalar_tensor_tensor(out=q_[:], in0=QT[:, (2 * j + 1) * 512:(2 * j + 2) * 512], scalar=scal_sb[:, 4:5],
                                                               in1=q_[:], op0=ALU.mult, op1=ALU.add, reads=[BQ[2 * j + 1], Bpar, qb], writes=[qb])
        p.dve.tensor_scalar(out=q_[:], in0=q_[:], scalar1=scale, scalar2=None, op0=ALU.mult, reads=[qb], writes=[qb])
        po, pob = c.psum_long()
        nmax = 8 * j + 7
        first = True
        for n in range(nmax, -1, -1):
            r = n - 8 * j
            pz, pzb = c.psum()
            masked = r >= 0
            mm(p, pz[:, :], KT[:, n * 128:(n + 1) * 128], q_[:], True, False, [BK[n // 4], qb], [pzb])
            if masked:
                mm(p, pz[:, :], ident_bf[:], mask[:, r, :], False, False, [Bc2, Bmask], [pzb])
            e_, eb = ebuf[k % 2], Be[k % 2]
            p.act.activation(out=e_[:], in_=pz[:, :], func=AF.Exp, reads=[pzb], writes=[eb])
            s_, sb_ = spb[k % 3], Bsp[k % 3]
            p.act.activation(out=s_[:], in_=e_[:], func=AF.Ln, bias=c.one_col[:, 0:1], reads=[eb, c.Bconst], writes=[sb_])
            mm(p, pz[:, :], negtri[:], s_[:], False, first, [Bc2, sb_], [pzb])
            if not first:
                lc_prev, lcb_prev = lcum[(k - 1) % 2], Blc[(k - 1) % 2]
                mm(p, pz[:, :], negones[:], lc_prev[:], False, True, [Bc2, lcb_prev], [pzb])
            a_, ab = at[k % 2], Bat[k % 2]
            p.act.activation(out=a_[:], in_=pz[:, :], func=AF.Exp, reads=[pzb], writes=[ab])
            mm(p, po[:, :], V[:, n, :], a_[:], first, n == 0, [BV[n // 4], ab], [pob])
            lc_, lcb = lcum[k % 2], Blc[k % 2]
            if n > 0:
                if first:
                    p.pool.tensor_copy(out=lc_[:], in_=s_[:], reads=[sb_], writes=[lcb])
                else:
                    p.pool.tensor_tensor(out=lc_[:], in0=lc_prev[:], in1=s_[:], op=ALU.add,
                         reads=[sb_, lcb_prev], writes=[lcb])
            first = False
            k += 1
        y_, yb = yo[j % 2], Byo[j % 2]
        p.dve.tensor_copy(out=y_[:], in_=po[:, :], reads=[pob], writes=[yb])
        p.dma("sp", y_sb[:, j * 512:(j + 1) * 512], y_[:], reads=[yb], is_output=True)


def la_dn(c, S, cm, tm, dB, dnconv_sb, scal_sb, Bpar, cst, y_dn):
    p = c.p
    NCH = S // 128
    NTL = S // 512
    ident, triu, mneg, mup, Bc = cst["ident_f"], cst["triu"], cst["mneg"], cst["mup"], cst["B"]
    AL = ALU

    ab = p.sb("dn_ab", [128, NCH, 2], F32); Bab = Buf()
    for t in range(NTL):
        p.dma("sp", ab[:, t * 4:(t + 1) * 4, :], tm[t * 512:(t + 1) * 512, 192:194].rearrange("(n p) c -> p n c", p=128),
              reads=[dB[("tm", t)]], writes=[Bab])
    def colt(name):
        return p.sb("dn_" + name, [128, NCH], F32)
    gcol = colt("gcol"); bcol = colt("bcol"); gccol = colt("gccol"); glast = colt("glast"); bgcol = colt("bgcol"); kdcol = colt("kdcol")
    na = p.sb("dn_na", [128, 1], F32)
    Bs = Buf()
    p.act.activation(out=na[:], in_=scal_sb[:, 1:2], func=AF.Exp, reads=[Bpar], writes=[Bs])
    p.dve.tensor_scalar(out=na[:], in0=na[:], scalar1=-1.0, scalar2=None, op0=AL.mult, reads=[Bs], writes=[Bs])
    p.act.activation(out=gcol[:], in_=ab[:, :, 0], func=AF.Exp, bias=scal_sb[:, 2:3], reads=[Bab, Bpar, Bs], writes=[Bs])
    p.act.activation(out=gcol[:], in_=gcol[:], func=AF.Ln, bias=c.one_col[:, 0:1], reads=[Bs, c.Bconst], writes=[Bs])
    p.dve.tensor_scalar(out=gcol[:], in0=gcol[:], scalar1=na[:, 0:1], scalar2=None, op0=AL.mult, reads=[Bs], writes=[Bs])
    p.act.activation(out=bcol[:], in_=ab[:, :, 1], func=AF.Exp, scale=-1.0, reads=[Bab, Bs], writes=[Bs])
    p.dve.tensor_scalar(out=bcol[:], in0=bcol[:], scalar1=1.0, scalar2=None, op0=AL.add, reads=[Bs], writes=[Bs])
    p.dve.reciprocal(out=bcol[:], in_=bcol[:], reads=[Bs], writes=[Bs])
    ps, pb = c.psum()
    mm(p, ps[:, 0:NCH], triu[:], gcol[:], True, True, [Bc, Bs], [pb])
    p.act.activation(out=gccol[:], in_=ps[:, 0:NCH], func=AF.Copy, reads=[pb, Bs], writes=[Bs])
    ps2, pb2 = c.psum()
    mm(p, ps2[:, 0:NCH], c.ones_f[:], gcol[:], True, True, [c.Bconst, Bs], [pb2])
    p.act.activation(out=glast[:], in_=ps2[:, 0:NCH], func=AF.Copy, reads=[pb2, Bs], writes=[Bs])
    p.dve.tensor_tensor(out=kdcol[:], in0=glast[:], in1=gccol[:], op=AL.subtract, reads=[Bs], writes=[Bs])
    p.act.activation(out=kdcol[:], in_=kdcol[:], func=AF.Exp, reads=[Bs], writes=[Bs])
    p.act.activation(out=glast[:], in_=glast[:], func=AF.Exp, reads=[Bs], writes=[Bs])
    p.act.activation(out=bgcol[:], in_=gccol[:], func=AF.Exp, reads=[Bs], writes=[Bs])
    p.dve.tensor_tensor(out=bgcol[:], in0=bgcol[:], in1=bcol[:], op=AL.mult, reads=[Bs], writes=[Bs])

    def f32t(name, w=128):
        return p.sb("dn_" + name, [128, w], F32)
    raw = [[f32t("raw%d_%d" % (w, i), 515) for i in range(2)] for w in range(3)]; Braw = [[Buf() for _ in range(2)] for _ in range(3)]
    cv = [[f32t("cv%d_%d" % (w, i), 512) for i in range(2)] for w in range(3)]; Bcv = [[Buf() for _ in range(2)] for _ in range(3)]
    sqt = f32t("sqt", 512); Bsqt = Buf()
    rn = f32t("rn", 512); Brn = Buf()
    zt = [f32t("zt%d" % i, 512) for i in range(2)]; Bzt = [Buf() for _ in range(2)]
    ot = [f32t("ot%d" % i, 512) for i in range(2)]; Bot = [BufGrid() for _ in range(2)]
    yt = [f32t("yt%d" % i, 512) for i in range(2)]; Byt = [Buf() for _ in range(2)]
    NSET = 2
    names = ["Ktm", "bV", "bgK", "kdec", "Gb", "Gs", "Gu", "grow", "QgT", "N", "X0", "db", "X0T", "Pa", "Pb", "Xa", "Xb", "XTa", "XTb",
             "wT", "u", "QK", "vnew"]
    sets = [{nm: f32t("%s_%d" % (nm, i)) for nm in names} for i in range(NSET)]
    Bsets = [{nm: Buf() for nm in names} for i in range(NSET)]
    Sst = [f32t("S%d" % i) for i in range(2)]; BS = [Buf() for _ in range(2)]
    p.pool.memset(Sst[0][:], 0.0, writes=[BS[0]])

    def evac(eng, out, ps, pb, ob, extra_reads=()):
        if eng == "act":
            p.act.activation(out=out, in_=ps, func=AF.Copy, reads=[pb] + list(extra_reads), writes=[ob])
        else:
            p.dve.tensor_copy(out=out, in_=ps, reads=[pb] + list(extra_reads), writes=[ob])

    def pre(n, tl):
        cc = n % 4
        s_, B_ = sets[n % NSET], Bsets[n % NSET]
        sl = slice(cc * 128, (cc + 1) * 128)
        QT, KT, Vc = cv[0][tl % 2], cv[1][tl % 2], cv[2][tl % 2]
        BQ, BK, BV = Bcv[0][tl % 2], Bcv[1][tl % 2], Bcv[2][tl % 2]
        col = slice(n, n + 1)
        ps, pb = c.psum()
        p.pe.transpose(ps[:, 0:128], KT[:, sl], ident[:], reads=[BK, Bc], writes=[pb])
        evac("act", s_["Ktm"][:], ps[:, 0:128], pb, B_["Ktm"])
        if DN_SUB == 1:
            return
        ps, pb = c.psum()
        p.pe.transpose(ps[:, 0:128], Vc[:, sl], ident[:], reads=[BV, Bc], writes=[pb])
        p.dve.tensor_scalar(out=s_["bV"][:], in0=ps[:, 0:128], scalar1=bcol[:, col], scalar2=None, op0=AL.mult,
             reads=[pb, Bs], writes=[B_["bV"]])
        if DN_SUB == 2:
            return
        p.pool.tensor_scalar(out=s_["bgK"][:], in0=s_["Ktm"][:], scalar1=bgcol[:, col], scalar2=None, op0=AL.mult,
             reads=[B_["Ktm"], Bs], writes=[B_["bgK"]])
        p.pool.tensor_scalar(out=s_["kdec"][:], in0=s_["Ktm"][:], scalar1=kdcol[:, col], scalar2=None, op0=AL.mult,
             reads=[B_["Ktm"], Bs], writes=[B_["kdec"]])
        p.pool.tensor_scalar(out=s_["Gb"][:], in0=c.ones_f[:], scalar1=gcol[:, col], scalar2=None, op0=AL.mult,
             reads=[c.Bconst, Bs], writes=[B_["Gb"]])
        if DN_SUB == 3:
            return
        pg, pgb = c.psum()
        mm(p, pg[:, 0:128], s_["Gb"][:], triu[:], True, True, [B_["Gb"], Bc], [pgb])
        if DN_SUB == 4:
            return
        p.dve.scalar_tensor_tensor(out=s_["Gs"][:], in0=pg[:, 0:128], scalar=gccol[:, col], in1=mneg[:], op0=AL.subtract, op1=AL.max,
             reads=[pgb, Bs, Bc], writes=[B_["Gs"]])
        p.act.activation(out=s_["Gs"][:], in_=s_["Gs"][:], func=AF.Exp, scale=-1.0, reads=[B_["Gs"]], writes=[B_["Gs"]])
        p.dve.scalar_tensor_tensor(out=s_["Gu"][:], in0=pg[:, 0:128], scalar=gccol[:, col], in1=mup[:], op0=AL.subtract, op1=AL.min,
             reads=[pgb, Bs, Bc], writes=[B_["Gu"]])
        p.act.activation(out=s_["Gu"][:], in_=s_["Gu"][:], func=AF.Exp, reads=[B_["Gu"]], writes=[B_["Gu"]])
        if DN_SUB == 5:
            return
        p.act.activation(out=s_["grow"][:], in_=pg[:, 0:128], func=AF.Exp, reads=[pgb], writes=[B_["grow"]])
        p.pool.tensor_tensor(out=s_["QgT"][:], in0=QT[:, sl], in1=s_["grow"][:], op=AL.mult,
             reads=[BQ, B_["grow"]], writes=[B_["QgT"]])
        if DN_SUB == 6:
            return
        ps, pb = c.psum()
        mm(p, ps[:, 0:128], KT[:, sl], KT[:, sl], True, True, [BK], [pb])
        if DN_SUB == 61:
            return
        p.dve.tensor_tensor(out=s_["N"][:], in0=ps[:, 0:128], in1=s_["Gs"][:], op=AL.mult, reads=[pb, B_["Gs"]], writes=[B_["N"]])
        if DN_SUB == 62:
            return
        p.pool.tensor_scalar(out=s_["X0"][:], in0=s_["N"][:], scalar1=bcol[:, col], scalar2=None, op0=AL.mult,
             reads=[B_["N"], Bs], writes=[B_["X0"]])
        p.pool.tensor_scalar(out=s_["db"][:], in0=ident[:], scalar1=bcol[:, col], scalar2=None, op0=AL.mult,
             reads=[Bc, Bs], writes=[B_["db"]])
        if DN_SUB == 63:
            return
        ps, pb = c.psum()
        mm(p, ps[:, 0:128], s_["N"][:], s_["db"][:], True, True, [B_["N"], B_["db"]], [pb])
        if DN_SUB == 64:
            return
        evac("act", s_["X0T"][:], ps[:, 0:128], pb, B_["X0T"])
        if DN_SUB == 65:
            return
        p.dve.tensor_tensor(out=s_["Pa"][:], in0=ident[:], in1=ps[:, 0:128], op=AL.subtract, reads=[pb, Bc], writes=[B_["Pa"]])
        if DN_SUB == 7:
            return
        X, XT, P = "X0", "X0T", "Pa"
        for k in range(1, 7):
            Xn = "Xa" if k % 2 == 1 else "Xb"
            XTn = "XTa" if k % 2 == 1 else "XTb"
            Pn = "Pb" if P == "Pa" else "Pa"
            ps, pb = c.psum()
            mm(p, ps[:, 0:128], s_[XT][:], s_[X][:], True, True, [B_[XT], B_[X]], [pb])
            evac("act", s_[Xn][:], ps[:, 0:128], pb, B_[Xn])
            if k < 6:
                ps2, pb2 = c.psum()
                mm(p, ps2[:, 0:128], s_[X][:], s_[XT][:], True, True, [B_[XT], B_[X]], [pb2])
                evac("dve", s_[XTn][:], ps2[:, 0:128], pb2, B_[XTn])
            ps3, pb3 = c.psum()
            mm(p, ps3[:, 0:128], s_[Xn][:], s_[P][:], True, True, [B_[Xn], B_[P]], [pb3])
            p.dve.tensor_tensor(out=s_[Pn][:], in0=s_[P][:], in1=ps3[:, 0:128], op=AL.add,
                 reads=[pb3, B_[P]], writes=[B_[Pn]])
            X, XT, P = Xn, XTn, Pn
            if DN_SUB == 70 + k:
                return
        ps, pb = c.psum()
        mm(p, ps[:, 0:128], s_["bgK"][:], s_[P][:], True, True, [B_["bgK"], B_[P]], [pb])
        evac("act", s_["wT"][:], ps[:, 0:128], pb, B_["wT"])
        ps, pb = c.psum()
        mm(p, ps[:, 0:128], s_[P][:], s_["bV"][:], True, True, [B_["bV"], B_[P]], [pb])
        evac("dve", s_["u"][:], ps[:, 0:128], pb, B_["u"])
        ps, pb = c.psum()
        mm(p, ps[:, 0:128], KT[:, sl], QT[:, sl], True, True, [BK, BQ], [pb])
        p.dve.tensor_tensor(out=s_["QK"][:], in0=ps[:, 0:128], in1=s_["Gu"][:], op=AL.mult, reads=[pb, B_["Gu"]], writes=[B_["QK"]])

    def seq(n, tl):
        cc = n % 4
        s_, B_ = sets[n % NSET], Bsets[n % NSET]
        S0, BS0 = Sst[n % 2], BS[n % 2]
        S1, BS1 = Sst[(n + 1) % 2], BS[(n + 1) % 2]
        col = slice(n, n + 1)
        ps, pb = c.psum()
        mm(p, ps[:, 0:128], s_["wT"][:], S0[:], True, True, [B_["wT"], BS0], [pb])
        p.dve.tensor_tensor(out=s_["vnew"][:], in0=s_["u"][:], in1=ps[:, 0:128], op=AL.subtract, reads=[pb, B_["u"]], writes=[B_["vnew"]])
        po, pob = c.psum()
        mm(p, po[:, 0:128], S0[:], s_["QgT"][:], True, False, [BS0, B_["QgT"]], [pob])
        mm(p, po[:, 0:128], s_["vnew"][:], s_["QK"][:], False, True, [B_["vnew"], B_["QK"]], [pob])
        o_, ob = ot[tl % 2], Bot[tl % 2][cc]
        p.act.activation(out=o_[:, cc * 128:(cc + 1) * 128], in_=po[:, 0:128], func=AF.Copy, reads=[pob], writes=[ob])
        pk, pkb = c.psum()
        mm(p, pk[:, 0:128], s_["kdec"][:], s_["vnew"][:], True, True, [B_["kdec"], B_["vnew"]], [pkb])
        p.dve.scalar_tensor_tensor(out=S1[:], in0=S0[:], scalar=glast[:, col], in1=pk[:, 0:128], op0=AL.mult, op1=AL.add,
             reads=[pkb, BS0, Bs], writes=[BS1])

    def tile_front(tl):
        i2 = tl % 2
        for w in range(3):
            r_, rb = raw[w][i2], Braw[w][i2]
            if tl == 0:
                p.pool.memset(r_[:, 0:3], 0.0, writes=[rb])
                p.dma("sp", r_[:, 3:515], cm[w * 128:(w + 1) * 128, 0:512], reads=[dB[("cm", w, 0)]], writes=[rb])
            else:
                p.dma("sp", r_[:, 0:515], cm[w * 128:(w + 1) * 128, tl * 512 - 3:(tl + 1) * 512],
                      reads=[dB[("cm", w, tl)], dB[("cm", w, tl - 1)]], writes=[rb])
            c_, cb = cv[w][i2], Bcv[w][i2]
            eng = "dve" if w != 1 else "pool"
            if eng == "dve":
                p.dve.tensor_scalar(out=c_[:], in0=r_[:, 0:512], scalar1=dnconv_sb[:, w:w + 1], scalar2=None, op0=AL.mult,
                     reads=[rb, Bpar], writes=[cb])
                for i in range(1, 4):
                    p.dve.scalar_tensor_tensor(out=c_[:], in0=r_[:, i:i + 512], scalar=dnconv_sb[:, i * 3 + w:i * 3 + w + 1],
                                                                                       in1=c_[:], op0=AL.mult, op1=AL.add, reads=[rb, Bpar, cb], writes=[cb])
            else:
                p.pool.tensor_scalar(out=c_[:], in0=r_[:, 0:512], scalar1=dnconv_sb[:, w:w + 1], scalar2=None, op0=AL.mult,
                     reads=[rb, Bpar], writes=[cb])
                for i in range(1, 4):
                    p.dve.scalar_tensor_tensor(out=c_[:], in0=r_[:, i:i + 512], scalar=dnconv_sb[:, i * 3 + w:i * 3 + w + 1],
                                                                                       in1=c_[:], op0=AL.mult, op1=AL.add, reads=[rb, Bpar, cb], writes=[cb])
            p.act.activation(out=c_[:], in_=c_[:], func=AF.Silu, reads=[cb], writes=[cb])
            if w < 2:
                p.act.activation(out=sqt[:], in_=c_[:], func=AF.Square, reads=[cb], writes=[Bsqt])
                ps, pb = c.psum()
                mm(p, ps[:, :], c.ones_f[:], sqt[:], True, True, [Bsqt, c.Bconst], [pb])
                p.act.activation(out=rn[:], in_=ps[:, :], func=AF.Sqrt, bias=c.eps_col[:, 0:1], reads=[pb, c.Bconst], writes=[Brn])
                p.dve.reciprocal(out=rn[:], in_=rn[:], reads=[Brn], writes=[Brn])
                sc = 128.0 ** -0.5 if w == 0 else 1.0
                p.dve.scalar_tensor_tensor(out=c_[:], in0=c_[:], scalar=sc, in1=rn[:], op0=AL.mult, op1=AL.mult,
                     reads=[cb, Brn], writes=[cb])
        z_, zb = zt[i2], Bzt[i2]
        p.dma("sp", z_[:], cm[384:512, tl * 512:(tl + 1) * 512], reads=[dB[("cm", 3, tl)]], writes=[zb])
        p.act.activation(out=z_[:], in_=z_[:], func=AF.Silu, reads=[zb], writes=[zb])

    def tile_back(tl):
        i2 = tl % 2
        o_ = ot[i2]
        allo = [Bot[i2][cc] for cc in range(4)]
        p.act.activation(out=sqt[:], in_=o_[:], func=AF.Square, reads=allo, writes=[Bsqt])
        ps, pb = c.psum()
        mm(p, ps[:, :], c.ones_f[:], sqt[:], True, True, [Bsqt, c.Bconst], [pb])
        p.act.activation(out=rn[:], in_=ps[:, :], func=AF.Sqrt, scale=1.0 / 128.0, bias=c.eps_col[:, 0:1],
             reads=[pb, c.Bconst], writes=[Brn])
        p.dve.reciprocal(out=rn[:], in_=rn[:], reads=[Brn], writes=[Brn])
        y_, yb = yt[i2], Byt[i2]
        p.dve.scalar_tensor_tensor(out=y_[:], in0=o_[:], scalar=scal_sb[:, 5:6], in1=rn[:], op0=AL.mult, op1=AL.mult,
             reads=allo + [Bpar, Brn], writes=[yb])
        p.pool.tensor_tensor(out=y_[:], in0=y_[:], in1=zt[i2][:], op=AL.mult, reads=[yb, Bzt[i2]], writes=[yb])
        p.dma("sp", y_dn[:, tl * 512:(tl + 1) * 512], y_[:], reads=[yb], is_output=True)

    if DN_STAGE == 0:
        return
    tile_front(0)
    if DN_STAGE == 1:
        return
    pre(0, 0)
    if DN_STAGE == 2:
        return
    for n in range(NCH):
        tl = n // 4
        if n + 1 < NCH:
            tl1 = (n + 1) // 4
            if (n + 1) % 4 == 0:
                tile_front(tl1)
            pre(n + 1, tl1)
        seq(n, tl)
        if n % 4 == 3:
            tile_back(tl)


def build_LA(S, phases="pwsd"):
    nc = bass.Bass("TRN2", target_bir_lowering=False)
    NSLOT = S // 1024

    def din(name, shape):
        return nc.dram_tensor(name, list(shape), F32, kind="ExternalInput").ap()

    hT = din("hT", [D, S]); wcm_d = din("wcm", [D, 896]); wtm_d = din("wtm", [D, 194])
    swa_bias_d = din("swa_bias", [128, 256]); sbmask_d = din("sbmask", [8, 128, 512])
    scal_d = din("scal", [128, 8]); dnconv_d = din("dnconv", [128, 12]); gmix_d = din("gmix", [128, 16])
    cst_d = din("cst", [128, 4, 128])
    y_dn = nc.dram_tensor("y_dn", [128, S], F32, kind="ExternalOutput").ap()
    y_swa = nc.dram_tensor("y_swa", [64, S], F32, kind="ExternalOutput").ap()
    y_sb = nc.dram_tensor("y_sb", [128, NSLOT * 512], F32, kind="ExternalOutput").ap()
    cm = nc.dram_tensor("cm", [896, S], F32, kind="Internal").ap()
    tm = nc.dram_tensor("tm", [S, 194], F32, kind="Internal").ap()
    with ExitStack() as es:
        p = Prog(nc, es)
        c = Ctx(p, n_wbuf=0)
        scal_sb = p.sb("scal_sb", [128, 8], F32); dnconv_sb = p.sb("dnconv_sb", [128, 12], F32); gmix_sb = p.sb("gmix_sb", [128, 16], F32)
        cst_sb = p.sb("cst_sb", [128, 4, 128], F32)
        ident_bf = p.sb("ident_bf", [128, 128], BF16); negtri = p.sb("negtri", [128, 128], BF16)
        Bpar = Buf(); Bc = Buf()
        p.dma("sp", scal_sb[:], scal_d, writes=[Bpar])
        p.dma("sp", dnconv_sb[:], dnconv_d, writes=[Bpar])
        p.dma("sp", gmix_sb[:], gmix_d, writes=[Bpar])
        p.dma("sp", cst_sb[:], cst_d, writes=[Bc])
        p.dve.tensor_copy(out=ident_bf[:], in_=cst_sb[:, 0, :], reads=[Bc], writes=[Bc])
        p.dve.tensor_tensor(out=negtri[:], in0=cst_sb[:, 1, :], in1=cst_sb[:, 0, :], op=ALU.subtract, reads=[Bc], writes=[Bc])
        p.dve.tensor_scalar(out=negtri[:], in0=negtri[:], scalar1=-1.0, scalar2=None, op0=ALU.add, reads=[Bc], writes=[Bc])
        dB = BufGrid()
        cst = {"ident_f": cst_sb[:, 0, :], "triu": cst_sb[:, 1, :], "mneg": cst_sb[:, 2, :], "mup": cst_sb[:, 3, :], "B": Bc}
        if "p" in phases:
            with p.phase():
                la_proj(c, S, hT, wcm_d, wtm_d, gmix_sb, Bpar, cm, tm, dB)
        if "w" in phases:
            with p.phase():
                la_swa(c, S, cm, tm, dB, swa_bias_d, scal_sb, Bpar, y_swa)
        if "s" in phases:
            with p.phase():
                la_sb(c, S, cm, tm, dB, sbmask_d, scal_sb, Bpar, (negtri, ident_bf, Bc), y_sb)
        if "d" in phases:
            with p.phase():
                la_dn(c, S, cm, tm, dB, dnconv_sb, scal_sb, Bpar, cst, y_dn)
        p.finish()
    return nc


def la_consts():
    i = np.arange(128)
    ident = (i[:, None] == i[None, :]).astype(np.float32)
    triu = (i[:, None] <= i[None, :]).astype(np.float32)
    mneg = np.where(i[:, None] > i[None, :], 0.0, -NEG).astype(np.float32)
    mup = np.where(i[None, :] >= i[:, None], 0.0, NEG).astype(np.float32)
    return np.ascontiguousarray(np.stack([ident, triu, mneg, mup], axis=1))


def la_core_inputs(inp, l, cidx):
    f32 = np.float32
    w = np.asarray(inp["w_in"][l], f32)
    hd = cidx; kvh = cidx // 4; sbh = cidx // 2; pr = cidx % 2
    cols = []
    cols.append(w[:, hd * 128:(hd + 1) * 128])
    cols.append(w[:, 1024 + hd * 128:1024 + (hd + 1) * 128])
    cols.append(w[:, 2048 + hd * 128:2048 + (hd + 1) * 128])
    cols.append(w[:, 3072 + hd * 128:3072 + (hd + 1) * 128])
    cols.append(w[:, 4880 + sbh * 128:4880 + (sbh + 1) * 128])
    cols.append(w[:, 5392 + sbh * 128:5392 + (sbh + 1) * 128])
    cols.append(w[:, 4112 + hd * 64:4112 + (hd + 1) * 64])
    cols.append(w[:, 4624 + kvh * 64:4624 + (kvh + 1) * 64])
    wcm = np.ascontiguousarray(np.concatenate(cols, axis=1))
    wtm = np.ascontiguousarray(np.concatenate([w[:, 5904 + sbh * 128:5904 + (sbh + 1) * 128],
                                               w[:, 4752 + kvh * 64:4752 + (kvh + 1) * 64],
                                               w[:, 4096 + hd:4097 + hd], w[:, 4104 + hd:4105 + hd]], axis=1))
    kp = np.arange(128)[:, None]; qf = np.arange(256)[None, :]
    slope = 2.0 ** (-(cidx + 1.0))
    dist = (qf - kp).astype(f32)
    swa_bias = np.where((dist >= 0) & (dist <= 127), -slope * dist, NEG).astype(f32)
    sp = np.arange(128)[:, None]; q5 = np.arange(512)[None, :]
    sbmask = np.zeros((8, 128, 512), f32)
    for r in range(8):
        if pr == 0:
            sbmask[r] = np.where(r * 128 + sp < q5, 0.0, NEG) if r < 4 else NEG
        else:
            sbmask[r] = 0.0 if r < 4 else np.where((r - 4) * 128 + sp < q5, 0.0, NEG)
    scal = np.zeros((128, 8), f32)
    scal[:, 0] = inp["swa_sinks"][l][cidx]
    scal[:, 1] = inp["dn_a_log"][l][hd]
    scal[:, 2] = inp["dn_dt_bias"][l][hd]
    scal[:, 3] = 1.0 if pr == 0 else 0.0
    scal[:, 4] = 1.0 if pr == 1 else 0.0
    scal[:, 5] = np.asarray(inp["dn_norm"][l], f32)
    dc = np.asarray(inp["dn_conv"][l], f32)
    dnconv = np.zeros((128, 12), f32)
    for i in range(4):
        for wch in range(3):
            dnconv[:, i * 3 + wch] = dc[i, wch * 1024 + hd * 128:wch * 1024 + (hd + 1) * 128]
    return {"wcm": wcm, "wtm": wtm, "swa_bias": swa_bias, "sbmask": sbmask, "scal": scal, "dnconv": dnconv,
            "gmix": _pcol(inp["norm_mix"][l], 16), "cst": la_consts()}


SEQ = 16384
DEPTH = 4
_CACHE = {}


def _prog(key, fn):
    if key not in _CACHE:
        _CACHE[key] = fn()
    return _CACHE[key]


def run_model(inp, S, NPASS, TW):
    inp = {k: np.asarray(v) for k, v in inp.items()}
    hT = np.ascontiguousarray(inp["x"][0, :S].T.astype(np.float32))
    PT = S // NCORES // NPASS
    T = PT + 2
    for l in range(DEPTH):
        ncA = _prog(("LA", S), lambda: build_LA(S))
        maps = []
        for c in range(NCORES):
            m = la_core_inputs(inp, l, c)
            m["hT"] = hT
            maps.append(m)
        resA = run_bass_kernel_spmd(ncA, maps, core_ids=list(range(NCORES))).results
        yT = np.empty((D, S), np.float32)
        for c in range(NCORES):
            r = resA[c]
            yT[c * 128:(c + 1) * 128] = r["y_dn"]
            yT[1024 + c * 64:1024 + (c + 1) * 64] = r["y_swa"]
            pr, hh = c % 2, c // 2
            for j in range(S // 1024):
                I = 2 * j + pr
                yT[1536 + hh * 128:1536 + (hh + 1) * 128, I * 512:(I + 1) * 512] = r["y_sb"][:, j * 512:(j + 1) * 512]
        del resA, maps
        final = (l == DEPTH - 1)
        ncB = _prog(("LB", T, TW, final, NPASS), lambda: build_LB(T, TW, final, NPASS))
        w = lb_weights(inp, l)
        maps = []
        for c in range(NCORES):
            m = dict(w)
            hs = np.zeros((NPASS, D, T), np.float32)
            ys = np.zeros((NPASS, D, T), np.float32)
            fl = np.ones((NPASS, 128, 1), np.float32)
            for pi in range(NPASS):
                a = (c * NPASS + pi) * PT
                if a == 0:
                    hs[pi, :, 2:] = hT[:, 0:PT]
                    ys[pi, :, 2:] = yT[:, 0:PT]
                    fl[pi] = 0.0
                else:
                    hs[pi] = hT[:, a - 2:a + PT]
                    ys[pi] = yT[:, a - 2:a + PT]
            m["hT"] = hs; m["yT"] = ys; m["flag"] = fl
            maps.append(m)
        resB = run_bass_kernel_spmd(ncB, maps, core_ids=list(range(NCORES))).results
        hT = np.empty((D, S), np.float32)
        for c in range(NCORES):
            for pi in range(NPASS):
                a = (c * NPASS + pi) * PT
                hT[:, a:a + PT] = resB[c]["hout"][pi]
        del resB, maps
    return np.ascontiguousarray(hT.T)[None].astype(np.float32)


def kernel(**inp):
    return run_model(inp, SEQ, 2, 342)
```

```python
import numpy as np
import concourse.bass as bass
import concourse.mybir as mybir
from contextlib import ExitStack

F32 = mybir.dt.float32
BF16 = mybir.dt.bfloat16
AF = mybir.ActivationFunctionType
ALU = mybir.AluOpType
AX = mybir.AxisListType

N_DMA_SEMS = 32


class Buf:
    __slots__ = ("name", "w", "r", "excl")

    def __init__(self, name="", excl=False):
        self.name = name
        self.excl = excl
        self.w = None
        self.r = {}


class _Rec:
    def __init__(self, p, eng):
        self._p = p
        self._eng = eng

    def __getattr__(self, name):
        p, eng = self._p, self._eng

        def f(*args, reads=(), writes=(), **kw):
            return p.op(eng, lambda e: getattr(e, name)(*args, **kw), reads=reads, writes=writes)
        return f


class _Phase:
    def __init__(self, p):
        self.p = p

    def __enter__(self):
        self.saved = self.p.es
        self.stack = ExitStack()
        self.p.es = self.stack
        return self

    def __exit__(self, *a):
        self.p.barrier()
        self.p.es = self.saved
        self.stack.close()
        return False


class Prog:
    ENG = ("pe", "dve", "act", "pool", "sp")

    def __init__(self, nc, es):
        self.nc = nc
        self.es = es
        self.eng = {"pe": nc.tensor, "dve": nc.vector, "act": nc.scalar, "pool": nc.gpsimd, "sp": nc.sync}
        self.q = {e: [] for e in self.ENG}
        self.cnt = {e: 0 for e in self.ENG}
        self.sem = {e: es.enter_context(nc.semaphore("sem_" + e)) for e in self.ENG}
        self.dsem = [es.enter_context(nc.semaphore("dsem%d" % i)) for i in range(N_DMA_SEMS)]
        self.dcnt = [0] * N_DMA_SEMS
        self.dnext = 0
        self.dnext2 = [0, 0]
        self.seen = {e: {} for e in self.ENG}
        self.out_tokens = []
        self.prefix = ""
        for _e in self.ENG:
            setattr(self, _e, _Rec(self, _e))

    def sb(self, name, shape, dt):
        return self.es.enter_context(self.nc.sbuf_tensor("S_" + self.prefix + name, list(shape), dt))

    def ps(self, name, shape, dt=F32):
        return self.es.enter_context(self.nc.psum_tensor(name, list(shape), dt))

    def _semh(self, key):
        return self.sem[key] if isinstance(key, str) else self.dsem[key]

    def _waits(self, e, reads, writes):
        deps = {}
        def add(tok):
            if tok is None:
                return
            k, v = tok
            if k == e and e == "pe":
                return
            if deps.get(k, 0) < v:
                deps[k] = v
        for b in reads:
            add(b.w)
            if b.excl:
                for t in b.r.items():
                    if t[0] != e:
                        add(t)
        for b in writes:
            add(b.w)
            for t in b.r.items():
                add(t)
        out = []
        for k, v in deps.items():
            if self.seen[e].get(k, 0) >= v:
                continue
            self.seen[e][k] = v
            out.append((k, v))
        return out

    def op(self, e, fn, reads=(), writes=()):
        waits = self._waits(e, reads, writes)
        self.cnt[e] += 1
        tok = (e, self.cnt[e])
        self.q[e].append((waits, fn, (e, 1)))
        self._mark(tok, reads, writes)
        return tok

    def _mark(self, tok, reads, writes):
        for b in reads:
            if b.r.get(tok[0], 0) < tok[1]:
                b.r[tok[0]] = tok[1]
        for b in writes:
            b.w = tok
            b.r = {}

    def dma(self, e, out, in_, reads=(), writes=(), is_output=False, **kw):
        half = N_DMA_SEMS // 2
        base = 0 if e == "sp" else half
        i = base + self.dnext2[e != "sp"]
        self.dnext2[e != "sp"] = (self.dnext2[e != "sp"] + 1) % half
        waits = self._waits(e, reads, writes)
        if self.dcnt[i] > 0 and self.seen[e].get(i, 0) < self.dcnt[i]:
            waits.append((i, self.dcnt[i]))
            self.seen[e][i] = self.dcnt[i]
        self.dcnt[i] += 16
        tok = (i, self.dcnt[i])
        self.q[e].append((waits, lambda eng: eng.dma_start(out=out, in_=in_, **kw), (i, 16)))
        self._mark(tok, reads, writes)
        if is_output:
            self.out_tokens.append(tok)
        return tok

    def barrier(self):
        for e in self.ENG:
            waits = []
            for k in self.ENG:
                if self.cnt[k] > 0 and not (k == e and e in ("pe", "sp")) and self.seen[e].get(k, 0) < self.cnt[k]:
                    waits.append((k, self.cnt[k]))
                    self.seen[e][k] = self.cnt[k]
            for i in range(N_DMA_SEMS):
                if self.dcnt[i] > 0 and self.seen[e].get(i, 0) < self.dcnt[i]:
                    waits.append((i, self.dcnt[i]))
                    self.seen[e][i] = self.dcnt[i]
            if waits:
                self.q[e].append((waits, None, None))

    def phase(self):
        return _Phase(self)

    def finish(self):
        fin = []
        for (k, v) in self.out_tokens:
            fin.append((k, v))
        for e in self.ENG:
            if e != "sp" and self.cnt[e] > 0:
                fin.append((e, self.cnt[e]))
        for i in range(N_DMA_SEMS):
            if self.dcnt[i] > 0:
                fin.append((i, self.dcnt[i]))
        self.q["sp"].append((fin, None, None))
        nc = self.nc
        with nc.Block() as block:
            def mk(ename):
                def run(eng):
                    for waits, fn, inc in self.q[ename]:
                        for (k, v) in waits:
                            eng.wait_ge(self._semh(k), v)
                        if fn is not None:
                            ins = fn(eng)
                            ins.then_inc(self._semh(inc[0]), inc[1])
                return run
            if self.q["pe"]:
                block.tensor(mk("pe"))
            if self.q["dve"]:
                block.vector(mk("dve"))
            if self.q["act"]:
                block.scalar(mk("act"))
            if self.q["pool"]:
                block.gpsimd(mk("pool"))
            block.sync(mk("sp"))


from concourse.bass_utils import run_bass_kernel_spmd

D = 2048
DC = 16
EPS = 1e-6
NCORES = 8


class Ctx:
    def __init__(self, p, n_wbuf=2, wbuf_elems=8192):
        self.p = p
        nc = p.nc
        self.ps_t = [p.ps("psb%d" % i, [128, 512]) for i in range(8)]
        self.ps_b = [Buf("psb%d" % i, excl=True) for i in range(8)]
        self.ps_i = 0
        self.psl_i = 0
        self.wb_t = [p.sb("wbuf%d" % i, [128, wbuf_elems], BF16) for i in range(n_wbuf)]
        self.wb_b = [Buf("wbuf%d" % i) for i in range(n_wbuf)]
        self.wb_i = 0
        self.wbuf_elems = wbuf_elems
        self.ones_bf = p.sb("ones_bf", [128, 128], BF16)
        self.ones_f = p.sb("ones_f", [128, 128], F32)
        self.Bconst = Buf("const")
        p.pool.memset(self.ones_bf[:], 1.0, writes=[self.Bconst])
        p.pool.memset(self.ones_f[:], 1.0, writes=[self.Bconst])
        self.eps_col = p.sb("eps_col", [128, 1], F32)
        self.one_col = p.sb("one_col", [128, 1], F32)
        p.pool.memset(self.eps_col[:], EPS, writes=[self.Bconst])
        p.pool.memset(self.one_col[:], 1.0, writes=[self.Bconst])

    def psum(self):
        i = self.ps_i
        self.ps_i = (i + 1) % 6
        return self.ps_t[i], self.ps_b[i]

    def psum_long(self):
        i = 6 + self.psl_i
        self.psl_i = 1 - self.psl_i
        return self.ps_t[i], self.ps_b[i]

    def wbuf(self):
        i = self.wb_i
        self.wb_i = (i + 1) % len(self.wb_t)
        return self.wb_t[i], self.wb_b[i]


def mm(p, out, lhsT, rhs, start, stop, reads, writes):
    return p.pe.matmul(out, lhsT, rhs, start=start, stop=stop, reads=reads, writes=writes)


def linear_T(c, w_ap, K, M, rhs_fn, tiles, epilogue, mg=512):
    p = c.p
    KC = K // 128
    while KC * mg > c.wbuf_elems:
        mg //= 2
    for m0 in range(0, M, mg):
        mw = min(mg, M - m0)
        wt, wb = c.wbuf()
        view = wt[:, 0:KC * mw].rearrange("p (k m) -> p k m", m=mw)
        p.dma("pool", view, w_ap[:, m0:m0 + mw].rearrange("(k p) m -> p k m", p=128), writes=[wb])
        for mi in range(mw // 128):
            for ti, (t0, tw) in enumerate(tiles):
                ps, pb = c.psum()
                for kc in range(KC):
                    r, rb = rhs_fn(kc, ti)
                    mm(p, ps[:, 0:tw], view[:, kc, mi * 128:(mi + 1) * 128], r, kc == 0, kc == KC - 1,
                       [wb, rb, c.Bconst], [pb])
                epilogue(m0 // 128 + mi, ti, (t0, tw), ps, pb)


def rms_stats(c, src_fn, nchunks, tiles, T, dim, name):
    p = c.p
    rstd = p.sb(name + "_rstd", [128, T], F32)
    Brstd = Buf(name + "_rstd")
    sq = [p.sb(name + "_sq%d" % i, [128, 512], BF16) for i in range(2)]
    Bsq = [Buf() for i in range(2)]
    k = 0
    for ti, (t0, tw) in enumerate(tiles):
        ps, pb = c.psum()
        for ci in range(nchunks):
            s, sbuf_ = src_fn(ci, ti)
            q, qb = sq[k % 2], Bsq[k % 2]
            k += 1
            p.act.activation(out=q[:, 0:tw], in_=s, func=AF.Square,
                 reads=[sbuf_], writes=[qb])
            mm(p, ps[:, 0:tw], c.ones_bf[:], q[:, 0:tw], ci == 0, ci == nchunks - 1, [qb, c.Bconst], [pb])
        p.act.activation(out=rstd[:, t0:t0 + tw], in_=ps[:, 0:tw], func=AF.Sqrt,
                                                                  scale=1.0 / dim, bias=c.eps_col[:, 0:1],
             reads=[pb, c.Bconst], writes=[Brstd])
        p.dve.reciprocal(out=rstd[:, t0:t0 + tw], in_=rstd[:, t0:t0 + tw],
             reads=[Brstd], writes=[Brstd])
    return rstd, Brstd


class BufGrid:
    def __init__(self):
        self.d = {}

    def __getitem__(self, k):
        if k not in self.d:
            self.d[k] = Buf(str(k))
        return self.d[k]


def build_LB(T, TW, final, NPASS=1):
    nc = bass.Bass("TRN2", target_bir_lowering=False)
    tiles = [(t0, min(TW, T - t0)) for t0 in range(0, T, TW)]
    NT = len(tiles)
    TWP = max(TW, 256)

    def din(name, shape):
        return nc.dram_tensor(name, list(shape), F32, kind="ExternalInput").ap()

    hT_all = din("hT", [NPASS, D, T]); yT_all = din("yT", [NPASS, D, T]); memT = din("memT", [D, 256])
    w_gate = din("w_gate", [D, 3 * D]); w_br = din("w_br", [D, D]); w_o = din("w_o", [D, D])
    w_xq = din("w_xq", [D, 512]); w_xkv = din("w_xkv", [D, 1024]); w_xo = din("w_xo", [512, D])
    w_up = din("w_up", [D, 8192]); w_down = din("w_down", [4096, D])
    gains = din("gains", [128, 5 * 16])
    fconv = din("fconv", [128, 3 * 64])
    flag_all = din("flag", [NPASS, 128, 1])
    hout_all = nc.dram_tensor("hout", [NPASS, D, T - 2], F32, kind="ExternalOutput").ap()
    hA = nc.dram_tensor("hA", [D, T], F32, kind="Internal").ap()
    hB = nc.dram_tensor("hB", [D, T], F32, kind="Internal").ap()
    hC = nc.dram_tensor("hC", [D, T], F32, kind="Internal").ap()

    with ExitStack() as es:
        p = Prog(nc, es)
        c = Ctx(p, n_wbuf=3, wbuf_elems=4096)
        dB = {"hT": BufGrid(), "hA": BufGrid(), "hB": BufGrid(), "hC": BufGrid()}

        def one_pass(pi):
            hT = hT_all[pi]; yT = yT_all[pi]; flag = flag_all[pi]; hout = hout_all[pi]
            xn = p.sb("xn", [128, 16, T], BF16); xnB = BufGrid()
            big = p.sb("big", [128, 32, T], BF16); bigB = BufGrid()
            qo = p.sb("qo", [128, 8, T], BF16); qoB = BufGrid()
            gains_sb = p.sb("gains_sb", [128, 80], F32); fconv_sb = p.sb("fconv_sb", [128, 192], F32)
            flag_sb = p.sb("flag_sb", [128, 1], F32)
            Bpar = Buf("params")
            p.dma("sp", gains_sb[:], gains, writes=[Bpar])
            p.dma("sp", fconv_sb[:], fconv, writes=[Bpar])
            p.dma("sp", flag_sb[:], flag, writes=[Bpar])
            ssq = p.sb("ssq", [128, T], F32); Bssq = BufGrid()
            rstd = p.sb("rstd", [128, T], F32); Brstd = BufGrid()
            NST = 3
            st_h = [p.sb("st_h%d" % i, [128, TWP], F32) for i in range(NST)]; Bst_h = [Buf() for _ in range(NST)]
            st_n = [p.sb("st_n%d" % i, [128, TWP], F32) for i in range(NST)]; Bst_n = [Buf() for _ in range(NST)]
            st_q = [p.sb("st_q%d" % i, [128, TWP], F32) for i in range(2)]; Bst_q = [Buf() for _ in range(2)]
            cnt = {"h": 0, "n": 0, "q": 0}

            def stg(kind):
                arr, bufs, n = {"h": (st_h, Bst_h, NST), "n": (st_n, Bst_n, NST), "q": (st_q, Bst_q, 2)}[kind]
                i = cnt[kind] % n
                cnt[kind] += 1
                return arr[i], bufs[i]


            def compute_rstd():
                for ti, (t0, tw) in enumerate(tiles):
                    ps, pb = c.psum()
                    mm(p, ps[:, 0:tw], c.ones_f[:], ssq[:, t0:t0 + tw], True, True, [Bssq[ti], c.Bconst], [pb])
                    p.act.activation(out=rstd[:, t0:t0 + tw], in_=ps[:, 0:tw],
                                                                              func=AF.Sqrt, scale=1.0 / D, bias=c.eps_col[:, 0:1],
                         reads=[pb, c.Bconst], writes=[Brstd[ti]])
                    p.dve.reciprocal(out=rstd[:, t0:t0 + tw], in_=rstd[:, t0:t0 + tw],
                         reads=[Brstd[ti]], writes=[Brstd[ti]])

            def accum_ssq(src, srcB, ci, ti, t0, tw):
                if ci == 0:
                    p.act.activation(out=ssq[:, t0:t0 + tw], in_=src, func=AF.Square,
                         reads=[srcB], writes=[Bssq[ti]])
                else:
                    q, qb = stg("q")
                    p.act.activation(out=q[:, 0:tw], in_=src, func=AF.Square, reads=[srcB], writes=[qb])
                    p.pool.tensor_tensor(out=ssq[:, t0:t0 + tw], in0=ssq[:, t0:t0 + tw], in1=q[:, 0:tw], op=ALU.add,
                         reads=[qb, Bssq[ti]], writes=[Bssq[ti]])

            def stats_from_dram(src, srcname):
                for ti, (t0, tw) in enumerate(tiles):
                    for ci in range(16):
                        s, sb_ = stg("h")
                        p.dma("sp", s[:, 0:tw], src[ci * 128:(ci + 1) * 128, t0:t0 + tw], reads=[dB[srcname][(ci, ti)]], writes=[sb_])
                        accum_ssq(s[:, 0:tw], sb_, ci, ti, t0, tw)
                compute_rstd()

            def normalize_from_dram(src, srcname, gj):
                for ti, (t0, tw) in enumerate(tiles):
                    for ci in range(16):
                        s, sb_ = stg("h")
                        p.dma("sp", s[:, 0:tw], src[ci * 128:(ci + 1) * 128, t0:t0 + tw], reads=[dB[srcname][(ci, ti)]], writes=[sb_])
                        p.dve.scalar_tensor_tensor(
                            out=xn[:, ci, t0:t0 + tw], in0=s[:, 0:tw], scalar=gains_sb[:, gj * 16 + ci:gj * 16 + ci + 1],
                            in1=rstd[:, t0:t0 + tw], op0=ALU.mult, op1=ALU.mult,
                            reads=[sb_, Bpar, Brstd[ti]], writes=[xnB[(ci, ti)]])

            def xn_rhs(kc, ti):
                t0, tw = tiles[ti]
                return xn[:, kc, t0:t0 + tw], xnB[(kc, ti)]

            def residual_epilogue(prev, prevname, nxt, nxtname, out_dram=None):
                def ep(m, ti, tt, ps, pb):
                    t0, tw = tt
                    s, sb_ = stg("h")
                    p.dma("sp", s[:, 0:tw], prev[m * 128:(m + 1) * 128, t0:t0 + tw], reads=[dB[prevname][(m, ti)]], writes=[sb_])
                    n, nb = stg("n")
                    p.dve.tensor_tensor(out=n[:, 0:tw], in0=s[:, 0:tw], in1=ps[:, 0:tw], op=ALU.add,
                         reads=[sb_, pb], writes=[nb])
                    if nxt is not None:
                        p.dma("sp", nxt[m * 128:(m + 1) * 128, t0:t0 + tw], n[:, 0:tw], reads=[nb], writes=[dB[nxtname][(m, ti)]])
                        accum_ssq(n[:, 0:tw], nb, m, ti, t0, tw)
                    if out_dram is not None:
                        a0 = max(t0, 2)
                        p.dma("sp", out_dram[m * 128:(m + 1) * 128, a0 - 2:t0 + tw - 2], n[:, a0 - t0:tw], reads=[nb], is_output=True)
                return ep


            stats_from_dram(hT, "hT")
            normalize_from_dram(hT, "hT", 0)
            for ci in range(16):
                p.dma("pool", big[:, 16 + ci, :], yT[ci * 128:(ci + 1) * 128, :], writes=[bigB[16 + ci]])

            macc = p.sb("macc", [128, 2, T], F32); maccB = BufGrid()
            sg = [p.sb("sg%d" % i, [128, TWP], F32) for i in range(2)]; Bsg = [Buf() for _ in range(2)]
            br_rows = [(0, 8), (8, 4), (12, 4)]
            sgi = 0
            for m0 in range(0, D, 256):
                for br in range(3):
                    wg, wgb = c.wbuf()
                    vg = wg[:, 0:16 * 256].rearrange("p (k m) -> p k m", m=256)
                    p.dma("pool", vg, w_gate[:, br * D + m0:br * D + m0 + 256].rearrange("(k p) m -> p k m", p=128), writes=[wgb])
                    k0, nk = br_rows[br]
                    wz, wzb = c.wbuf()
                    vz = wz[:, 0:nk * 256].rearrange("p (k m) -> p k m", m=256)
                    p.dma("pool", vz, w_br[k0 * 128:(k0 + nk) * 128, m0:m0 + 256].rearrange("(k p) m -> p k m", p=128), writes=[wzb])
                    for mi in range(2):
                        for ti, (t0, tw) in enumerate(tiles):
                            pg, pgb = c.psum()
                            for kc in range(16):
                                mm(p, pg[:, 0:tw], vg[:, kc, mi * 128:(mi + 1) * 128], xn[:, kc, t0:t0 + tw], kc == 0, kc == 15,
                                   [wgb, xnB[(kc, ti)]], [pgb])
                            pz, pzb = c.psum()
                            for kc in range(nk):
                                mm(p, pz[:, 0:tw], vz[:, kc, mi * 128:(mi + 1) * 128], big[:, 16 + k0 + kc, t0:t0 + tw], kc == 0, kc == nk - 1,
                                   [wzb, bigB[16 + k0 + kc]], [pzb])
                            s_, sb_ = sg[sgi % 2], Bsg[sgi % 2]
                            sgi += 1
                            p.act.activation(out=s_[:, 0:tw], in_=pg[:, 0:tw], func=AF.Sigmoid,
                                 reads=[pgb], writes=[sb_])
                            mB = maccB[(mi, ti)]
                            if br == 0:
                                p.dve.tensor_tensor(
                                    out=macc[:, mi, t0:t0 + tw], in0=s_[:, 0:tw], in1=pz[:, 0:tw], op=ALU.mult,
                                    reads=[sb_, pzb], writes=[mB])
                            else:
                                p.dve.tensor_tensor(
                                    out=s_[:, 0:tw], in0=s_[:, 0:tw], in1=pz[:, 0:tw], op=ALU.mult,
                                    reads=[sb_, pzb], writes=[sb_])
                                if br == 1:
                                    p.pool.tensor_tensor(
                                        out=macc[:, mi, t0:t0 + tw], in0=macc[:, mi, t0:t0 + tw], in1=s_[:, 0:tw], op=ALU.add,
                                        reads=[sb_, mB], writes=[mB])
                                else:
                                    mc = m0 // 128 + mi
                                    p.pool.tensor_tensor(
                                        out=big[:, mc, t0:t0 + tw], in0=macc[:, mi, t0:t0 + tw], in1=s_[:, 0:tw], op=ALU.add,
                                        reads=[sb_, mB], writes=[bigB[(mc, ti)]])

            linear_T(c, w_o, D, D, lambda kc, ti: (big[:, kc, tiles[ti][0]:tiles[ti][0] + tiles[ti][1]], bigB[(kc, ti)]),
                     tiles, residual_epilogue(hT, "hT", hA, "hA"))
            compute_rstd()
            normalize_from_dram(hA, "hA", 1)

            memn = p.sb("memn", [128, 16, 256], BF16); memnB = BufGrid()
            mst = [p.sb("mst%d" % i, [128, 256], F32) for i in range(2)]; Bmst = [Buf() for _ in range(2)]
            mssq = p.sb("mssq", [128, 256], F32); Bmssq = Buf()
            mrstd = p.sb("mrstd", [128, 256], F32); Bmrstd = Buf()
            for ci in range(16):
                s, sb_ = mst[ci % 2], Bmst[ci % 2]
                p.dma("sp", s[:], memT[ci * 128:(ci + 1) * 128, :], writes=[sb_])
                if ci == 0:
                    p.act.activation(out=mssq[:], in_=s[:], func=AF.Square, reads=[sb_], writes=[Bmssq])
                else:
                    q, qb = stg("q")
                    p.act.activation(out=q[:, 0:256], in_=s[:], func=AF.Square, reads=[sb_], writes=[qb])
                    p.pool.tensor_tensor(out=mssq[:], in0=mssq[:], in1=q[:, 0:256], op=ALU.add,
                         reads=[qb, Bmssq], writes=[Bmssq])
            ps, pb = c.psum()
            mm(p, ps[:, 0:256], c.ones_f[:], mssq[:], True, True, [Bmssq, c.Bconst], [pb])
            p.act.activation(out=mrstd[:], in_=ps[:, 0:256], func=AF.Sqrt, scale=1.0 / D, bias=c.eps_col[:, 0:1],
                 reads=[pb, c.Bconst], writes=[Bmrstd])
            p.dve.reciprocal(out=mrstd[:], in_=mrstd[:], reads=[Bmrstd], writes=[Bmrstd])
            for ci in range(16):
                s, sb_ = mst[ci % 2], Bmst[ci % 2]
                p.dma("sp", s[:], memT[ci * 128:(ci + 1) * 128, :], writes=[sb_])
                p.dve.scalar_tensor_tensor(
                    out=memn[:, ci, :], in0=s[:], scalar=gains_sb[:, 2 * 16 + ci:2 * 16 + ci + 1], in1=mrstd[:],
                    op0=ALU.mult, op1=ALU.mult, reads=[sb_, Bpar, Bmrstd], writes=[memnB[ci]])
            KxT = p.sb("KxT", [128, 4, 256], BF16); KxTB = BufGrid()
            Vx = p.sb("Vx", [128, 2, 512], BF16); VxB = BufGrid()

            def ep_k(m, ti, tt, ps, pb):
                p.act.activation(out=KxT[:, m, :], in_=ps[:, 0:256], func=AF.Copy, reads=[pb], writes=[KxTB[m]])
            linear_T(c, w_xkv[:, 0:512], D, 512, lambda kc, ti: (memn[:, kc, :], memnB[kc]), [(0, 256)], ep_k)
            for half in range(2):
                wt, wb = c.wbuf()
                vw = wt[:, 0:16 * 256].rearrange("p (k m) -> p k m", m=256)
                p.dma("pool", vw, w_xkv[:, 512 + half * 256:512 + (half + 1) * 256].rearrange("(k p) m -> p k m", p=128), writes=[wb])
                for mc in range(2):
                    ps, pb = c.psum()
                    for kc in range(16):
                        mm(p, ps[:, 0:256], memn[:, kc, mc * 128:(mc + 1) * 128], vw[:, kc, :], kc == 0, kc == 15, [wb, memnB[kc]], [pb])
                    p.act.activation(out=Vx[:, mc, half * 256:(half + 1) * 256], in_=ps[:, 0:256], func=AF.Copy,
                         reads=[pb], writes=[VxB[(mc, half)]])

            def ep_q(m, ti, tt, ps, pb):
                t0, tw = tt
                p.act.activation(out=qo[:, m, t0:t0 + tw], in_=ps[:, 0:tw], func=AF.Copy, scale=128.0 ** -0.5,
                     reads=[pb], writes=[qoB[(m, ti)]])
            linear_T(c, w_xq, D, 512, xn_rhs, tiles, ep_q)
            pt = [p.sb("pt%d" % i, [128, TWP], BF16) for i in range(4)]; Bpt = [Buf() for _ in range(4)]
            rden = [p.sb("rden%d" % i, [128, TWP], F32) for i in range(2)]; Brden = [Buf() for _ in range(2)]
            k = 0
            for hd in range(4):
                for ti, (t0, tw) in enumerate(tiles):
                    pts = []
                    for mc in range(2):
                        ps, pb = c.psum()
                        mm(p, ps[:, 0:tw], KxT[:, hd, mc * 128:(mc + 1) * 128], qo[:, hd, t0:t0 + tw], True, True,
                           [KxTB[hd], qoB[(hd, ti)]], [pb])
                        e_, eb = pt[k % 4], Bpt[k % 4]
                        k += 1
                        p.act.activation(out=e_[:, 0:tw], in_=ps[:, 0:tw], func=AF.Exp,
                             reads=[pb], writes=[eb])
                        pts.append((e_, eb))
                    po, pob = c.psum()
                    pd, pdb = c.psum()
                    for mc in range(2):
                        mm(p, po[:, 0:tw], Vx[:, mc, hd * 128:(hd + 1) * 128], pts[mc][0][:, 0:tw], mc == 0, mc == 1,
                           [VxB[(mc, hd // 2)], pts[mc][1]], [pob])
                    for mc in range(2):
                        mm(p, pd[:, 0:tw], c.ones_bf[:], pts[mc][0][:, 0:tw], mc == 0, mc == 1, [c.Bconst, pts[mc][1]], [pdb])
                    r_, rb = rden[(k // 2) % 2], Brden[(k // 2) % 2]
                    p.dve.reciprocal(out=r_[:, 0:tw], in_=pd[:, 0:tw], reads=[pdb], writes=[rb])
                    p.dve.tensor_tensor(
                        out=qo[:, 4 + hd, t0:t0 + tw], in0=po[:, 0:tw], in1=r_[:, 0:tw], op=ALU.mult,
                        reads=[pob, rb], writes=[qoB[(4 + hd, ti)]])

            linear_T(c, w_xo, 512, D, lambda kc, ti: (qo[:, 4 + kc, tiles[ti][0]:tiles[ti][0] + tiles[ti][1]], qoB[(4 + kc, ti)]),
                     tiles, residual_epilogue(hA, "hA", hB, "hB"))
            compute_rstd()
            normalize_from_dram(hB, "hB", 3)

            ug = p.sb("ug", [128, T + 2], F32); uv = p.sb("uv", [128, T + 2], F32)
            Bug = BufGrid(); Buv = BufGrid()
            cg = p.sb("cg", [128, T], F32); cv = p.sb("cv", [128, T], F32); Bcg = Buf(); Bcv = Buf()
            p.pool.memset(ug[:, 0:2], 0.0, writes=[Bug["pad"]])
            p.pool.memset(uv[:, 0:2], 0.0, writes=[Buv["pad"]])
            for f in range(32):
                for which, (u, Bu, col0) in enumerate(((ug, Bug, f * 128), (uv, Buv, 4096 + f * 128))):
                    wt, wb = c.wbuf()
                    vw = wt[:, 0:16 * 128].rearrange("p (k m) -> p k m", m=128)
                    p.dma("pool", vw, w_up[:, col0:col0 + 128].rearrange("(k p) m -> p k m", p=128), writes=[wb])
                    for ti, (t0, tw) in enumerate(tiles):
                        ps, pb = c.psum()
                        for kc in range(16):
                            mm(p, ps[:, 0:tw], vw[:, kc, :], xn[:, kc, t0:t0 + tw], kc == 0, kc == 15, [wb, xnB[(kc, ti)]], [pb])
                        p.act.activation(out=u[:, 2 + t0:2 + t0 + tw], in_=ps[:, 0:tw], func=AF.Copy,
                             reads=[pb], writes=[Bu[ti]])
                    p.dve.tensor_scalar(out=u[:, 2:4], in0=u[:, 2:4], scalar1=flag_sb[:, 0:1], scalar2=None, op0=ALU.mult,
                         reads=[Bu[0], Bpar], writes=[Bu[0]])
                    cc, Bcc = (cg, Bcg) if which == 0 else (cv, Bcv)
                    ch = (col0 // 128)
                    allu = [Bu[ti] for ti in range(NT)] + [Bu["pad"], Bpar]
                    p.dve.tensor_scalar(out=cc[:, :], in0=u[:, 0:T], scalar1=fconv_sb[:, 0 * 64 + ch:0 * 64 + ch + 1],
                                                                          scalar2=None, op0=ALU.mult, reads=allu, writes=[Bcc])
                    p.dve.scalar_tensor_tensor(out=cc[:, :], in0=u[:, 1:T + 1], scalar=fconv_sb[:, 1 * 64 + ch:1 * 64 + ch + 1],
                                                                                 in1=cc[:, :], op0=ALU.mult, op1=ALU.add, reads=allu + [Bcc], writes=[Bcc])
                    p.dve.scalar_tensor_tensor(out=cc[:, :], in0=u[:, 2:T + 2], scalar=fconv_sb[:, 2 * 64 + ch:2 * 64 + ch + 1],
                                                                                 in1=cc[:, :], op0=ALU.mult, op1=ALU.add, reads=allu + [Bcc], writes=[Bcc])
                p.act.activation(out=cg[:, :], in_=cg[:, :], func=AF.Silu, reads=[Bcg], writes=[Bcg])
                p.pool.tensor_tensor(out=big[:, f, :], in0=cg[:, :], in1=cv[:, :], op=ALU.mult,
                     reads=[Bcg, Bcv], writes=[bigB[("A", f)]])

            if not final:
                linear_T(c, w_down, 4096, D, lambda kc, ti: (big[:, kc, tiles[ti][0]:tiles[ti][0] + tiles[ti][1]], bigB[("A", kc)]),
                         tiles, residual_epilogue(hB, "hB", None, None, out_dram=hout))
            else:
                linear_T(c, w_down, 4096, D, lambda kc, ti: (big[:, kc, tiles[ti][0]:tiles[ti][0] + tiles[ti][1]], bigB[("A", kc)]),
                         tiles, residual_epilogue(hB, "hB", hC, "hC"))
                compute_rstd()
                for ti, (t0, tw) in enumerate(tiles):
                    for ci in range(16):
                        s, sb_ = stg("h")
                        p.dma("sp", s[:, 0:tw], hC[ci * 128:(ci + 1) * 128, t0:t0 + tw], reads=[dB["hC"][(ci, ti)]], writes=[sb_])
                        n, nb = stg("n")
                        p.dve.scalar_tensor_tensor(
                            out=n[:, 0:tw], in0=s[:, 0:tw], scalar=gains_sb[:, 4 * 16 + ci:4 * 16 + ci + 1],
                            in1=rstd[:, t0:t0 + tw], op0=ALU.mult, op1=ALU.mult,
                            reads=[sb_, Bpar, Brstd[ti]], writes=[nb])
                        a0 = max(t0, 2)
                        p.dma("sp", hout[ci * 128:(ci + 1) * 128, a0 - 2:t0 + tw - 2], n[:, a0 - t0:tw], reads=[nb], is_output=True)
        for pi in range(NPASS):
            p.prefix = "p%d_" % pi
            with p.phase():
                one_pass(pi)
        p.finish()
    return nc


def _pcol(v, n):
    return np.ascontiguousarray(np.asarray(v, np.float32).reshape(n, 128).T)


def lb_weights(inp, l):
    f32 = np.float32
    gains = np.concatenate([_pcol(inp["norm_mix"][l], 16), _pcol(inp["norm_xattn"][l], 16), _pcol(inp["norm_mem"][l], 16),
                            _pcol(inp["norm_ffn"][l], 16), _pcol(inp["norm_final"], 16)], axis=1)
    fc = np.asarray(inp["ffn_conv"][l], f32)
    fconv = np.concatenate([_pcol(fc[i], 64) for i in range(3)], axis=1)
    return {
        "memT": np.ascontiguousarray(np.asarray(inp["mem"][0], f32).T),
        "w_gate": np.ascontiguousarray(inp["w_in"][l][:, 6416:]),
        "w_br": np.ascontiguousarray(np.concatenate([inp["w_br_dn"][l], inp["w_br_swa"][l], inp["w_br_sb"][l]], axis=0)),
        "w_o": np.ascontiguousarray(inp["w_o"][l]),
        "w_xq": np.ascontiguousarray(inp["w_xq"][l]), "w_xkv": np.ascontiguousarray(inp["w_xkv"][l]),
        "w_xo": np.ascontiguousarray(inp["w_xo"][l]),
        "w_up": np.ascontiguousarray(inp["w_up"][l]), "w_down": np.ascontiguousarray(inp["w_down"][l]),
        "gains": np.ascontiguousarray(gains), "fconv": np.ascontiguousarray(fconv),
    }


NEG = -30000.0
DN_STAGE = 9
DN_SUB = 99


def la_proj(c, S, hT, wcm_d, wtm_d, gmix_sb, Bpar, cm, tm, dB):
    p = c.p
    KC = 16
    wcm = p.sb("wcm_sb", [128, 16, 896], BF16); Bwcm = Buf()
    wtm = p.sb("wtm_sb", [128, 16, 194], BF16); Bwtm = Buf()
    for k0 in range(0, 16, 4):
        p.dma("pool", wcm[:, k0:k0 + 4, :], wcm_d[k0 * 128:(k0 + 4) * 128, :].rearrange("(k p) m -> p k m", p=128), writes=[Bwcm])
    for k0 in range(0, 16, 8):
        p.dma("pool", wtm[:, k0:k0 + 8, :], wtm_d[k0 * 128:(k0 + 8) * 128, :].rearrange("(k p) m -> p k m", p=128), writes=[Bwtm])
    hs = [p.sb("la_hs%d" % i, [128, 16, 512], F32) for i in range(2)]; Bhs = [Buf() for _ in range(2)]
    sq = p.sb("la_sq", [128, 16, 512], BF16); Bsq = Buf()
    xn = [p.sb("la_xn%d" % i, [128, 16, 512], BF16) for i in range(2)]; Bxn = [Buf() for _ in range(2)]
    rstd = p.sb("la_rstd", [128, 512], F32); Brstd = Buf()
    ev = [p.sb("la_ev%d" % i, [128, 512], F32) for i in range(4)]; Bev = [Buf() for _ in range(4)]
    evi = 0
    groups = [(g * 128, 128) for g in range(6)] + [(768, 64), (832, 64)]
    for t in range(S // 512):
        h_, hb = hs[t % 2], Bhs[t % 2]
        for k0 in range(0, 16, 4):
            p.dma("sp", h_[:, k0:k0 + 4, :], hT[k0 * 128:(k0 + 4) * 128, t * 512:(t + 1) * 512].rearrange("(k p) m -> p k m", p=128), writes=[hb])
        p.act.activation(out=sq[:], in_=h_[:], func=AF.Square, reads=[hb], writes=[Bsq])
        ps, pb = c.psum()
        for kc in range(16):
            mm(p, ps[:, :], c.ones_bf[:], sq[:, kc, :], kc == 0, kc == 15, [Bsq, c.Bconst], [pb])
        p.act.activation(out=rstd[:], in_=ps[:, :], func=AF.Sqrt, scale=1.0 / D, bias=c.eps_col[:, 0:1],
             reads=[pb, c.Bconst], writes=[Brstd])
        p.dve.reciprocal(out=rstd[:], in_=rstd[:], reads=[Brstd], writes=[Brstd])
        x_, xb = xn[t % 2], Bxn[t % 2]
        for kc in range(16):
            eng = "dve" if kc % 2 == 0 else "pool"
            if eng == "dve":
                p.dve.scalar_tensor_tensor(
                    out=x_[:, kc, :], in0=h_[:, kc, :], scalar=gmix_sb[:, kc:kc + 1], in1=rstd[:], op0=ALU.mult, op1=ALU.mult,
                    reads=[hb, Bpar, Brstd], writes=[xb])
            else:
                p.dve.scalar_tensor_tensor(
                    out=x_[:, kc, :], in0=h_[:, kc, :], scalar=gmix_sb[:, kc:kc + 1], in1=rstd[:], op0=ALU.mult, op1=ALU.mult,
                    reads=[hb, Bpar, Brstd], writes=[xb])
        for g, (c0, mw) in enumerate(groups):
            ps, pb = c.psum()
            for kc in range(16):
                mm(p, ps[0:mw, :], wcm[:, kc, c0:c0 + mw], x_[:, kc, :], kc == 0, kc == 15, [Bwcm, xb], [pb])
            e_, eb = ev[evi % 4], Bev[evi % 4]
            evi += 1
            p.act.activation(out=e_[0:mw, :], in_=ps[0:mw, :], func=AF.Copy, reads=[pb], writes=[eb])
            r0 = g * 128 if g < 6 else (768 if g == 6 else 832)
            p.dma("sp", cm[r0:r0 + mw, t * 512:(t + 1) * 512], e_[0:mw, :], reads=[eb], writes=[dB[("cm", g, t)]])
        for s4 in range(4):
            ps, pb = c.psum()
            for kc in range(16):
                mm(p, ps[:, 0:194], x_[:, kc, s4 * 128:(s4 + 1) * 128], wtm[:, kc, :], kc == 0, kc == 15, [Bwtm, xb], [pb])
            e_, eb = ev[evi % 4], Bev[evi % 4]
            evi += 1
            p.dve.tensor_copy(out=e_[:, 0:194], in_=ps[:, 0:194], reads=[pb], writes=[eb])
            p.dma("sp", tm[t * 512 + s4 * 128:t * 512 + (s4 + 1) * 128, :], e_[:, 0:194], reads=[eb], writes=[dB[("tm", t)]])


def la_swa(c, S, cm, tm, dB, swa_bias_d, scal_sb, Bpar, y_swa):
    p = c.p
    NB = S // 128
    NTL = S // 512
    QT = p.sb("swa_QT", [64, S], BF16); KT = p.sb("swa_KT", [64, S], BF16)
    V = p.sb("swa_V", [128, NB, 64], BF16)
    BQ = BufGrid(); BK = BufGrid(); BV = BufGrid()
    for t in range(NTL):
        p.dma("pool", QT[:, t * 512:(t + 1) * 512], cm[768:832, t * 512:(t + 1) * 512], reads=[dB[("cm", 6, t)]], writes=[BQ[t]])
        p.dma("pool", KT[:, t * 512:(t + 1) * 512], cm[832:896, t * 512:(t + 1) * 512], reads=[dB[("cm", 7, t)]], writes=[BK[t]])
        p.dma("pool", V[:, t * 4:(t + 1) * 4, :], tm[t * 512:(t + 1) * 512, 128:192].rearrange("(b p) d -> p b d", p=128),
              reads=[dB[("tm", t)]], writes=[BV[t]])
    bias = p.sb("swa_bias", [128, 256], F32); Bbias = Buf()
    p.dma("sp", bias[:], swa_bias_d, writes=[Bbias])
    es = p.sb("swa_es", [128, 1], F32); Bes = Buf()
    p.act.activation(out=es[:], in_=scal_sb[:, 0:1], func=AF.Exp, reads=[Bpar], writes=[Bes])
    ones64 = c.ones_bf[:, 0:64]
    tmp = [p.sb("swa_tmp%d" % i, [128, 256], F32) for i in range(2)]; Btmp = [Buf() for _ in range(2)]
    PT = [p.sb("swa_PT%d" % i, [128, 256], BF16) for i in range(3)]; BPT = [Buf() for _ in range(3)]
    den = [p.sb("swa_den%d" % i, [64, 512], F32) for i in range(2)]; Bden = [Buf() for _ in range(2)]
    yo = [p.sb("swa_yo%d" % i, [64, 512], F32) for i in range(2)]; Byo = [Buf() for _ in range(2)]
    prev = None
    po = pd = None
    for b in range(NB):
        qw = 256 if b < NB - 1 else 128
        t_q = [BQ[(b * 128) // 512], BQ[min((b * 128 + 128) // 512, NTL - 1)]]
        ps, pb = c.psum()
        mm(p, ps[:, 0:qw], KT[:, b * 128:(b + 1) * 128], QT[:, b * 128:b * 128 + qw], True, True, [BK[b // 4]] + t_q, [pb])
        tm_, tb = tmp[b % 2], Btmp[b % 2]
        p.dve.scalar_tensor_tensor(out=tm_[:, 0:qw], in0=ps[:, 0:qw], scalar=0.125, in1=bias[:, 0:qw],
                                                                           op0=ALU.mult, op1=ALU.add, reads=[pb, Bbias], writes=[tb])
        pt_, ptb = PT[b % 3], BPT[b % 3]
        p.act.activation(out=pt_[:, 0:qw], in_=tm_[:, 0:qw], func=AF.Exp, reads=[tb], writes=[ptb])
        j = b % 4
        if j == 0:
            po, pob = c.psum()
            pd, pdb = c.psum()
        if prev is not None:
            ppt, pptb, pb_ = prev
            mm(p, po[0:64, j * 128:(j + 1) * 128], V[:, b - 1, :], ppt[:, 128:256], True, False, [BV[(b - 1) // 4], pptb], [pob])
            mm(p, po[0:64, j * 128:(j + 1) * 128], V[:, b, :], pt_[:, 0:128], False, True, [BV[b // 4], ptb], [pob])
            mm(p, pd[0:64, j * 128:(j + 1) * 128], ones64, ppt[:, 128:256], True, False, [c.Bconst, pptb], [pdb])
            mm(p, pd[0:64, j * 128:(j + 1) * 128], ones64, pt_[:, 0:128], False, True, [c.Bconst, ptb], [pdb])
        else:
            mm(p, po[0:64, j * 128:(j + 1) * 128], V[:, b, :], pt_[:, 0:128], True, True, [BV[b // 4], ptb], [pob])
            mm(p, pd[0:64, j * 128:(j + 1) * 128], ones64, pt_[:, 0:128], True, True, [c.Bconst, ptb], [pdb])
        prev = (pt_, ptb, b)
        if j == 3:
            t = b // 4
            d_, db_ = den[t % 2], Bden[t % 2]
            y_, yb = yo[t % 2], Byo[t % 2]
            p.dve.tensor_scalar(out=d_[:], in0=pd[0:64, :], scalar1=es[0:64, 0:1], scalar2=None, op0=ALU.add,
                 reads=[pdb, Bes], writes=[db_])
            p.dve.reciprocal(out=d_[:], in_=d_[:], reads=[db_], writes=[db_])
            p.dve.tensor_tensor(out=y_[:], in0=po[0:64, :], in1=d_[:], op=ALU.mult,
                 reads=[pob, db_], writes=[yb])
            p.dma("sp", y_swa[:, t * 512:(t + 1) * 512], y_[:], reads=[yb], is_output=True)


def la_sb(c, S, cm, tm, dB, sbmask_d, scal_sb, Bpar, consts, y_sb):
    p = c.p
    NB = S // 128
    NTL = S // 512
    NSLOT = NTL // 2
    negtri, ident_bf, Bc2 = consts
    QT = p.sb("sb_QT", [128, S], BF16); KT = p.sb("sb_KT", [128, S], BF16)
    V = p.sb("sb_V", [128, NB, 128], BF16)
    BQ = BufGrid(); BK = BufGrid(); BV = BufGrid()
    for t in range(NTL):
        p.dma("pool", QT[:, t * 512:(t + 1) * 512], cm[512:640, t * 512:(t + 1) * 512], reads=[dB[("cm", 4, t)]], writes=[BQ[t]])
        p.dma("pool", KT[:, t * 512:(t + 1) * 512], cm[640:768, t * 512:(t + 1) * 512], reads=[dB[("cm", 5, t)]], writes=[BK[t]])
        p.dma("pool", V[:, t * 4:(t + 1) * 4, :], tm[t * 512:(t + 1) * 512, 0:128].rearrange("(b p) d -> p b d", p=128),
              reads=[dB[("tm", t)]], writes=[BV[t]])
    mask = p.sb("sb_mask", [128, 8, 512], BF16); Bmask = Buf()
    for r in range(8):
        p.dma("pool", mask[:, r, :], sbmask_d[r], writes=[Bmask])
    qs = [p.sb("sb_qs%d" % i, [128, 512], BF16) for i in range(2)]; Bqs = [Buf() for _ in range(2)]
    ebuf = [p.sb("sb_e%d" % i, [128, 512], F32) for i in range(2)]; Be = [Buf() for _ in range(2)]
    spb = [p.sb("sb_sp%d" % i, [128, 512], BF16) for i in range(3)]; Bsp = [Buf() for _ in range(3)]
    at = [p.sb("sb_at%d" % i, [128, 512], BF16) for i in range(2)]; Bat = [Buf() for _ in range(2)]
    lcum = [p.sb("sb_lc%d" % i, [128, 512], BF16) for i in range(2)]; Blc = [Buf() for _ in range(2)]
    yo = [p.sb("sb_yo%d" % i, [128, 512], F32) for i in range(2)]; Byo = [Buf() for _ in range(2)]
    negones = p.sb("sb_negones", [128, 128], BF16)
    p.pool.memset(negones[:], -1.0, writes=[Bc2])
    scale = 128.0 ** -0.5
    at3 = [p.sb("sb_at3_%d" % i, [128, 512], BF16) for i in range(3)]; Bat3 = [Buf() for _ in range(3)]
    lc4 = [p.sb("sb_lc4_%d" % i, [128, 512], BF16) for i in range(4)]; Blc4 = [Buf() for _ in range(4)]
    blks = []
    for j in range(NSLOT):
        nmax = 8 * j + 7
        for n in range(nmax, -1, -1):
            blks.append((j, n, n == nmax, n == 0))
    NBLK = len(blks)
    st = {}
    slot_po = {}

    def stageA(i):
        j, n, first, last = blks[i]
        q_, qb = qs[j % 2], Bqs[j % 2]
        if first:
            p.dve.tensor_scalar(out=q_[:], in0=QT[:, (2 * j) * 512:(2 * j + 1) * 512], scalar1=scal_sb[:, 3:4],
                                scalar2=None, op0=ALU.mult, reads=[BQ[2 * j], Bpar], writes=[qb])
            p.dve.scalar_tensor_tensor(out=q_[:], in0=QT[:, (2 * j + 1) * 512:(2 * j + 2) * 512], scalar=scal_sb[:, 4:5],
                                       in1=q_[:], op0=ALU.mult, op1=ALU.add, reads=[BQ[2 * j + 1], Bpar, qb], writes=[qb])
            p.dve.tensor_scalar(out=q_[:], in0=q_[:], scalar1=scale, scalar2=None, op0=ALU.mult, reads=[qb], writes=[qb])
            slot_po[j] = c.psum_long()
        r = n - 8 * j
        pz, pzb = c.psum()
        mm(p, pz[:, :], KT[:, n * 128:(n + 1) * 128], q_[:], True, False, [BK[n // 4], qb], [pzb])
        if r >= 0:
            mm(p, pz[:, :], ident_bf[:], mask[:, r, :], False, False, [Bc2, Bmask], [pzb])
        e_, eb = ebuf[i % 2], Be[i % 2]
        p.act.activation(out=e_[:], in_=pz[:, :], func=AF.Exp, reads=[pzb], writes=[eb])
        s_, sb_ = spb[i % 3], Bsp[i % 3]
        p.act.activation(out=s_[:], in_=e_[:], func=AF.Ln, bias=c.one_col[:, 0:1], reads=[eb, c.Bconst], writes=[sb_])
        if not last:
            lc_, lcb = lc4[i % 4], Blc4[i % 4]
            if first:
                p.pool.tensor_copy(out=lc_[:], in_=s_[:], reads=[sb_], writes=[lcb])
            else:
                p.pool.tensor_tensor(out=lc_[:], in0=lc4[(i - 1) % 4][:], in1=s_[:], op=ALU.add,
                                     reads=[sb_, Blc4[(i - 1) % 4]], writes=[lcb])
        st[i] = (pz, pzb)

    def stageB(i):
        j, n, first, last = blks[i]
        pz, pzb = st[i]
        s_, sb_ = spb[i % 3], Bsp[i % 3]
        mm(p, pz[:, :], negtri[:], s_[:], False, first, [Bc2, sb_], [pzb])
        if not first:
            mm(p, pz[:, :], negones[:], lc4[(i - 1) % 4][:], False, True, [Bc2, Blc4[(i - 1) % 4]], [pzb])
        a_, ab = at3[i % 3], Bat3[i % 3]
        p.act.activation(out=a_[:], in_=pz[:, :], func=AF.Exp, reads=[pzb], writes=[ab])

    def stageC(i):
        j, n, first, last = blks[i]
        po, pob = slot_po[j]
        a_, ab = at3[i % 3], Bat3[i % 3]
        mm(p, po[:, :], V[:, n, :], a_[:], first, last, [BV[n // 4], ab], [pob])
        del st[i]
        if last:
            y_, yb = yo[j % 2], Byo[j % 2]
            p.dve.tensor_copy(out=y_[:], in_=po[:, :], reads=[pob], writes=[yb])
            p.dma("sp", y_sb[:, j * 512:(j + 1) * 512], y_[:], reads=[yb], is_output=True)

    for i in range(NBLK + 2):
        if i < NBLK:
            stageA(i)
        if 0 <= i - 1 < NBLK:
            stageB(i - 1)
        if 0 <= i - 2 < NBLK:
            stageC(i - 2)


def la_dn(c, S, cm, tm, dB, dnconv_sb, scal_sb, Bpar, cst, y_dn):
    p = c.p
    NCH = S // 128
    NTL = S // 512
    ident, triu, mneg, mup, Bc = cst["ident_f"], cst["triu"], cst["mneg"], cst["mup"], cst["B"]
    AL = ALU

    ab = p.sb("dn_ab", [128, NCH, 2], F32); Bab = Buf()
    for t in range(NTL):
        p.dma("sp", ab[:, t * 4:(t + 1) * 4, :], tm[t * 512:(t + 1) * 512, 192:194].rearrange("(n p) c -> p n c", p=128),
              reads=[dB[("tm", t)]], writes=[Bab])
    def colt(name):
        return p.sb("dn_" + name, [128, NCH], F32)
    gcol = colt("gcol"); bcol = colt("bcol"); gccol = colt("gccol"); glast = colt("glast"); bgcol = colt("bgcol"); kdcol = colt("kdcol")
    na = p.sb("dn_na", [128, 1], F32)
    Bs = Buf()
    p.act.activation(out=na[:], in_=scal_sb[:, 1:2], func=AF.Exp, reads=[Bpar], writes=[Bs])
    p.dve.tensor_scalar(out=na[:], in0=na[:], scalar1=-1.0, scalar2=None, op0=AL.mult, reads=[Bs], writes=[Bs])
    p.act.activation(out=gcol[:], in_=ab[:, :, 0], func=AF.Exp, bias=scal_sb[:, 2:3], reads=[Bab, Bpar, Bs], writes=[Bs])
    p.act.activation(out=gcol[:], in_=gcol[:], func=AF.Ln, bias=c.one_col[:, 0:1], reads=[Bs, c.Bconst], writes=[Bs])
    p.dve.tensor_scalar(out=gcol[:], in0=gcol[:], scalar1=na[:, 0:1], scalar2=None, op0=AL.mult, reads=[Bs], writes=[Bs])
    p.act.activation(out=bcol[:], in_=ab[:, :, 1], func=AF.Exp, scale=-1.0, reads=[Bab, Bs], writes=[Bs])
    p.dve.tensor_scalar(out=bcol[:], in0=bcol[:], scalar1=1.0, scalar2=None, op0=AL.add, reads=[Bs], writes=[Bs])
    p.dve.reciprocal(out=bcol[:], in_=bcol[:], reads=[Bs], writes=[Bs])
    ps, pb = c.psum()
    mm(p, ps[:, 0:NCH], triu[:], gcol[:], True, True, [Bc, Bs], [pb])
    p.act.activation(out=gccol[:], in_=ps[:, 0:NCH], func=AF.Copy, reads=[pb, Bs], writes=[Bs])
    ps2, pb2 = c.psum()
    mm(p, ps2[:, 0:NCH], c.ones_f[:], gcol[:], True, True, [c.Bconst, Bs], [pb2])
    p.act.activation(out=glast[:], in_=ps2[:, 0:NCH], func=AF.Copy, reads=[pb2, Bs], writes=[Bs])
    p.dve.tensor_tensor(out=kdcol[:], in0=glast[:], in1=gccol[:], op=AL.subtract, reads=[Bs], writes=[Bs])
    p.act.activation(out=kdcol[:], in_=kdcol[:], func=AF.Exp, reads=[Bs], writes=[Bs])
    p.act.activation(out=glast[:], in_=glast[:], func=AF.Exp, reads=[Bs], writes=[Bs])
    p.act.activation(out=bgcol[:], in_=gccol[:], func=AF.Exp, reads=[Bs], writes=[Bs])
    p.dve.tensor_tensor(out=bgcol[:], in0=bgcol[:], in1=bcol[:], op=AL.mult, reads=[Bs], writes=[Bs])

    def f32t(name, w=128):
        return p.sb("dn_" + name, [128, w], F32)
    raw = [[f32t("raw%d_%d" % (w, i), 515) for i in range(2)] for w in range(3)]; Braw = [[Buf() for _ in range(2)] for _ in range(3)]
    cv = [[f32t("cv%d_%d" % (w, i), 512) for i in range(2)] for w in range(3)]; Bcv = [[Buf() for _ in range(2)] for _ in range(3)]
    sqt = f32t("sqt", 512); Bsqt = Buf()
    rn = f32t("rn", 512); Brn = Buf()
    zt = [f32t("zt%d" % i, 512) for i in range(2)]; Bzt = [Buf() for _ in range(2)]
    ot = [f32t("ot%d" % i, 512) for i in range(2)]; Bot = [BufGrid() for _ in range(2)]
    yt = [f32t("yt%d" % i, 512) for i in range(2)]; Byt = [Buf() for _ in range(2)]
    NSET = 4
    names = ["Ktm", "bV", "bgK", "kdec", "Gb", "Gs", "Gu", "grow", "QgT", "N", "X0", "db", "X0T", "Pa", "Pb", "Xa", "Xb", "XTa", "XTb",
             "wT", "u", "QK", "vnew"]
    sets = [{nm: f32t("%s_%d" % (nm, i)) for nm in names} for i in range(NSET)]
    Bsets = [{nm: Buf() for nm in names} for i in range(NSET)]
    Sst = [f32t("S%d" % i) for i in range(2)]; BS = [Buf() for _ in range(2)]
    p.pool.memset(Sst[0][:], 0.0, writes=[BS[0]])

    def evac(eng, out, ps, pb, ob, extra_reads=()):
        if eng == "act":
            p.act.activation(out=out, in_=ps, func=AF.Copy, reads=[pb] + list(extra_reads), writes=[ob])
        else:
            p.dve.tensor_copy(out=out, in_=ps, reads=[pb] + list(extra_reads), writes=[ob])

    def pre(n, tl):
        cc = n % 4
        s_, B_ = sets[n % NSET], Bsets[n % NSET]
        sl = slice(cc * 128, (cc + 1) * 128)
        QT, KT, Vc = cv[0][tl % 2], cv[1][tl % 2], cv[2][tl % 2]
        BQ, BK, BV = Bcv[0][tl % 2], Bcv[1][tl % 2], Bcv[2][tl % 2]
        col = slice(n, n + 1)
        yield
        ps, pb = c.psum()
        p.pe.transpose(ps[:, 0:128], KT[:, sl], ident[:], reads=[BK, Bc], writes=[pb])
        evac("act", s_["Ktm"][:], ps[:, 0:128], pb, B_["Ktm"])
        if DN_SUB == 1:
            return
        yield
        ps, pb = c.psum()
        p.pe.transpose(ps[:, 0:128], Vc[:, sl], ident[:], reads=[BV, Bc], writes=[pb])
        p.dve.tensor_scalar(out=s_["bV"][:], in0=ps[:, 0:128], scalar1=bcol[:, col], scalar2=None, op0=AL.mult,
             reads=[pb, Bs], writes=[B_["bV"]])
        if DN_SUB == 2:
            return
        p.pool.tensor_scalar(out=s_["bgK"][:], in0=s_["Ktm"][:], scalar1=bgcol[:, col], scalar2=None, op0=AL.mult,
             reads=[B_["Ktm"], Bs], writes=[B_["bgK"]])
        p.pool.tensor_scalar(out=s_["kdec"][:], in0=s_["Ktm"][:], scalar1=kdcol[:, col], scalar2=None, op0=AL.mult,
             reads=[B_["Ktm"], Bs], writes=[B_["kdec"]])
        p.pool.tensor_scalar(out=s_["Gb"][:], in0=c.ones_f[:], scalar1=gcol[:, col], scalar2=None, op0=AL.mult,
             reads=[c.Bconst, Bs], writes=[B_["Gb"]])
        if DN_SUB == 3:
            return
        yield
        pg, pgb = c.psum()
        mm(p, pg[:, 0:128], s_["Gb"][:], triu[:], True, True, [B_["Gb"], Bc], [pgb])
        if DN_SUB == 4:
            return
        p.dve.scalar_tensor_tensor(out=s_["Gs"][:], in0=pg[:, 0:128], scalar=gccol[:, col], in1=mneg[:], op0=AL.subtract, op1=AL.max,
             reads=[pgb, Bs, Bc], writes=[B_["Gs"]])
        p.act.activation(out=s_["Gs"][:], in_=s_["Gs"][:], func=AF.Exp, scale=-1.0, reads=[B_["Gs"]], writes=[B_["Gs"]])
        p.dve.scalar_tensor_tensor(out=s_["Gu"][:], in0=pg[:, 0:128], scalar=gccol[:, col], in1=mup[:], op0=AL.subtract, op1=AL.min,
             reads=[pgb, Bs, Bc], writes=[B_["Gu"]])
        p.act.activation(out=s_["Gu"][:], in_=s_["Gu"][:], func=AF.Exp, reads=[B_["Gu"]], writes=[B_["Gu"]])
        if DN_SUB == 5:
            return
        p.act.activation(out=s_["grow"][:], in_=pg[:, 0:128], func=AF.Exp, reads=[pgb], writes=[B_["grow"]])
        p.pool.tensor_tensor(out=s_["QgT"][:], in0=QT[:, sl], in1=s_["grow"][:], op=AL.mult,
             reads=[BQ, B_["grow"]], writes=[B_["QgT"]])
        if DN_SUB == 6:
            return
        yield
        ps, pb = c.psum()
        mm(p, ps[:, 0:128], KT[:, sl], KT[:, sl], True, True, [BK], [pb])
        if DN_SUB == 61:
            return
        p.dve.tensor_tensor(out=s_["N"][:], in0=ps[:, 0:128], in1=s_["Gs"][:], op=AL.mult, reads=[pb, B_["Gs"]], writes=[B_["N"]])
        if DN_SUB == 62:
            return
        p.pool.tensor_scalar(out=s_["X0"][:], in0=s_["N"][:], scalar1=bcol[:, col], scalar2=None, op0=AL.mult,
             reads=[B_["N"], Bs], writes=[B_["X0"]])
        p.pool.tensor_scalar(out=s_["db"][:], in0=ident[:], scalar1=bcol[:, col], scalar2=None, op0=AL.mult,
             reads=[Bc, Bs], writes=[B_["db"]])
        if DN_SUB == 63:
            return
        yield
        ps, pb = c.psum()
        mm(p, ps[:, 0:128], s_["N"][:], s_["db"][:], True, True, [B_["N"], B_["db"]], [pb])
        if DN_SUB == 64:
            return
        evac("act", s_["X0T"][:], ps[:, 0:128], pb, B_["X0T"])
        if DN_SUB == 65:
            return
        p.dve.tensor_tensor(out=s_["Pa"][:], in0=ident[:], in1=ps[:, 0:128], op=AL.subtract, reads=[pb, Bc], writes=[B_["Pa"]])
        if DN_SUB == 7:
            return
        X, XT, P = "X0", "X0T", "Pa"
        for k in range(1, 7):
            Xn = "Xa" if k % 2 == 1 else "Xb"
            XTn = "XTa" if k % 2 == 1 else "XTb"
            Pn = "Pb" if P == "Pa" else "Pa"
            yield
            ps, pb = c.psum()
            mm(p, ps[:, 0:128], s_[XT][:], s_[X][:], True, True, [B_[XT], B_[X]], [pb])
            evac("act", s_[Xn][:], ps[:, 0:128], pb, B_[Xn])
            if k < 6:
                yield
                ps2, pb2 = c.psum()
                mm(p, ps2[:, 0:128], s_[X][:], s_[XT][:], True, True, [B_[XT], B_[X]], [pb2])
                evac("dve", s_[XTn][:], ps2[:, 0:128], pb2, B_[XTn])
            yield
            ps3, pb3 = c.psum()
            mm(p, ps3[:, 0:128], s_[Xn][:], s_[P][:], True, True, [B_[Xn], B_[P]], [pb3])
            p.dve.tensor_tensor(out=s_[Pn][:], in0=s_[P][:], in1=ps3[:, 0:128], op=AL.add,
                 reads=[pb3, B_[P]], writes=[B_[Pn]])
            X, XT, P = Xn, XTn, Pn
            if DN_SUB == 70 + k:
                return
        yield
        ps, pb = c.psum()
        mm(p, ps[:, 0:128], s_["bgK"][:], s_[P][:], True, True, [B_["bgK"], B_[P]], [pb])
        evac("act", s_["wT"][:], ps[:, 0:128], pb, B_["wT"])
        yield
        ps, pb = c.psum()
        mm(p, ps[:, 0:128], s_[P][:], s_["bV"][:], True, True, [B_["bV"], B_[P]], [pb])
        evac("dve", s_["u"][:], ps[:, 0:128], pb, B_["u"])
        yield
        ps, pb = c.psum()
        mm(p, ps[:, 0:128], KT[:, sl], QT[:, sl], True, True, [BK, BQ], [pb])
        p.dve.tensor_tensor(out=s_["QK"][:], in0=ps[:, 0:128], in1=s_["Gu"][:], op=AL.mult, reads=[pb, B_["Gu"]], writes=[B_["QK"]])

    def seq(n, tl):
        cc = n % 4
        s_, B_ = sets[n % NSET], Bsets[n % NSET]
        S0, BS0 = Sst[n % 2], BS[n % 2]
        S1, BS1 = Sst[(n + 1) % 2], BS[(n + 1) % 2]
        col = slice(n, n + 1)
        yield
        ps, pb = c.psum()
        mm(p, ps[:, 0:128], s_["wT"][:], S0[:], True, True, [B_["wT"], BS0], [pb])
        p.dve.tensor_tensor(out=s_["vnew"][:], in0=s_["u"][:], in1=ps[:, 0:128], op=AL.subtract, reads=[pb, B_["u"]], writes=[B_["vnew"]])
        yield
        po, pob = c.psum()
        mm(p, po[:, 0:128], S0[:], s_["QgT"][:], True, False, [BS0, B_["QgT"]], [pob])
        mm(p, po[:, 0:128], s_["vnew"][:], s_["QK"][:], False, True, [B_["vnew"], B_["QK"]], [pob])
        o_, ob = ot[tl % 2], Bot[tl % 2][cc]
        p.act.activation(out=o_[:, cc * 128:(cc + 1) * 128], in_=po[:, 0:128], func=AF.Copy, reads=[pob], writes=[ob])
        yield
        pk, pkb = c.psum()
        mm(p, pk[:, 0:128], s_["kdec"][:], s_["vnew"][:], True, True, [B_["kdec"], B_["vnew"]], [pkb])
        p.dve.scalar_tensor_tensor(out=S1[:], in0=S0[:], scalar=glast[:, col], in1=pk[:, 0:128], op0=AL.mult, op1=AL.add,
             reads=[pkb, BS0, Bs], writes=[BS1])

    def tile_front(tl):
        i2 = tl % 2
        for w in range(3):
            r_, rb = raw[w][i2], Braw[w][i2]
            if tl == 0:
                p.pool.memset(r_[:, 0:3], 0.0, writes=[rb])
                p.dma("sp", r_[:, 3:515], cm[w * 128:(w + 1) * 128, 0:512], reads=[dB[("cm", w, 0)]], writes=[rb])
            else:
                p.dma("sp", r_[:, 0:515], cm[w * 128:(w + 1) * 128, tl * 512 - 3:(tl + 1) * 512],
                      reads=[dB[("cm", w, tl)], dB[("cm", w, tl - 1)]], writes=[rb])
            c_, cb = cv[w][i2], Bcv[w][i2]
            eng = "dve" if w != 1 else "pool"
            if eng == "dve":
                p.dve.tensor_scalar(out=c_[:], in0=r_[:, 0:512], scalar1=dnconv_sb[:, w:w + 1], scalar2=None, op0=AL.mult,
                     reads=[rb, Bpar], writes=[cb])
                for i in range(1, 4):
                    p.dve.scalar_tensor_tensor(out=c_[:], in0=r_[:, i:i + 512], scalar=dnconv_sb[:, i * 3 + w:i * 3 + w + 1],
                                                                                       in1=c_[:], op0=AL.mult, op1=AL.add, reads=[rb, Bpar, cb], writes=[cb])
            else:
                p.pool.tensor_scalar(out=c_[:], in0=r_[:, 0:512], scalar1=dnconv_sb[:, w:w + 1], scalar2=None, op0=AL.mult,
                     reads=[rb, Bpar], writes=[cb])
                for i in range(1, 4):
                    p.dve.scalar_tensor_tensor(out=c_[:], in0=r_[:, i:i + 512], scalar=dnconv_sb[:, i * 3 + w:i * 3 + w + 1],
                                                                                       in1=c_[:], op0=AL.mult, op1=AL.add, reads=[rb, Bpar, cb], writes=[cb])
            p.act.activation(out=c_[:], in_=c_[:], func=AF.Silu, reads=[cb], writes=[cb])
            if w < 2:
                p.act.activation(out=sqt[:], in_=c_[:], func=AF.Square, reads=[cb], writes=[Bsqt])
                ps, pb = c.psum()
                mm(p, ps[:, :], c.ones_f[:], sqt[:], True, True, [Bsqt, c.Bconst], [pb])
                p.act.activation(out=rn[:], in_=ps[:, :], func=AF.Sqrt, bias=c.eps_col[:, 0:1], reads=[pb, c.Bconst], writes=[Brn])
                p.dve.reciprocal(out=rn[:], in_=rn[:], reads=[Brn], writes=[Brn])
                sc = 128.0 ** -0.5 if w == 0 else 1.0
                p.dve.scalar_tensor_tensor(out=c_[:], in0=c_[:], scalar=sc, in1=rn[:], op0=AL.mult, op1=AL.mult,
                     reads=[cb, Brn], writes=[cb])
        z_, zb = zt[i2], Bzt[i2]
        p.dma("sp", z_[:], cm[384:512, tl * 512:(tl + 1) * 512], reads=[dB[("cm", 3, tl)]], writes=[zb])
        p.act.activation(out=z_[:], in_=z_[:], func=AF.Silu, reads=[zb], writes=[zb])

    def tile_back(tl):
        i2 = tl % 2
        o_ = ot[i2]
        allo = [Bot[i2][cc] for cc in range(4)]
        p.act.activation(out=sqt[:], in_=o_[:], func=AF.Square, reads=allo, writes=[Bsqt])
        ps, pb = c.psum()
        mm(p, ps[:, :], c.ones_f[:], sqt[:], True, True, [Bsqt, c.Bconst], [pb])
        p.act.activation(out=rn[:], in_=ps[:, :], func=AF.Sqrt, scale=1.0 / 128.0, bias=c.eps_col[:, 0:1],
             reads=[pb, c.Bconst], writes=[Brn])
        p.dve.reciprocal(out=rn[:], in_=rn[:], reads=[Brn], writes=[Brn])
        y_, yb = yt[i2], Byt[i2]
        p.dve.scalar_tensor_tensor(out=y_[:], in0=o_[:], scalar=scal_sb[:, 5:6], in1=rn[:], op0=AL.mult, op1=AL.mult,
             reads=allo + [Bpar, Brn], writes=[yb])
        p.pool.tensor_tensor(out=y_[:], in0=y_[:], in1=zt[i2][:], op=AL.mult, reads=[yb, Bzt[i2]], writes=[yb])
        p.dma("sp", y_dn[:, tl * 512:(tl + 1) * 512], y_[:], reads=[yb], is_output=True)

    def rr(gens):
        gens = list(gens)
        while gens:
            for g in list(gens):
                try:
                    next(g)
                except StopIteration:
                    gens.remove(g)

    def seqpair(n0):
        yield from seq(n0, n0 // 4)
        yield from seq(n0 + 1, (n0 + 1) // 4)

    tile_front(0)
    rr([pre(0, 0), pre(1, 0)])
    for m in range(NCH // 2):
        n0 = 2 * m
        gl = []
        if n0 + 2 < NCH:
            tl1 = (n0 + 2) // 4
            if (n0 + 2) % 4 == 0:
                tile_front(tl1)
            gl += [pre(n0 + 2, tl1), pre(n0 + 3, tl1)]
        gl.append(seqpair(n0))
        rr(gl)
        if (n0 + 1) % 4 == 3:
            tile_back(n0 // 4)


def build_LA(S, phases="pwsd"):
    nc = bass.Bass("TRN2", target_bir_lowering=False)
    NSLOT = S // 1024

    def din(name, shape):
        return nc.dram_tensor(name, list(shape), F32, kind="ExternalInput").ap()

    hT = din("hT", [D, S]); wcm_d = din("wcm", [D, 896]); wtm_d = din("wtm", [D, 194])
    swa_bias_d = din("swa_bias", [128, 256]); sbmask_d = din("sbmask", [8, 128, 512])
    scal_d = din("scal", [128, 8]); dnconv_d = din("dnconv", [128, 12]); gmix_d = din("gmix", [128, 16])
    cst_d = din("cst", [128, 4, 128])
    y_dn = nc.dram_tensor("y_dn", [128, S], F32, kind="ExternalOutput").ap()
    y_swa = nc.dram_tensor("y_swa", [64, S], F32, kind="ExternalOutput").ap()
    y_sb = nc.dram_tensor("y_sb", [128, NSLOT * 512], F32, kind="ExternalOutput").ap()
    cm = nc.dram_tensor("cm", [896, S], F32, kind="Internal").ap()
    tm = nc.dram_tensor("tm", [S, 194], F32, kind="Internal").ap()
    with ExitStack() as es:
        p = Prog(nc, es)
        c = Ctx(p, n_wbuf=0)
        scal_sb = p.sb("scal_sb", [128, 8], F32); dnconv_sb = p.sb("dnconv_sb", [128, 12], F32); gmix_sb = p.sb("gmix_sb", [128, 16], F32)
        cst_sb = p.sb("cst_sb", [128, 4, 128], F32)
        ident_bf = p.sb("ident_bf", [128, 128], BF16); negtri = p.sb("negtri", [128, 128], BF16)
        Bpar = Buf(); Bc = Buf()
        p.dma("sp", scal_sb[:], scal_d, writes=[Bpar])
        p.dma("sp", dnconv_sb[:], dnconv_d, writes=[Bpar])
        p.dma("sp", gmix_sb[:], gmix_d, writes=[Bpar])
        p.dma("sp", cst_sb[:], cst_d, writes=[Bc])
        p.dve.tensor_copy(out=ident_bf[:], in_=cst_sb[:, 0, :], reads=[Bc], writes=[Bc])
        p.dve.tensor_tensor(out=negtri[:], in0=cst_sb[:, 1, :], in1=cst_sb[:, 0, :], op=ALU.subtract, reads=[Bc], writes=[Bc])
        p.dve.tensor_scalar(out=negtri[:], in0=negtri[:], scalar1=-1.0, scalar2=None, op0=ALU.add, reads=[Bc], writes=[Bc])
        dB = BufGrid()
        cst = {"ident_f": cst_sb[:, 0, :], "triu": cst_sb[:, 1, :], "mneg": cst_sb[:, 2, :], "mup": cst_sb[:, 3, :], "B": Bc}
        if "p" in phases:
            with p.phase():
                la_proj(c, S, hT, wcm_d, wtm_d, gmix_sb, Bpar, cm, tm, dB)
        if "w" in phases:
            with p.phase():
                la_swa(c, S, cm, tm, dB, swa_bias_d, scal_sb, Bpar, y_swa)
        if "s" in phases:
            with p.phase():
                la_sb(c, S, cm, tm, dB, sbmask_d, scal_sb, Bpar, (negtri, ident_bf, Bc), y_sb)
        if "d" in phases:
            with p.phase():
                la_dn(c, S, cm, tm, dB, dnconv_sb, scal_sb, Bpar, cst, y_dn)
        p.finish()
    return nc


def la_consts():
    i = np.arange(128)
    ident = (i[:, None] == i[None, :]).astype(np.float32)
    triu = (i[:, None] <= i[None, :]).astype(np.float32)
    mneg = np.where(i[:, None] > i[None, :], 0.0, -NEG).astype(np.float32)
    mup = np.where(i[None, :] >= i[:, None], 0.0, NEG).astype(np.float32)
    return np.ascontiguousarray(np.stack([ident, triu, mneg, mup], axis=1))


def la_core_inputs(inp, l, cidx):
    f32 = np.float32
    w = np.asarray(inp["w_in"][l], f32)
    hd = cidx; kvh = cidx // 4; sbh = cidx // 2; pr = cidx % 2
    cols = []
    cols.append(w[:, hd * 128:(hd + 1) * 128])
    cols.append(w[:, 1024 + hd * 128:1024 + (hd + 1) * 128])
    cols.append(w[:, 2048 + hd * 128:2048 + (hd + 1) * 128])
    cols.append(w[:, 3072 + hd * 128:3072 + (hd + 1) * 128])
    cols.append(w[:, 4880 + sbh * 128:4880 + (sbh + 1) * 128])
    cols.append(w[:, 5392 + sbh * 128:5392 + (sbh + 1) * 128])
    cols.append(w[:, 4112 + hd * 64:4112 + (hd + 1) * 64])
    cols.append(w[:, 4624 + kvh * 64:4624 + (kvh + 1) * 64])
    wcm = np.ascontiguousarray(np.concatenate(cols, axis=1))
    wtm = np.ascontiguousarray(np.concatenate([w[:, 5904 + sbh * 128:5904 + (sbh + 1) * 128],
                                               w[:, 4752 + kvh * 64:4752 + (kvh + 1) * 64],
                                               w[:, 4096 + hd:4097 + hd], w[:, 4104 + hd:4105 + hd]], axis=1))
    kp = np.arange(128)[:, None]; qf = np.arange(256)[None, :]
    slope = 2.0 ** (-(cidx + 1.0))
    dist = (qf - kp).astype(f32)
    swa_bias = np.where((dist >= 0) & (dist <= 127), -slope * dist, NEG).astype(f32)
    sp = np.arange(128)[:, None]; q5 = np.arange(512)[None, :]
    sbmask = np.zeros((8, 128, 512), f32)
    for r in range(8):
        if pr == 0:
            sbmask[r] = np.where(r * 128 + sp < q5, 0.0, NEG) if r < 4 else NEG
        else:
            sbmask[r] = 0.0 if r < 4 else np.where((r - 4) * 128 + sp < q5, 0.0, NEG)
    scal = np.zeros((128, 8), f32)
    scal[:, 0] = inp["swa_sinks"][l][cidx]
    scal[:, 1] = inp["dn_a_log"][l][hd]
    scal[:, 2] = inp["dn_dt_bias"][l][hd]
    scal[:, 3] = 1.0 if pr == 0 else 0.0
    scal[:, 4] = 1.0 if pr == 1 else 0.0
    scal[:, 5] = np.asarray(inp["dn_norm"][l], f32)
    dc = np.asarray(inp["dn_conv"][l], f32)
    dnconv = np.zeros((128, 12), f32)
    for i in range(4):
        for wch in range(3):
            dnconv[:, i * 3 + wch] = dc[i, wch * 1024 + hd * 128:wch * 1024 + (hd + 1) * 128]
    return {"wcm": wcm, "wtm": wtm, "swa_bias": swa_bias, "sbmask": sbmask, "scal": scal, "dnconv": dnconv,
            "gmix": _pcol(inp["norm_mix"][l], 16), "cst": la_consts()}


SEQ = 16384
DEPTH = 4
_CACHE = {}


def _prog(key, fn):
    if key not in _CACHE:
        _CACHE[key] = fn()
    return _CACHE[key]


def run_model(inp, S, NPASS, TW):
    inp = {k: np.asarray(v) for k, v in inp.items()}
    hT = np.ascontiguousarray(inp["x"][0, :S].T.astype(np.float32))
    PT = S // NCORES // NPASS
    T = PT + 2
    for l in range(DEPTH):
        ncA = _prog(("LA", S), lambda: build_LA(S))
        maps = []
        for c in range(NCORES):
            m = la_core_inputs(inp, l, c)
            m["hT"] = hT
            maps.append(m)
        resA = run_bass_kernel_spmd(ncA, maps, core_ids=list(range(NCORES))).results
        yT = np.empty((D, S), np.float32)
        for c in range(NCORES):
            r = resA[c]
            yT[c * 128:(c + 1) * 128] = r["y_dn"]
            yT[1024 + c * 64:1024 + (c + 1) * 64] = r["y_swa"]
            pr, hh = c % 2, c // 2
            for j in range(S // 1024):
                I = 2 * j + pr
                yT[1536 + hh * 128:1536 + (hh + 1) * 128, I * 512:(I + 1) * 512] = r["y_sb"][:, j * 512:(j + 1) * 512]
        del resA, maps
        final = (l == DEPTH - 1)
        ncB = _prog(("LB", T, TW, final, NPASS), lambda: build_LB(T, TW, final, NPASS))
        w = lb_weights(inp, l)
        maps = []
        for c in range(NCORES):
            m = dict(w)
            hs = np.zeros((NPASS, D, T), np.float32)
            ys = np.zeros((NPASS, D, T), np.float32)
            fl = np.ones((NPASS, 128, 1), np.float32)
            for pi in range(NPASS):
                a = (c * NPASS + pi) * PT
                if a == 0:
                    hs[pi, :, 2:] = hT[:, 0:PT]
                    ys[pi, :, 2:] = yT[:, 0:PT]
                    fl[pi] = 0.0
                else:
                    hs[pi] = hT[:, a - 2:a + PT]
                    ys[pi] = yT[:, a - 2:a + PT]
            m["hT"] = hs; m["yT"] = ys; m["flag"] = fl
            maps.append(m)
        resB = run_bass_kernel_spmd(ncB, maps, core_ids=list(range(NCORES))).results
        hT = np.empty((D, S), np.float32)
        for c in range(NCORES):
            for pi in range(NPASS):
                a = (c * NPASS + pi) * PT
                hT[:, a:a + PT] = resB[c]["hout"][pi]
        del resB, maps
    return np.ascontiguousarray(hT.T)[None].astype(np.float32)


def kernel(**inp):
    return run_model(inp, SEQ, 2, 342)
```

```python
import numpy as np
import concourse.bass as bass
import concourse.mybir as mybir
from contextlib import ExitStack

F32 = mybir.dt.float32
BF16 = mybir.dt.bfloat16
AF = mybir.ActivationFunctionType
ALU = mybir.AluOpType
AX = mybir.AxisListType

N_DMA_SEMS = 32


class Buf:
    __slots__ = ("name", "w", "r", "excl")

    def __init__(self, name="", excl=False):
        self.name = name
        self.excl = excl
        self.w = None
        self.r = {}


class _Rec:
    def __init__(self, p, eng):
        self._p = p
        self._eng = eng

    def __getattr__(self, name):
        p, eng = self._p, self._eng

        def f(*args, reads=(), writes=(), **kw):
            return p.op(eng, lambda e: getattr(e, name)(*args, **kw), reads=reads, writes=writes)
        return f


class _Phase:
    def __init__(self, p):
        self.p = p

    def __enter__(self):
        self.saved = self.p.es
        self.stack = ExitStack()
        self.p.es = self.stack
        return self

    def __exit__(self, *a):
        self.p.barrier()
        self.p.es = self.saved
        self.stack.close()
        return False


class Prog:
    ENG = ("pe", "dve", "act", "pool", "sp")

    def __init__(self, nc, es):
        self.nc = nc
        self.es = es
        self.eng = {"pe": nc.tensor, "dve": nc.vector, "act": nc.scalar, "pool": nc.gpsimd, "sp": nc.sync}
        self.q = {e: [] for e in self.ENG}
        self.cnt = {e: 0 for e in self.ENG}
        self.sem = {e: es.enter_context(nc.semaphore("sem_" + e)) for e in self.ENG}
        self.dsem = [es.enter_context(nc.semaphore("dsem%d" % i)) for i in range(N_DMA_SEMS)]
        self.dcnt = [0] * N_DMA_SEMS
        self.dnext = 0
        self.dnext2 = [0, 0]
        self.seen = {e: {} for e in self.ENG}
        self.out_tokens = []
        self.prefix = ""
        for _e in self.ENG:
            setattr(self, _e, _Rec(self, _e))

    def sb(self, name, shape, dt):
        return self.es.enter_context(self.nc.sbuf_tensor("S_" + self.prefix + name, list(shape), dt))

    def ps(self, name, shape, dt=F32):
        return self.es.enter_context(self.nc.psum_tensor(name, list(shape), dt))

    def _semh(self, key):
        return self.sem[key] if isinstance(key, str) else self.dsem[key]

    def _waits(self, e, reads, writes):
        deps = {}
        def add(tok):
            if tok is None:
                return
            k, v = tok
            if k == e and e == "pe":
                return
            if deps.get(k, 0) < v:
                deps[k] = v
        for b in reads:
            add(b.w)
            if b.excl:
                for t in b.r.items():
                    if t[0] != e:
                        add(t)
        for b in writes:
            add(b.w)
            for t in b.r.items():
                add(t)
        out = []
        for k, v in deps.items():
            if self.seen[e].get(k, 0) >= v:
                continue
            self.seen[e][k] = v
            out.append((k, v))
        return out

    def op(self, e, fn, reads=(), writes=()):
        waits = self._waits(e, reads, writes)
        self.cnt[e] += 1
        tok = (e, self.cnt[e])
        self.q[e].append((waits, fn, (e, 1)))
        self._mark(tok, reads, writes)
        return tok

    def _mark(self, tok, reads, writes):
        for b in reads:
            if b.r.get(tok[0], 0) < tok[1]:
                b.r[tok[0]] = tok[1]
        for b in writes:
            b.w = tok
            b.r = {}

    def dma(self, e, out, in_, reads=(), writes=(), is_output=False, **kw):
        half = N_DMA_SEMS // 2
        base = 0 if e == "sp" else half
        i = base + self.dnext2[e != "sp"]
        self.dnext2[e != "sp"] = (self.dnext2[e != "sp"] + 1) % half
        waits = self._waits(e, reads, writes)
        if self.dcnt[i] > 0 and self.seen[e].get(i, 0) < self.dcnt[i]:
            waits.append((i, self.dcnt[i]))
            self.seen[e][i] = self.dcnt[i]
        self.dcnt[i] += 16
        tok = (i, self.dcnt[i])
        self.q[e].append((waits, lambda eng: eng.dma_start(out=out, in_=in_, **kw), (i, 16)))
        self._mark(tok, reads, writes)
        if is_output:
            self.out_tokens.append(tok)
        return tok

    def barrier(self):
        for e in self.ENG:
            waits = []
            for k in self.ENG:
                if self.cnt[k] > 0 and not (k == e and e in ("pe", "sp")) and self.seen[e].get(k, 0) < self.cnt[k]:
                    waits.append((k, self.cnt[k]))
                    self.seen[e][k] = self.cnt[k]
            for i in range(N_DMA_SEMS):
                if self.dcnt[i] > 0 and self.seen[e].get(i, 0) < self.dcnt[i]:
                    waits.append((i, self.dcnt[i]))
                    self.seen[e][i] = self.dcnt[i]
            if waits:
                self.q[e].append((waits, None, None))

    def phase(self):
        return _Phase(self)

    def finish(self):
        fin = []
        for (k, v) in self.out_tokens:
            fin.append((k, v))
        for e in self.ENG:
            if e != "sp" and self.cnt[e] > 0:
                fin.append((e, self.cnt[e]))
        for i in range(N_DMA_SEMS):
            if self.dcnt[i] > 0:
                fin.append((i, self.dcnt[i]))
        self.q["sp"].append((fin, None, None))
        nc = self.nc
        with nc.Block() as block:
            def mk(ename):
                def run(eng):
                    for waits, fn, inc in self.q[ename]:
                        for (k, v) in waits:
                            eng.wait_ge(self._semh(k), v)
                        if fn is not None:
                            ins = fn(eng)
                            ins.then_inc(self._semh(inc[0]), inc[1])
                return run
            if self.q["pe"]:
                block.tensor(mk("pe"))
            if self.q["dve"]:
                block.vector(mk("dve"))
            if self.q["act"]:
                block.scalar(mk("act"))
            if self.q["pool"]:
                block.gpsimd(mk("pool"))
            block.sync(mk("sp"))


from concourse.bass_utils import run_bass_kernel_spmd

D = 2048
DC = 16
EPS = 1e-6
NCORES = 8


class Ctx:
    def __init__(self, p, n_wbuf=2, wbuf_elems=8192):
        self.p = p
        nc = p.nc
        self.ps_t = [p.ps("psb%d" % i, [128, 512]) for i in range(8)]
        self.ps_b = [Buf("psb%d" % i, excl=True) for i in range(8)]
        self.ps_i = 0
        self.psl_i = 0
        self.wb_t = [p.sb("wbuf%d" % i, [128, wbuf_elems], BF16) for i in range(n_wbuf)]
        self.wb_b = [Buf("wbuf%d" % i) for i in range(n_wbuf)]
        self.wb_i = 0
        self.wbuf_elems = wbuf_elems
        self.ones_bf = p.sb("ones_bf", [128, 128], BF16)
        self.ones_f = p.sb("ones_f", [128, 128], F32)
        self.Bconst = Buf("const")
        p.pool.memset(self.ones_bf[:], 1.0, writes=[self.Bconst])
        p.pool.memset(self.ones_f[:], 1.0, writes=[self.Bconst])
        self.eps_col = p.sb("eps_col", [128, 1], F32)
        self.one_col = p.sb("one_col", [128, 1], F32)
        p.pool.memset(self.eps_col[:], EPS, writes=[self.Bconst])
        p.pool.memset(self.one_col[:], 1.0, writes=[self.Bconst])

    def psum(self):
        i = self.ps_i
        self.ps_i = (i + 1) % 6
        return self.ps_t[i], self.ps_b[i]

    def psum_long(self):
        i = 6 + self.psl_i
        self.psl_i = 1 - self.psl_i
        return self.ps_t[i], self.ps_b[i]

    def wbuf(self):
        i = self.wb_i
        self.wb_i = (i + 1) % len(self.wb_t)
        return self.wb_t[i], self.wb_b[i]


def mm(p, out, lhsT, rhs, start, stop, reads, writes):
    return p.pe.matmul(out, lhsT, rhs, start=start, stop=stop, reads=reads, writes=writes)


def linear_T(c, w_ap, K, M, rhs_fn, tiles, epilogue, mg=512):
    p = c.p
    KC = K // 128
    while KC * mg > c.wbuf_elems:
        mg //= 2
    for m0 in range(0, M, mg):
        mw = min(mg, M - m0)
        wt, wb = c.wbuf()
        view = wt[:, 0:KC * mw].rearrange("p (k m) -> p k m", m=mw)
        p.dma("pool", view, w_ap[:, m0:m0 + mw].rearrange("(k p) m -> p k m", p=128), writes=[wb])
        for mi in range(mw // 128):
            for ti, (t0, tw) in enumerate(tiles):
                ps, pb = c.psum()
                for kc in range(KC):
                    r, rb = rhs_fn(kc, ti)
                    mm(p, ps[:, 0:tw], view[:, kc, mi * 128:(mi + 1) * 128], r, kc == 0, kc == KC - 1,
                       [wb, rb, c.Bconst], [pb])
                epilogue(m0 // 128 + mi, ti, (t0, tw), ps, pb)


def rms_stats(c, src_fn, nchunks, tiles, T, dim, name):
    p = c.p
    rstd = p.sb(name + "_rstd", [128, T], F32)
    Brstd = Buf(name + "_rstd")
    sq = [p.sb(name + "_sq%d" % i, [128, 512], BF16) for i in range(2)]
    Bsq = [Buf() for i in range(2)]
    k = 0
    for ti, (t0, tw) in enumerate(tiles):
        ps, pb = c.psum()
        for ci in range(nchunks):
            s, sbuf_ = src_fn(ci, ti)
            q, qb = sq[k % 2], Bsq[k % 2]
            k += 1
            p.act.activation(out=q[:, 0:tw], in_=s, func=AF.Square,
                 reads=[sbuf_], writes=[qb])
            mm(p, ps[:, 0:tw], c.ones_bf[:], q[:, 0:tw], ci == 0, ci == nchunks - 1, [qb, c.Bconst], [pb])
        p.act.activation(out=rstd[:, t0:t0 + tw], in_=ps[:, 0:tw], func=AF.Sqrt,
                                                                  scale=1.0 / dim, bias=c.eps_col[:, 0:1],
             reads=[pb, c.Bconst], writes=[Brstd])
        p.dve.reciprocal(out=rstd[:, t0:t0 + tw], in_=rstd[:, t0:t0 + tw],
             reads=[Brstd], writes=[Brstd])
    return rstd, Brstd


class BufGrid:
    def __init__(self):
        self.d = {}

    def __getitem__(self, k):
        if k not in self.d:
            self.d[k] = Buf(str(k))
        return self.d[k]


def build_LB(T, TW, final, NPASS=1):
    nc = bass.Bass("TRN2", target_bir_lowering=False)
    tiles = [(t0, min(TW, T - t0)) for t0 in range(0, T, TW)]
    NT = len(tiles)
    TWP = max(TW, 256)

    def din(name, shape):
        return nc.dram_tensor(name, list(shape), F32, kind="ExternalInput").ap()

    hT_all = din("hT", [NPASS, D, T]); yT_all = din("yT", [NPASS, D, T]); memT = din("memT", [D, 256])
    w_gate = din("w_gate", [D, 3 * D]); w_br = din("w_br", [D, D]); w_o = din("w_o", [D, D])
    w_xq = din("w_xq", [D, 512]); w_xkv = din("w_xkv", [D, 1024]); w_xo = din("w_xo", [512, D])
    w_up = din("w_up", [D, 8192]); w_down = din("w_down", [4096, D])
    gains = din("gains", [128, 5 * 16])
    fconv = din("fconv", [128, 3 * 64])
    flag_all = din("flag", [NPASS, 128, 1])
    hout_all = nc.dram_tensor("hout", [NPASS, D, T - 2], F32, kind="ExternalOutput").ap()
    hA = nc.dram_tensor("hA", [D, T], F32, kind="Internal").ap()
    hB = nc.dram_tensor("hB", [D, T], F32, kind="Internal").ap()
    hC = nc.dram_tensor("hC", [D, T], F32, kind="Internal").ap()

    with ExitStack() as es:
        p = Prog(nc, es)
        c = Ctx(p, n_wbuf=3, wbuf_elems=4096)
        dB = {"hT": BufGrid(), "hA": BufGrid(), "hB": BufGrid(), "hC": BufGrid()}

        def one_pass(pi):
            hT = hT_all[pi]; yT = yT_all[pi]; flag = flag_all[pi]; hout = hout_all[pi]
            xn = p.sb("xn", [128, 16, T], BF16); xnB = BufGrid()
            big = p.sb("big", [128, 32, T], BF16); bigB = BufGrid()
            qo = p.sb("qo", [128, 8, T], BF16); qoB = BufGrid()
            gains_sb = p.sb("gains_sb", [128, 80], F32); fconv_sb = p.sb("fconv_sb", [128, 192], F32)
            flag_sb = p.sb("flag_sb", [128, 1], F32)
            Bpar = Buf("params")
            p.dma("sp", gains_sb[:], gains, writes=[Bpar])
            p.dma("sp", fconv_sb[:], fconv, writes=[Bpar])
            p.dma("sp", flag_sb[:], flag, writes=[Bpar])
            ssq = p.sb("ssq", [128, T], F32); Bssq = BufGrid()
            rstd = p.sb("rstd", [128, T], F32); Brstd = BufGrid()
            NST = 3
            st_h = [p.sb("st_h%d" % i, [128, TWP], F32) for i in range(NST)]; Bst_h = [Buf() for _ in range(NST)]
            st_n = [p.sb("st_n%d" % i, [128, TWP], F32) for i in range(NST)]; Bst_n = [Buf() for _ in range(NST)]
            st_q = [p.sb("st_q%d" % i, [128, TWP], F32) for i in range(2)]; Bst_q = [Buf() for _ in range(2)]
            cnt = {"h": 0, "n": 0, "q": 0}

            def stg(kind):
                arr, bufs, n = {"h": (st_h, Bst_h, NST), "n": (st_n, Bst_n, NST), "q": (st_q, Bst_q, 2)}[kind]
                i = cnt[kind] % n
                cnt[kind] += 1
                return arr[i], bufs[i]


            def compute_rstd():
                for ti, (t0, tw) in enumerate(tiles):
                    ps, pb = c.psum()
                    mm(p, ps[:, 0:tw], c.ones_f[:], ssq[:, t0:t0 + tw], True, True, [Bssq[ti], c.Bconst], [pb])
                    p.act.activation(out=rstd[:, t0:t0 + tw], in_=ps[:, 0:tw],
                                                                              func=AF.Sqrt, scale=1.0 / D, bias=c.eps_col[:, 0:1],
                         reads=[pb, c.Bconst], writes=[Brstd[ti]])
                    p.dve.reciprocal(out=rstd[:, t0:t0 + tw], in_=rstd[:, t0:t0 + tw],
                         reads=[Brstd[ti]], writes=[Brstd[ti]])

            def accum_ssq(src, srcB, ci, ti, t0, tw):
                if ci == 0:
                    p.act.activation(out=ssq[:, t0:t0 + tw], in_=src, func=AF.Square,
                         reads=[srcB], writes=[Bssq[ti]])
                else:
                    q, qb = stg("q")
                    p.act.activation(out=q[:, 0:tw], in_=src, func=AF.Square, reads=[srcB], writes=[qb])
                    getattr(p, LB_EW).tensor_tensor(out=ssq[:, t0:t0 + tw], in0=ssq[:, t0:t0 + tw], in1=q[:, 0:tw], op=ALU.add,
                         reads=[qb, Bssq[ti]], writes=[Bssq[ti]])

            def stats_from_dram(src, srcname):
                for ti, (t0, tw) in enumerate(tiles):
                    for ci in range(16):
                        s, sb_ = stg("h")
                        p.dma("sp", s[:, 0:tw], src[ci * 128:(ci + 1) * 128, t0:t0 + tw], reads=[dB[srcname][(ci, ti)]], writes=[sb_])
                        accum_ssq(s[:, 0:tw], sb_, ci, ti, t0, tw)
                compute_rstd()

            def normalize_from_dram(src, srcname, gj):
                for ti, (t0, tw) in enumerate(tiles):
                    for ci in range(16):
                        s, sb_ = stg("h")
                        p.dma("sp", s[:, 0:tw], src[ci * 128:(ci + 1) * 128, t0:t0 + tw], reads=[dB[srcname][(ci, ti)]], writes=[sb_])
                        p.dve.scalar_tensor_tensor(
                            out=xn[:, ci, t0:t0 + tw], in0=s[:, 0:tw], scalar=gains_sb[:, gj * 16 + ci:gj * 16 + ci + 1],
                            in1=rstd[:, t0:t0 + tw], op0=ALU.mult, op1=ALU.mult,
                            reads=[sb_, Bpar, Brstd[ti]], writes=[xnB[(ci, ti)]])

            def xn_rhs(kc, ti):
                t0, tw = tiles[ti]
                return xn[:, kc, t0:t0 + tw], xnB[(kc, ti)]

            def residual_epilogue(prev, prevname, nxt, nxtname, out_dram=None):
                def ep(m, ti, tt, ps, pb):
                    t0, tw = tt
                    s, sb_ = stg("h")
                    p.dma("sp", s[:, 0:tw], prev[m * 128:(m + 1) * 128, t0:t0 + tw], reads=[dB[prevname][(m, ti)]], writes=[sb_])
                    n, nb = stg("n")
                    p.dve.tensor_tensor(out=n[:, 0:tw], in0=s[:, 0:tw], in1=ps[:, 0:tw], op=ALU.add,
                         reads=[sb_, pb], writes=[nb])
                    if nxt is not None:
                        p.dma("sp", nxt[m * 128:(m + 1) * 128, t0:t0 + tw], n[:, 0:tw], reads=[nb], writes=[dB[nxtname][(m, ti)]])
                        accum_ssq(n[:, 0:tw], nb, m, ti, t0, tw)
                    if out_dram is not None:
                        a0 = max(t0, 2)
                        p.dma("sp", out_dram[m * 128:(m + 1) * 128, a0 - 2:t0 + tw - 2], n[:, a0 - t0:tw], reads=[nb], is_output=True)
                return ep


            stats_from_dram(hT, "hT")
            normalize_from_dram(hT, "hT", 0)
            for ci in range(16):
                p.dma("pool", big[:, 16 + ci, :], yT[ci * 128:(ci + 1) * 128, :], writes=[bigB[16 + ci]])

            macc = p.sb("macc", [128, 2, T], F32); maccB = BufGrid()
            sg = [p.sb("sg%d" % i, [128, TWP], F32) for i in range(2)]; Bsg = [Buf() for _ in range(2)]
            br_rows = [(0, 8), (8, 4), (12, 4)]
            sgi = 0
            for m0 in range(0, D, 256):
                for br in range(3):
                    wg, wgb = c.wbuf()
                    vg = wg[:, 0:16 * 256].rearrange("p (k m) -> p k m", m=256)
                    p.dma("pool", vg, w_gate[:, br * D + m0:br * D + m0 + 256].rearrange("(k p) m -> p k m", p=128), writes=[wgb])
                    k0, nk = br_rows[br]
                    wz, wzb = c.wbuf()
                    vz = wz[:, 0:nk * 256].rearrange("p (k m) -> p k m", m=256)
                    p.dma("pool", vz, w_br[k0 * 128:(k0 + nk) * 128, m0:m0 + 256].rearrange("(k p) m -> p k m", p=128), writes=[wzb])
                    for mi in range(2):
                        for ti, (t0, tw) in enumerate(tiles):
                            pg, pgb = c.psum()
                            for kc in range(16):
                                mm(p, pg[:, 0:tw], vg[:, kc, mi * 128:(mi + 1) * 128], xn[:, kc, t0:t0 + tw], kc == 0, kc == 15,
                                   [wgb, xnB[(kc, ti)]], [pgb])
                            pz, pzb = c.psum()
                            for kc in range(nk):
                                mm(p, pz[:, 0:tw], vz[:, kc, mi * 128:(mi + 1) * 128], big[:, 16 + k0 + kc, t0:t0 + tw], kc == 0, kc == nk - 1,
                                   [wzb, bigB[16 + k0 + kc]], [pzb])
                            s_, sb_ = sg[sgi % 2], Bsg[sgi % 2]
                            sgi += 1
                            p.act.activation(out=s_[:, 0:tw], in_=pg[:, 0:tw], func=AF.Sigmoid,
                                 reads=[pgb], writes=[sb_])
                            mB = maccB[(mi, ti)]
                            if br == 0:
                                p.dve.tensor_tensor(
                                    out=macc[:, mi, t0:t0 + tw], in0=s_[:, 0:tw], in1=pz[:, 0:tw], op=ALU.mult,
                                    reads=[sb_, pzb], writes=[mB])
                            else:
                                p.dve.tensor_tensor(
                                    out=s_[:, 0:tw], in0=s_[:, 0:tw], in1=pz[:, 0:tw], op=ALU.mult,
                                    reads=[sb_, pzb], writes=[sb_])
                                if br == 1:
                                    getattr(p, LB_EW).tensor_tensor(
                                        out=macc[:, mi, t0:t0 + tw], in0=macc[:, mi, t0:t0 + tw], in1=s_[:, 0:tw], op=ALU.add,
                                        reads=[sb_, mB], writes=[mB])
                                else:
                                    mc = m0 // 128 + mi
                                    getattr(p, LB_EW).tensor_tensor(
                                        out=big[:, mc, t0:t0 + tw], in0=macc[:, mi, t0:t0 + tw], in1=s_[:, 0:tw], op=ALU.add,
                                        reads=[sb_, mB], writes=[bigB[(mc, ti)]])

            linear_T(c, w_o, D, D, lambda kc, ti: (big[:, kc, tiles[ti][0]:tiles[ti][0] + tiles[ti][1]], bigB[(kc, ti)]),
                     tiles, residual_epilogue(hT, "hT", hA, "hA"))
            compute_rstd()
            normalize_from_dram(hA, "hA", 1)

            memn = p.sb("memn", [128, 16, 256], BF16); memnB = BufGrid()
            mst = [p.sb("mst%d" % i, [128, 256], F32) for i in range(2)]; Bmst = [Buf() for _ in range(2)]
            mssq = p.sb("mssq", [128, 256], F32); Bmssq = Buf()
            mrstd = p.sb("mrstd", [128, 256], F32); Bmrstd = Buf()
            for ci in range(16):
                s, sb_ = mst[ci % 2], Bmst[ci % 2]
                p.dma("sp", s[:], memT[ci * 128:(ci + 1) * 128, :], writes=[sb_])
                if ci == 0:
                    p.act.activation(out=mssq[:], in_=s[:], func=AF.Square, reads=[sb_], writes=[Bmssq])
                else:
                    q, qb = stg("q")
                    p.act.activation(out=q[:, 0:256], in_=s[:], func=AF.Square, reads=[sb_], writes=[qb])
                    getattr(p, LB_EW).tensor_tensor(out=mssq[:], in0=mssq[:], in1=q[:, 0:256], op=ALU.add,
                         reads=[qb, Bmssq], writes=[Bmssq])
            ps, pb = c.psum()
            mm(p, ps[:, 0:256], c.ones_f[:], mssq[:], True, True, [Bmssq, c.Bconst], [pb])
            p.act.activation(out=mrstd[:], in_=ps[:, 0:256], func=AF.Sqrt, scale=1.0 / D, bias=c.eps_col[:, 0:1],
                 reads=[pb, c.Bconst], writes=[Bmrstd])
            p.dve.reciprocal(out=mrstd[:], in_=mrstd[:], reads=[Bmrstd], writes=[Bmrstd])
            for ci in range(16):
                s, sb_ = mst[ci % 2], Bmst[ci % 2]
                p.dma("sp", s[:], memT[ci * 128:(ci + 1) * 128, :], writes=[sb_])
                p.dve.scalar_tensor_tensor(
                    out=memn[:, ci, :], in0=s[:], scalar=gains_sb[:, 2 * 16 + ci:2 * 16 + ci + 1], in1=mrstd[:],
                    op0=ALU.mult, op1=ALU.mult, reads=[sb_, Bpar, Bmrstd], writes=[memnB[ci]])
            KxT = p.sb("KxT", [128, 4, 256], BF16); KxTB = BufGrid()
            Vx = p.sb("Vx", [128, 2, 512], BF16); VxB = BufGrid()

            def ep_k(m, ti, tt, ps, pb):
                p.act.activation(out=KxT[:, m, :], in_=ps[:, 0:256], func=AF.Copy, reads=[pb], writes=[KxTB[m]])
            linear_T(c, w_xkv[:, 0:512], D, 512, lambda kc, ti: (memn[:, kc, :], memnB[kc]), [(0, 256)], ep_k)
            for half in range(2):
                wt, wb = c.wbuf()
                vw = wt[:, 0:16 * 256].rearrange("p (k m) -> p k m", m=256)
                p.dma("pool", vw, w_xkv[:, 512 + half * 256:512 + (half + 1) * 256].rearrange("(k p) m -> p k m", p=128), writes=[wb])
                for mc in range(2):
                    ps, pb = c.psum()
                    for kc in range(16):
                        mm(p, ps[:, 0:256], memn[:, kc, mc * 128:(mc + 1) * 128], vw[:, kc, :], kc == 0, kc == 15, [wb, memnB[kc]], [pb])
                    p.act.activation(out=Vx[:, mc, half * 256:(half + 1) * 256], in_=ps[:, 0:256], func=AF.Copy,
                         reads=[pb], writes=[VxB[(mc, half)]])

            def ep_q(m, ti, tt, ps, pb):
                t0, tw = tt
                p.act.activation(out=qo[:, m, t0:t0 + tw], in_=ps[:, 0:tw], func=AF.Copy, scale=128.0 ** -0.5,
                     reads=[pb], writes=[qoB[(m, ti)]])
            linear_T(c, w_xq, D, 512, xn_rhs, tiles, ep_q)
            pt = [p.sb("pt%d" % i, [128, TWP], BF16) for i in range(4)]; Bpt = [Buf() for _ in range(4)]
            rden = [p.sb("rden%d" % i, [128, TWP], F32) for i in range(2)]; Brden = [Buf() for _ in range(2)]
            k = 0
            for hd in range(4):
                for ti, (t0, tw) in enumerate(tiles):
                    pts = []
                    for mc in range(2):
                        ps, pb = c.psum()
                        mm(p, ps[:, 0:tw], KxT[:, hd, mc * 128:(mc + 1) * 128], qo[:, hd, t0:t0 + tw], True, True,
                           [KxTB[hd], qoB[(hd, ti)]], [pb])
                        e_, eb = pt[k % 4], Bpt[k % 4]
                        k += 1
                        p.act.activation(out=e_[:, 0:tw], in_=ps[:, 0:tw], func=AF.Exp,
                             reads=[pb], writes=[eb])
                        pts.append((e_, eb))
                    po, pob = c.psum()
                    pd, pdb = c.psum()
                    for mc in range(2):
                        mm(p, po[:, 0:tw], Vx[:, mc, hd * 128:(hd + 1) * 128], pts[mc][0][:, 0:tw], mc == 0, mc == 1,
                           [VxB[(mc, hd // 2)], pts[mc][1]], [pob])
                    for mc in range(2):
                        mm(p, pd[:, 0:tw], c.ones_bf[:], pts[mc][0][:, 0:tw], mc == 0, mc == 1, [c.Bconst, pts[mc][1]], [pdb])
                    r_, rb = rden[(k // 2) % 2], Brden[(k // 2) % 2]
                    p.dve.reciprocal(out=r_[:, 0:tw], in_=pd[:, 0:tw], reads=[pdb], writes=[rb])
                    p.dve.tensor_tensor(
                        out=qo[:, 4 + hd, t0:t0 + tw], in0=po[:, 0:tw], in1=r_[:, 0:tw], op=ALU.mult,
                        reads=[pob, rb], writes=[qoB[(4 + hd, ti)]])

            linear_T(c, w_xo, 512, D, lambda kc, ti: (qo[:, 4 + kc, tiles[ti][0]:tiles[ti][0] + tiles[ti][1]], qoB[(4 + kc, ti)]),
                     tiles, residual_epilogue(hA, "hA", hB, "hB"))
            compute_rstd()
            normalize_from_dram(hB, "hB", 3)

            ug = p.sb("ug", [128, T + 2], F32); uv = p.sb("uv", [128, T + 2], F32)
            Bug = BufGrid(); Buv = BufGrid()
            cg = p.sb("cg", [128, T], F32); cv = p.sb("cv", [128, T], F32); Bcg = Buf(); Bcv = Buf()
            p.pool.memset(ug[:, 0:2], 0.0, writes=[Bug["pad"]])
            p.pool.memset(uv[:, 0:2], 0.0, writes=[Buv["pad"]])
            for f in range(32):
                for which, (u, Bu, col0) in enumerate(((ug, Bug, f * 128), (uv, Buv, 4096 + f * 128))):
                    wt, wb = c.wbuf()
                    vw = wt[:, 0:16 * 128].rearrange("p (k m) -> p k m", m=128)
                    p.dma("pool", vw, w_up[:, col0:col0 + 128].rearrange("(k p) m -> p k m", p=128), writes=[wb])
                    for ti, (t0, tw) in enumerate(tiles):
                        ps, pb = c.psum()
                        for kc in range(16):
                            mm(p, ps[:, 0:tw], vw[:, kc, :], xn[:, kc, t0:t0 + tw], kc == 0, kc == 15, [wb, xnB[(kc, ti)]], [pb])
                        p.act.activation(out=u[:, 2 + t0:2 + t0 + tw], in_=ps[:, 0:tw], func=AF.Copy,
                             reads=[pb], writes=[Bu[ti]])
                    p.dve.tensor_scalar(out=u[:, 2:4], in0=u[:, 2:4], scalar1=flag_sb[:, 0:1], scalar2=None, op0=ALU.mult,
                         reads=[Bu[0], Bpar], writes=[Bu[0]])
                    cc, Bcc = (cg, Bcg) if which == 0 else (cv, Bcv)
                    ch = (col0 // 128)
                    allu = [Bu[ti] for ti in range(NT)] + [Bu["pad"], Bpar]
                    p.dve.tensor_scalar(out=cc[:, :], in0=u[:, 0:T], scalar1=fconv_sb[:, 0 * 64 + ch:0 * 64 + ch + 1],
                                                                          scalar2=None, op0=ALU.mult, reads=allu, writes=[Bcc])
                    p.dve.scalar_tensor_tensor(out=cc[:, :], in0=u[:, 1:T + 1], scalar=fconv_sb[:, 1 * 64 + ch:1 * 64 + ch + 1],
                                                                                 in1=cc[:, :], op0=ALU.mult, op1=ALU.add, reads=allu + [Bcc], writes=[Bcc])
                    p.dve.scalar_tensor_tensor(out=cc[:, :], in0=u[:, 2:T + 2], scalar=fconv_sb[:, 2 * 64 + ch:2 * 64 + ch + 1],
                                                                                 in1=cc[:, :], op0=ALU.mult, op1=ALU.add, reads=allu + [Bcc], writes=[Bcc])
                p.act.activation(out=cg[:, :], in_=cg[:, :], func=AF.Silu, reads=[Bcg], writes=[Bcg])
                getattr(p, LB_EW).tensor_tensor(out=big[:, f, :], in0=cg[:, :], in1=cv[:, :], op=ALU.mult,
                     reads=[Bcg, Bcv], writes=[bigB[("A", f)]])

            if not final:
                linear_T(c, w_down, 4096, D, lambda kc, ti: (big[:, kc, tiles[ti][0]:tiles[ti][0] + tiles[ti][1]], bigB[("A", kc)]),
                         tiles, residual_epilogue(hB, "hB", None, None, out_dram=hout))
            else:
                linear_T(c, w_down, 4096, D, lambda kc, ti: (big[:, kc, tiles[ti][0]:tiles[ti][0] + tiles[ti][1]], bigB[("A", kc)]),
                         tiles, residual_epilogue(hB, "hB", hC, "hC"))
                compute_rstd()
                for ti, (t0, tw) in enumerate(tiles):
                    for ci in range(16):
                        s, sb_ = stg("h")
                        p.dma("sp", s[:, 0:tw], hC[ci * 128:(ci + 1) * 128, t0:t0 + tw], reads=[dB["hC"][(ci, ti)]], writes=[sb_])
                        n, nb = stg("n")
                        p.dve.scalar_tensor_tensor(
                            out=n[:, 0:tw], in0=s[:, 0:tw], scalar=gains_sb[:, 4 * 16 + ci:4 * 16 + ci + 1],
                            in1=rstd[:, t0:t0 + tw], op0=ALU.mult, op1=ALU.mult,
                            reads=[sb_, Bpar, Brstd[ti]], writes=[nb])
                        a0 = max(t0, 2)
                        p.dma("sp", hout[ci * 128:(ci + 1) * 128, a0 - 2:t0 + tw - 2], n[:, a0 - t0:tw], reads=[nb], is_output=True)
        for pi in range(NPASS):
            p.prefix = "p%d_" % pi
            with p.phase():
                one_pass(pi)
        p.finish()
    return nc


def _pcol(v, n):
    return np.ascontiguousarray(np.asarray(v, np.float32).reshape(n, 128).T)


def lb_weights(inp, l):
    f32 = np.float32
    gains = np.concatenate([_pcol(inp["norm_mix"][l], 16), _pcol(inp["norm_xattn"][l], 16), _pcol(inp["norm_mem"][l], 16),
                            _pcol(inp["norm_ffn"][l], 16), _pcol(inp["norm_final"], 16)], axis=1)
    fc = np.asarray(inp["ffn_conv"][l], f32)
    fconv = np.concatenate([_pcol(fc[i], 64) for i in range(3)], axis=1)
    return {
        "memT": np.ascontiguousarray(np.asarray(inp["mem"][0], f32).T),
        "w_gate": np.ascontiguousarray(inp["w_in"][l][:, 6416:]),
        "w_br": np.ascontiguousarray(np.concatenate([inp["w_br_dn"][l], inp["w_br_swa"][l], inp["w_br_sb"][l]], axis=0)),
        "w_o": np.ascontiguousarray(inp["w_o"][l]),
        "w_xq": np.ascontiguousarray(inp["w_xq"][l]), "w_xkv": np.ascontiguousarray(inp["w_xkv"][l]),
        "w_xo": np.ascontiguousarray(inp["w_xo"][l]),
        "w_up": np.ascontiguousarray(inp["w_up"][l]), "w_down": np.ascontiguousarray(inp["w_down"][l]),
        "gains": np.ascontiguousarray(gains), "fconv": np.ascontiguousarray(fconv),
    }


NEG = -30000.0
LA_EW = "dve"
LB_EW = "dve"
DN_STAGE = 9
DN_SUB = 99


def la_proj(c, S, hT, wcm_d, wtm_d, gmix_sb, Bpar, cm, tm, dB):
    p = c.p
    KC = 16
    wcm = p.sb("wcm_sb", [128, 16, 896], BF16); Bwcm = Buf()
    wtm = p.sb("wtm_sb", [128, 16, 194], BF16); Bwtm = Buf()
    for k0 in range(0, 16, 4):
        p.dma("pool", wcm[:, k0:k0 + 4, :], wcm_d[k0 * 128:(k0 + 4) * 128, :].rearrange("(k p) m -> p k m", p=128), writes=[Bwcm])
    for k0 in range(0, 16, 8):
        p.dma("pool", wtm[:, k0:k0 + 8, :], wtm_d[k0 * 128:(k0 + 8) * 128, :].rearrange("(k p) m -> p k m", p=128), writes=[Bwtm])
    hs = [p.sb("la_hs%d" % i, [128, 16, 512], F32) for i in range(2)]; Bhs = [Buf() for _ in range(2)]
    sq = p.sb("la_sq", [128, 16, 512], BF16); Bsq = Buf()
    xn = [p.sb("la_xn%d" % i, [128, 16, 512], BF16) for i in range(2)]; Bxn = [Buf() for _ in range(2)]
    rstd = p.sb("la_rstd", [128, 512], F32); Brstd = Buf()
    ev = [p.sb("la_ev%d" % i, [128, 512], F32) for i in range(4)]; Bev = [Buf() for _ in range(4)]
    evi = 0
    groups = [(g * 128, 128) for g in range(6)] + [(768, 64), (832, 64)]
    for t in range(S // 512):
        h_, hb = hs[t % 2], Bhs[t % 2]
        for k0 in range(0, 16, 4):
            p.dma("sp", h_[:, k0:k0 + 4, :], hT[k0 * 128:(k0 + 4) * 128, t * 512:(t + 1) * 512].rearrange("(k p) m -> p k m", p=128), writes=[hb])
        p.act.activation(out=sq[:], in_=h_[:], func=AF.Square, reads=[hb], writes=[Bsq])
        ps, pb = c.psum()
        for kc in range(16):
            mm(p, ps[:, :], c.ones_bf[:], sq[:, kc, :], kc == 0, kc == 15, [Bsq, c.Bconst], [pb])
        p.act.activation(out=rstd[:], in_=ps[:, :], func=AF.Sqrt, scale=1.0 / D, bias=c.eps_col[:, 0:1],
             reads=[pb, c.Bconst], writes=[Brstd])
        p.dve.reciprocal(out=rstd[:], in_=rstd[:], reads=[Brstd], writes=[Brstd])
        x_, xb = xn[t % 2], Bxn[t % 2]
        for kc in range(16):
            eng = "dve" if kc % 2 == 0 else "pool"
            if eng == "dve":
                p.dve.scalar_tensor_tensor(
                    out=x_[:, kc, :], in0=h_[:, kc, :], scalar=gmix_sb[:, kc:kc + 1], in1=rstd[:], op0=ALU.mult, op1=ALU.mult,
                    reads=[hb, Bpar, Brstd], writes=[xb])
            else:
                p.dve.scalar_tensor_tensor(
                    out=x_[:, kc, :], in0=h_[:, kc, :], scalar=gmix_sb[:, kc:kc + 1], in1=rstd[:], op0=ALU.mult, op1=ALU.mult,
                    reads=[hb, Bpar, Brstd], writes=[xb])
        for g, (c0, mw) in enumerate(groups):
            ps, pb = c.psum()
            for kc in range(16):
                mm(p, ps[0:mw, :], wcm[:, kc, c0:c0 + mw], x_[:, kc, :], kc == 0, kc == 15, [Bwcm, xb], [pb])
            e_, eb = ev[evi % 4], Bev[evi % 4]
            evi += 1
            p.act.activation(out=e_[0:mw, :], in_=ps[0:mw, :], func=AF.Copy, reads=[pb], writes=[eb])
            r0 = g * 128 if g < 6 else (768 if g == 6 else 832)
            p.dma("sp", cm[r0:r0 + mw, t * 512:(t + 1) * 512], e_[0:mw, :], reads=[eb], writes=[dB[("cm", g, t)]])
        for s4 in range(4):
            ps, pb = c.psum()
            for kc in range(16):
                mm(p, ps[:, 0:194], x_[:, kc, s4 * 128:(s4 + 1) * 128], wtm[:, kc, :], kc == 0, kc == 15, [Bwtm, xb], [pb])
            e_, eb = ev[evi % 4], Bev[evi % 4]
            evi += 1
            p.dve.tensor_copy(out=e_[:, 0:194], in_=ps[:, 0:194], reads=[pb], writes=[eb])
            p.dma("sp", tm[t * 512 + s4 * 128:t * 512 + (s4 + 1) * 128, :], e_[:, 0:194], reads=[eb], writes=[dB[("tm", t)]])


def la_swa(c, S, cm, tm, dB, swa_bias_d, scal_sb, Bpar, y_swa):
    p = c.p
    NB = S // 128
    NTL = S // 512
    QT = p.sb("swa_QT", [64, S], BF16); KT = p.sb("swa_KT", [64, S], BF16)
    V = p.sb("swa_V", [128, NB, 64], BF16)
    BQ = BufGrid(); BK = BufGrid(); BV = BufGrid()
    for t in range(NTL):
        p.dma("pool", QT[:, t * 512:(t + 1) * 512], cm[768:832, t * 512:(t + 1) * 512], reads=[dB[("cm", 6, t)]], writes=[BQ[t]])
        p.dma("pool", KT[:, t * 512:(t + 1) * 512], cm[832:896, t * 512:(t + 1) * 512], reads=[dB[("cm", 7, t)]], writes=[BK[t]])
        p.dma("pool", V[:, t * 4:(t + 1) * 4, :], tm[t * 512:(t + 1) * 512, 128:192].rearrange("(b p) d -> p b d", p=128),
              reads=[dB[("tm", t)]], writes=[BV[t]])
    bias = p.sb("swa_bias", [128, 256], F32); Bbias = Buf()
    p.dma("sp", bias[:], swa_bias_d, writes=[Bbias])
    es = p.sb("swa_es", [128, 1], F32); Bes = Buf()
    p.act.activation(out=es[:], in_=scal_sb[:, 0:1], func=AF.Exp, reads=[Bpar], writes=[Bes])
    ones64 = c.ones_bf[:, 0:64]
    tmp = [p.sb("swa_tmp%d" % i, [128, 256], F32) for i in range(2)]; Btmp = [Buf() for _ in range(2)]
    PT = [p.sb("swa_PT%d" % i, [128, 256], BF16) for i in range(3)]; BPT = [Buf() for _ in range(3)]
    den = [p.sb("swa_den%d" % i, [64, 512], F32) for i in range(2)]; Bden = [Buf() for _ in range(2)]
    yo = [p.sb("swa_yo%d" % i, [64, 512], F32) for i in range(2)]; Byo = [Buf() for _ in range(2)]
    prev = None
    po = pd = None
    for b in range(NB):
        qw = 256 if b < NB - 1 else 128
        t_q = [BQ[(b * 128) // 512], BQ[min((b * 128 + 128) // 512, NTL - 1)]]
        ps, pb = c.psum()
        mm(p, ps[:, 0:qw], KT[:, b * 128:(b + 1) * 128], QT[:, b * 128:b * 128 + qw], True, True, [BK[b // 4]] + t_q, [pb])
        tm_, tb = tmp[b % 2], Btmp[b % 2]
        p.dve.scalar_tensor_tensor(out=tm_[:, 0:qw], in0=ps[:, 0:qw], scalar=0.125, in1=bias[:, 0:qw],
                                                                           op0=ALU.mult, op1=ALU.add, reads=[pb, Bbias], writes=[tb])
        pt_, ptb = PT[b % 3], BPT[b % 3]
        p.act.activation(out=pt_[:, 0:qw], in_=tm_[:, 0:qw], func=AF.Exp, reads=[tb], writes=[ptb])
        j = b % 4
        if j == 0:
            po, pob = c.psum()
            pd, pdb = c.psum()
        if prev is not None:
            ppt, pptb, pb_ = prev
            mm(p, po[0:64, j * 128:(j + 1) * 128], V[:, b - 1, :], ppt[:, 128:256], True, False, [BV[(b - 1) // 4], pptb], [pob])
            mm(p, po[0:64, j * 128:(j + 1) * 128], V[:, b, :], pt_[:, 0:128], False, True, [BV[b // 4], ptb], [pob])
            mm(p, pd[0:64, j * 128:(j + 1) * 128], ones64, ppt[:, 128:256], True, False, [c.Bconst, pptb], [pdb])
            mm(p, pd[0:64, j * 128:(j + 1) * 128], ones64, pt_[:, 0:128], False, True, [c.Bconst, ptb], [pdb])
        else:
            mm(p, po[0:64, j * 128:(j + 1) * 128], V[:, b, :], pt_[:, 0:128], True, True, [BV[b // 4], ptb], [pob])
            mm(p, pd[0:64, j * 128:(j + 1) * 128], ones64, pt_[:, 0:128], True, True, [c.Bconst, ptb], [pdb])
        prev = (pt_, ptb, b)
        if j == 3:
            t = b // 4
            d_, db_ = den[t % 2], Bden[t % 2]
            y_, yb = yo[t % 2], Byo[t % 2]
            p.dve.tensor_scalar(out=d_[:], in0=pd[0:64, :], scalar1=es[0:64, 0:1], scalar2=None, op0=ALU.add,
                 reads=[pdb, Bes], writes=[db_])
            p.dve.reciprocal(out=d_[:], in_=d_[:], reads=[db_], writes=[db_])
            p.dve.tensor_tensor(out=y_[:], in0=po[0:64, :], in1=d_[:], op=ALU.mult,
                 reads=[pob, db_], writes=[yb])
            p.dma("sp", y_swa[:, t * 512:(t + 1) * 512], y_[:], reads=[yb], is_output=True)


def la_sb(c, S, cm, tm, dB, sbmask_d, scal_sb, Bpar, consts, y_sb):
    p = c.p
    NB = S // 128
    NTL = S // 512
    NSLOT = NTL // 2
    negtri, ident_bf, Bc2 = consts
    QT = p.sb("sb_QT", [128, S], BF16); KT = p.sb("sb_KT", [128, S], BF16)
    V = p.sb("sb_V", [128, NB, 128], BF16)
    BQ = BufGrid(); BK = BufGrid(); BV = BufGrid()
    for t in range(NTL):
        p.dma("pool", QT[:, t * 512:(t + 1) * 512], cm[512:640, t * 512:(t + 1) * 512], reads=[dB[("cm", 4, t)]], writes=[BQ[t]])
        p.dma("pool", KT[:, t * 512:(t + 1) * 512], cm[640:768, t * 512:(t + 1) * 512], reads=[dB[("cm", 5, t)]], writes=[BK[t]])
        p.dma("pool", V[:, t * 4:(t + 1) * 4, :], tm[t * 512:(t + 1) * 512, 0:128].rearrange("(b p) d -> p b d", p=128),
              reads=[dB[("tm", t)]], writes=[BV[t]])
    mask = p.sb("sb_mask", [128, 8, 512], BF16); Bmask = Buf()
    for r in range(8):
        p.dma("pool", mask[:, r, :], sbmask_d[r], writes=[Bmask])
    qs = [p.sb("sb_qs%d" % i, [128, 512], BF16) for i in range(2)]; Bqs = [Buf() for _ in range(2)]
    ebuf = [p.sb("sb_e%d" % i, [128, 512], F32) for i in range(2)]; Be = [Buf() for _ in range(2)]
    spb = [p.sb("sb_sp%d" % i, [128, 512], BF16) for i in range(3)]; Bsp = [Buf() for _ in range(3)]
    at = [p.sb("sb_at%d" % i, [128, 512], BF16) for i in range(2)]; Bat = [Buf() for _ in range(2)]
    lcum = [p.sb("sb_lc%d" % i, [128, 512], BF16) for i in range(2)]; Blc = [Buf() for _ in range(2)]
    yo = [p.sb("sb_yo%d" % i, [128, 512], F32) for i in range(2)]; Byo = [Buf() for _ in range(2)]
    negones = p.sb("sb_negones", [128, 128], BF16)
    p.pool.memset(negones[:], -1.0, writes=[Bc2])
    scale = 128.0 ** -0.5
    at3 = [p.sb("sb_at3_%d" % i, [128, 512], BF16) for i in range(3)]; Bat3 = [Buf() for _ in range(3)]
    lc4 = [p.sb("sb_lc4_%d" % i, [128, 512], BF16) for i in range(4)]; Blc4 = [Buf() for _ in range(4)]
    blks = []
    for j in range(NSLOT):
        nmax = 8 * j + 7
        for n in range(nmax, -1, -1):
            blks.append((j, n, n == nmax, n == 0))
    NBLK = len(blks)
    st = {}
    slot_po = {}

    def stageA(i):
        j, n, first, last = blks[i]
        q_, qb = qs[j % 2], Bqs[j % 2]
        if first:
            p.dve.tensor_scalar(out=q_[:], in0=QT[:, (2 * j) * 512:(2 * j + 1) * 512], scalar1=scal_sb[:, 3:4],
                                scalar2=None, op0=ALU.mult, reads=[BQ[2 * j], Bpar], writes=[qb])
            p.dve.scalar_tensor_tensor(out=q_[:], in0=QT[:, (2 * j + 1) * 512:(2 * j + 2) * 512], scalar=scal_sb[:, 4:5],
                                       in1=q_[:], op0=ALU.mult, op1=ALU.add, reads=[BQ[2 * j + 1], Bpar, qb], writes=[qb])
            p.dve.tensor_scalar(out=q_[:], in0=q_[:], scalar1=scale, scalar2=None, op0=ALU.mult, reads=[qb], writes=[qb])
            slot_po[j] = c.psum_long()
        r = n - 8 * j
        pz, pzb = c.psum()
        mm(p, pz[:, :], KT[:, n * 128:(n + 1) * 128], q_[:], True, False, [BK[n // 4], qb], [pzb])
        if r >= 0:
            mm(p, pz[:, :], ident_bf[:], mask[:, r, :], False, False, [Bc2, Bmask], [pzb])
        e_, eb = ebuf[i % 2], Be[i % 2]
        p.act.activation(out=e_[:], in_=pz[:, :], func=AF.Exp, reads=[pzb], writes=[eb])
        s_, sb_ = spb[i % 3], Bsp[i % 3]
        p.act.activation(out=s_[:], in_=e_[:], func=AF.Ln, bias=c.one_col[:, 0:1], reads=[eb, c.Bconst], writes=[sb_])
        if not last:
            lc_, lcb = lc4[i % 4], Blc4[i % 4]
            if first:
                getattr(p, LA_EW).tensor_copy(out=lc_[:], in_=s_[:], reads=[sb_], writes=[lcb])
            else:
                getattr(p, LA_EW).tensor_tensor(out=lc_[:], in0=lc4[(i - 1) % 4][:], in1=s_[:], op=ALU.add,
                                     reads=[sb_, Blc4[(i - 1) % 4]], writes=[lcb])
        st[i] = (pz, pzb)

    def stageB(i):
        j, n, first, last = blks[i]
        pz, pzb = st[i]
        s_, sb_ = spb[i % 3], Bsp[i % 3]
        mm(p, pz[:, :], negtri[:], s_[:], False, first, [Bc2, sb_], [pzb])
        if not first:
            mm(p, pz[:, :], negones[:], lc4[(i - 1) % 4][:], False, True, [Bc2, Blc4[(i - 1) % 4]], [pzb])
        a_, ab = at3[i % 3], Bat3[i % 3]
        p.act.activation(out=a_[:], in_=pz[:, :], func=AF.Exp, reads=[pzb], writes=[ab])

    def stageC(i):
        j, n, first, last = blks[i]
        po, pob = slot_po[j]
        a_, ab = at3[i % 3], Bat3[i % 3]
        mm(p, po[:, :], V[:, n, :], a_[:], first, last, [BV[n // 4], ab], [pob])
        del st[i]
        if last:
            y_, yb = yo[j % 2], Byo[j % 2]
            p.dve.tensor_copy(out=y_[:], in_=po[:, :], reads=[pob], writes=[yb])
            p.dma("sp", y_sb[:, j * 512:(j + 1) * 512], y_[:], reads=[yb], is_output=True)

    for i in range(NBLK + 2):
        if i < NBLK:
            stageA(i)
        if 0 <= i - 1 < NBLK:
            stageB(i - 1)
        if 0 <= i - 2 < NBLK:
            stageC(i - 2)


def la_dn(c, S, cm, tm, dB, dnconv_sb, scal_sb, Bpar, cst, y_dn):
    p = c.p
    NCH = S // 128
    NTL = S // 512
    ident, triu, mneg, mup, Bc = cst["ident_f"], cst["triu"], cst["mneg"], cst["mup"], cst["B"]
    AL = ALU

    ab = p.sb("dn_ab", [128, NCH, 2], F32); Bab = Buf()
    for t in range(NTL):
        p.dma("sp", ab[:, t * 4:(t + 1) * 4, :], tm[t * 512:(t + 1) * 512, 192:194].rearrange("(n p) c -> p n c", p=128),
              reads=[dB[("tm", t)]], writes=[Bab])
    def colt(name):
        return p.sb("dn_" + name, [128, NCH], F32)
    gcol = colt("gcol"); bcol = colt("bcol"); gccol = colt("gccol"); glast = colt("glast"); bgcol = colt("bgcol"); kdcol = colt("kdcol")
    na = p.sb("dn_na", [128, 1], F32)
    Bs = Buf()
    p.act.activation(out=na[:], in_=scal_sb[:, 1:2], func=AF.Exp, reads=[Bpar], writes=[Bs])
    p.dve.tensor_scalar(out=na[:], in0=na[:], scalar1=-1.0, scalar2=None, op0=AL.mult, reads=[Bs], writes=[Bs])
    p.act.activation(out=gcol[:], in_=ab[:, :, 0], func=AF.Exp, bias=scal_sb[:, 2:3], reads=[Bab, Bpar, Bs], writes=[Bs])
    p.act.activation(out=gcol[:], in_=gcol[:], func=AF.Ln, bias=c.one_col[:, 0:1], reads=[Bs, c.Bconst], writes=[Bs])
    p.dve.tensor_scalar(out=gcol[:], in0=gcol[:], scalar1=na[:, 0:1], scalar2=None, op0=AL.mult, reads=[Bs], writes=[Bs])
    p.act.activation(out=bcol[:], in_=ab[:, :, 1], func=AF.Exp, scale=-1.0, reads=[Bab, Bs], writes=[Bs])
    p.dve.tensor_scalar(out=bcol[:], in0=bcol[:], scalar1=1.0, scalar2=None, op0=AL.add, reads=[Bs], writes=[Bs])
    p.dve.reciprocal(out=bcol[:], in_=bcol[:], reads=[Bs], writes=[Bs])
    ps, pb = c.psum()
    mm(p, ps[:, 0:NCH], triu[:], gcol[:], True, True, [Bc, Bs], [pb])
    p.act.activation(out=gccol[:], in_=ps[:, 0:NCH], func=AF.Copy, reads=[pb, Bs], writes=[Bs])
    ps2, pb2 = c.psum()
    mm(p, ps2[:, 0:NCH], c.ones_f[:], gcol[:], True, True, [c.Bconst, Bs], [pb2])
    p.act.activation(out=glast[:], in_=ps2[:, 0:NCH], func=AF.Copy, reads=[pb2, Bs], writes=[Bs])
    p.dve.tensor_tensor(out=kdcol[:], in0=glast[:], in1=gccol[:], op=AL.subtract, reads=[Bs], writes=[Bs])
    p.act.activation(out=kdcol[:], in_=kdcol[:], func=AF.Exp, reads=[Bs], writes=[Bs])
    p.act.activation(out=glast[:], in_=glast[:], func=AF.Exp, reads=[Bs], writes=[Bs])
    p.act.activation(out=bgcol[:], in_=gccol[:], func=AF.Exp, reads=[Bs], writes=[Bs])
    p.dve.tensor_tensor(out=bgcol[:], in0=bgcol[:], in1=bcol[:], op=AL.mult, reads=[Bs], writes=[Bs])

    def f32t(name, w=128):
        return p.sb("dn_" + name, [128, w], F32)
    raw = [[f32t("raw%d_%d" % (w, i), 515) for i in range(2)] for w in range(3)]; Braw = [[Buf() for _ in range(2)] for _ in range(3)]
    cv = [[f32t("cv%d_%d" % (w, i), 512) for i in range(2)] for w in range(3)]; Bcv = [[Buf() for _ in range(2)] for _ in range(3)]
    sqt = f32t("sqt", 512); Bsqt = Buf()
    rn = f32t("rn", 512); Brn = Buf()
    zt = [f32t("zt%d" % i, 512) for i in range(2)]; Bzt = [Buf() for _ in range(2)]
    ot = [f32t("ot%d" % i, 512) for i in range(2)]; Bot = [BufGrid() for _ in range(2)]
    yt = [f32t("yt%d" % i, 512) for i in range(2)]; Byt = [Buf() for _ in range(2)]
    NSET = 4
    names = ["Ktm", "bV", "bgK", "kdec", "Gb", "Gs", "Gu", "grow", "QgT", "N", "X0", "db", "X0T", "Pa", "Pb", "Xa", "Xb", "XTa", "XTb",
             "wT", "u", "QK", "vnew"]
    sets = [{nm: f32t("%s_%d" % (nm, i)) for nm in names} for i in range(NSET)]
    Bsets = [{nm: Buf() for nm in names} for i in range(NSET)]
    Sst = [f32t("S%d" % i) for i in range(2)]; BS = [Buf() for _ in range(2)]
    p.pool.memset(Sst[0][:], 0.0, writes=[BS[0]])

    def evac(eng, out, ps, pb, ob, extra_reads=()):
        if eng == "act":
            p.act.activation(out=out, in_=ps, func=AF.Copy, reads=[pb] + list(extra_reads), writes=[ob])
        else:
            p.dve.tensor_copy(out=out, in_=ps, reads=[pb] + list(extra_reads), writes=[ob])

    def pre(n, tl):
        cc = n % 4
        s_, B_ = sets[n % NSET], Bsets[n % NSET]
        sl = slice(cc * 128, (cc + 1) * 128)
        QT, KT, Vc = cv[0][tl % 2], cv[1][tl % 2], cv[2][tl % 2]
        BQ, BK, BV = Bcv[0][tl % 2], Bcv[1][tl % 2], Bcv[2][tl % 2]
        col = slice(n, n + 1)
        yield
        ps, pb = c.psum()
        p.pe.transpose(ps[:, 0:128], KT[:, sl], ident[:], reads=[BK, Bc], writes=[pb])
        evac("act", s_["Ktm"][:], ps[:, 0:128], pb, B_["Ktm"])
        if DN_SUB == 1:
            return
        yield
        ps, pb = c.psum()
        p.pe.transpose(ps[:, 0:128], Vc[:, sl], ident[:], reads=[BV, Bc], writes=[pb])
        p.dve.tensor_scalar(out=s_["bV"][:], in0=ps[:, 0:128], scalar1=bcol[:, col], scalar2=None, op0=AL.mult,
             reads=[pb, Bs], writes=[B_["bV"]])
        if DN_SUB == 2:
            return
        getattr(p, LA_EW).tensor_scalar(out=s_["bgK"][:], in0=s_["Ktm"][:], scalar1=bgcol[:, col], scalar2=None, op0=AL.mult,
             reads=[B_["Ktm"], Bs], writes=[B_["bgK"]])
        getattr(p, LA_EW).tensor_scalar(out=s_["kdec"][:], in0=s_["Ktm"][:], scalar1=kdcol[:, col], scalar2=None, op0=AL.mult,
             reads=[B_["Ktm"], Bs], writes=[B_["kdec"]])
        getattr(p, LA_EW).tensor_scalar(out=s_["Gb"][:], in0=c.ones_f[:], scalar1=gcol[:, col], scalar2=None, op0=AL.mult,
             reads=[c.Bconst, Bs], writes=[B_["Gb"]])
        if DN_SUB == 3:
            return
        yield
        pg, pgb = c.psum()
        mm(p, pg[:, 0:128], s_["Gb"][:], triu[:], True, True, [B_["Gb"], Bc], [pgb])
        if DN_SUB == 4:
            return
        p.dve.scalar_tensor_tensor(out=s_["Gs"][:], in0=pg[:, 0:128], scalar=gccol[:, col], in1=mneg[:], op0=AL.subtract, op1=AL.max,
             reads=[pgb, Bs, Bc], writes=[B_["Gs"]])
        p.act.activation(out=s_["Gs"][:], in_=s_["Gs"][:], func=AF.Exp, scale=-1.0, reads=[B_["Gs"]], writes=[B_["Gs"]])
        p.dve.scalar_tensor_tensor(out=s_["Gu"][:], in0=pg[:, 0:128], scalar=gccol[:, col], in1=mup[:], op0=AL.subtract, op1=AL.min,
             reads=[pgb, Bs, Bc], writes=[B_["Gu"]])
        p.act.activation(out=s_["Gu"][:], in_=s_["Gu"][:], func=AF.Exp, reads=[B_["Gu"]], writes=[B_["Gu"]])
        if DN_SUB == 5:
            return
        p.act.activation(out=s_["grow"][:], in_=pg[:, 0:128], func=AF.Exp, reads=[pgb], writes=[B_["grow"]])
        getattr(p, LA_EW).tensor_tensor(out=s_["QgT"][:], in0=QT[:, sl], in1=s_["grow"][:], op=AL.mult,
             reads=[BQ, B_["grow"]], writes=[B_["QgT"]])
        if DN_SUB == 6:
            return
        yield
        ps, pb = c.psum()
        mm(p, ps[:, 0:128], KT[:, sl], KT[:, sl], True, True, [BK], [pb])
        if DN_SUB == 61:
            return
        p.dve.tensor_tensor(out=s_["N"][:], in0=ps[:, 0:128], in1=s_["Gs"][:], op=AL.mult, reads=[pb, B_["Gs"]], writes=[B_["N"]])
        if DN_SUB == 62:
            return
        getattr(p, LA_EW).tensor_scalar(out=s_["X0"][:], in0=s_["N"][:], scalar1=bcol[:, col], scalar2=None, op0=AL.mult,
             reads=[B_["N"], Bs], writes=[B_["X0"]])
        getattr(p, LA_EW).tensor_scalar(out=s_["db"][:], in0=ident[:], scalar1=bcol[:, col], scalar2=None, op0=AL.mult,
             reads=[Bc, Bs], writes=[B_["db"]])
        if DN_SUB == 63:
            return
        yield
        ps, pb = c.psum()
        mm(p, ps[:, 0:128], s_["N"][:], s_["db"][:], True, True, [B_["N"], B_["db"]], [pb])
        if DN_SUB == 64:
            return
        evac("act", s_["X0T"][:], ps[:, 0:128], pb, B_["X0T"])
        if DN_SUB == 65:
            return
        p.dve.tensor_tensor(out=s_["Pa"][:], in0=ident[:], in1=ps[:, 0:128], op=AL.subtract, reads=[pb, Bc], writes=[B_["Pa"]])
        if DN_SUB == 7:
            return
        X, XT, P = "X0", "X0T", "Pa"
        for k in range(1, 7):
            Xn = "Xa" if k % 2 == 1 else "Xb"
            XTn = "XTa" if k % 2 == 1 else "XTb"
            Pn = "Pb" if P == "Pa" else "Pa"
            yield
            ps, pb = c.psum()
            mm(p, ps[:, 0:128], s_[XT][:], s_[X][:], True, True, [B_[XT], B_[X]], [pb])
            evac("act", s_[Xn][:], ps[:, 0:128], pb, B_[Xn])
            if k < 6:
                yield
                ps2, pb2 = c.psum()
                mm(p, ps2[:, 0:128], s_[X][:], s_[XT][:], True, True, [B_[XT], B_[X]], [pb2])
                evac("dve", s_[XTn][:], ps2[:, 0:128], pb2, B_[XTn])
            yield
            ps3, pb3 = c.psum()
            mm(p, ps3[:, 0:128], s_[Xn][:], s_[P][:], True, True, [B_[Xn], B_[P]], [pb3])
            p.dve.tensor_tensor(out=s_[Pn][:], in0=s_[P][:], in1=ps3[:, 0:128], op=AL.add,
                 reads=[pb3, B_[P]], writes=[B_[Pn]])
            X, XT, P = Xn, XTn, Pn
            if DN_SUB == 70 + k:
                return
        yield
        ps, pb = c.psum()
        mm(p, ps[:, 0:128], s_["bgK"][:], s_[P][:], True, True, [B_["bgK"], B_[P]], [pb])
        evac("act", s_["wT"][:], ps[:, 0:128], pb, B_["wT"])
        yield
        ps, pb = c.psum()
        mm(p, ps[:, 0:128], s_[P][:], s_["bV"][:], True, True, [B_["bV"], B_[P]], [pb])
        evac("dve", s_["u"][:], ps[:, 0:128], pb, B_["u"])
        yield
        ps, pb = c.psum()
        mm(p, ps[:, 0:128], KT[:, sl], QT[:, sl], True, True, [BK, BQ], [pb])
        p.dve.tensor_tensor(out=s_["QK"][:], in0=ps[:, 0:128], in1=s_["Gu"][:], op=AL.mult, reads=[pb, B_["Gu"]], writes=[B_["QK"]])

    def seq(n, tl):
        cc = n % 4
        s_, B_ = sets[n % NSET], Bsets[n % NSET]
        S0, BS0 = Sst[n % 2], BS[n % 2]
        S1, BS1 = Sst[(n + 1) % 2], BS[(n + 1) % 2]
        col = slice(n, n + 1)
        yield
        ps, pb = c.psum()
        mm(p, ps[:, 0:128], s_["wT"][:], S0[:], True, True, [B_["wT"], BS0], [pb])
        p.dve.tensor_tensor(out=s_["vnew"][:], in0=s_["u"][:], in1=ps[:, 0:128], op=AL.subtract, reads=[pb, B_["u"]], writes=[B_["vnew"]])
        yield
        po, pob = c.psum()
        mm(p, po[:, 0:128], S0[:], s_["QgT"][:], True, False, [BS0, B_["QgT"]], [pob])
        mm(p, po[:, 0:128], s_["vnew"][:], s_["QK"][:], False, True, [B_["vnew"], B_["QK"]], [pob])
        o_, ob = ot[tl % 2], Bot[tl % 2][cc]
        p.act.activation(out=o_[:, cc * 128:(cc + 1) * 128], in_=po[:, 0:128], func=AF.Copy, reads=[pob], writes=[ob])
        yield
        pk, pkb = c.psum()
        mm(p, pk[:, 0:128], s_["kdec"][:], s_["vnew"][:], True, True, [B_["kdec"], B_["vnew"]], [pkb])
        p.dve.scalar_tensor_tensor(out=S1[:], in0=S0[:], scalar=glast[:, col], in1=pk[:, 0:128], op0=AL.mult, op1=AL.add,
             reads=[pkb, BS0, Bs], writes=[BS1])

    def tile_front(tl):
        i2 = tl % 2
        for w in range(3):
            r_, rb = raw[w][i2], Braw[w][i2]
            if tl == 0:
                p.pool.memset(r_[:, 0:3], 0.0, writes=[rb])
                p.dma("sp", r_[:, 3:515], cm[w * 128:(w + 1) * 128, 0:512], reads=[dB[("cm", w, 0)]], writes=[rb])
            else:
                p.dma("sp", r_[:, 0:515], cm[w * 128:(w + 1) * 128, tl * 512 - 3:(tl + 1) * 512],
                      reads=[dB[("cm", w, tl)], dB[("cm", w, tl - 1)]], writes=[rb])
            c_, cb = cv[w][i2], Bcv[w][i2]
            eng = "dve" if w != 1 else "pool"
            if eng == "dve":
                p.dve.tensor_scalar(out=c_[:], in0=r_[:, 0:512], scalar1=dnconv_sb[:, w:w + 1], scalar2=None, op0=AL.mult,
                     reads=[rb, Bpar], writes=[cb])
                for i in range(1, 4):
                    p.dve.scalar_tensor_tensor(out=c_[:], in0=r_[:, i:i + 512], scalar=dnconv_sb[:, i * 3 + w:i * 3 + w + 1],
                                                                                       in1=c_[:], op0=AL.mult, op1=AL.add, reads=[rb, Bpar, cb], writes=[cb])
            else:
                getattr(p, LA_EW).tensor_scalar(out=c_[:], in0=r_[:, 0:512], scalar1=dnconv_sb[:, w:w + 1], scalar2=None, op0=AL.mult,
                     reads=[rb, Bpar], writes=[cb])
                for i in range(1, 4):
                    p.dve.scalar_tensor_tensor(out=c_[:], in0=r_[:, i:i + 512], scalar=dnconv_sb[:, i * 3 + w:i * 3 + w + 1],
                                                                                       in1=c_[:], op0=AL.mult, op1=AL.add, reads=[rb, Bpar, cb], writes=[cb])
            p.act.activation(out=c_[:], in_=c_[:], func=AF.Silu, reads=[cb], writes=[cb])
            if w < 2:
                p.act.activation(out=sqt[:], in_=c_[:], func=AF.Square, reads=[cb], writes=[Bsqt])
                ps, pb = c.psum()
                mm(p, ps[:, :], c.ones_f[:], sqt[:], True, True, [Bsqt, c.Bconst], [pb])
                p.act.activation(out=rn[:], in_=ps[:, :], func=AF.Sqrt, bias=c.eps_col[:, 0:1], reads=[pb, c.Bconst], writes=[Brn])
                p.dve.reciprocal(out=rn[:], in_=rn[:], reads=[Brn], writes=[Brn])
                sc = 128.0 ** -0.5 if w == 0 else 1.0
                p.dve.scalar_tensor_tensor(out=c_[:], in0=c_[:], scalar=sc, in1=rn[:], op0=AL.mult, op1=AL.mult,
                     reads=[cb, Brn], writes=[cb])
        z_, zb = zt[i2], Bzt[i2]
        p.dma("sp", z_[:], cm[384:512, tl * 512:(tl + 1) * 512], reads=[dB[("cm", 3, tl)]], writes=[zb])
        p.act.activation(out=z_[:], in_=z_[:], func=AF.Silu, reads=[zb], writes=[zb])

    def tile_back(tl):
        i2 = tl % 2
        o_ = ot[i2]
        allo = [Bot[i2][cc] for cc in range(4)]
        p.act.activation(out=sqt[:], in_=o_[:], func=AF.Square, reads=allo, writes=[Bsqt])
        ps, pb = c.psum()
        mm(p, ps[:, :], c.ones_f[:], sqt[:], True, True, [Bsqt, c.Bconst], [pb])
        p.act.activation(out=rn[:], in_=ps[:, :], func=AF.Sqrt, scale=1.0 / 128.0, bias=c.eps_col[:, 0:1],
             reads=[pb, c.Bconst], writes=[Brn])
        p.dve.reciprocal(out=rn[:], in_=rn[:], reads=[Brn], writes=[Brn])
        y_, yb = yt[i2], Byt[i2]
        p.dve.scalar_tensor_tensor(out=y_[:], in0=o_[:], scalar=scal_sb[:, 5:6], in1=rn[:], op0=AL.mult, op1=AL.mult,
             reads=allo + [Bpar, Brn], writes=[yb])
        getattr(p, LA_EW).tensor_tensor(out=y_[:], in0=y_[:], in1=zt[i2][:], op=AL.mult, reads=[yb, Bzt[i2]], writes=[yb])
        p.dma("sp", y_dn[:, tl * 512:(tl + 1) * 512], y_[:], reads=[yb], is_output=True)

    def rr(gens):
        gens = list(gens)
        while gens:
            for g in list(gens):
                try:
                    next(g)
                except StopIteration:
                    gens.remove(g)

    def seqpair(n0):
        yield from seq(n0, n0 // 4)
        yield from seq(n0 + 1, (n0 + 1) // 4)

    tile_front(0)
    rr([pre(0, 0), pre(1, 0)])
    for m in range(NCH // 2):
        n0 = 2 * m
        gl = []
        if n0 + 2 < NCH:
            tl1 = (n0 + 2) // 4
            if (n0 + 2) % 4 == 0:
                tile_front(tl1)
            gl += [pre(n0 + 2, tl1), pre(n0 + 3, tl1)]
        gl.append(seqpair(n0))
        rr(gl)
        if (n0 + 1) % 4 == 3:
            tile_back(n0 // 4)


def build_LA(S, phases="pwsd"):
    nc = bass.Bass("TRN2", target_bir_lowering=False)
    NSLOT = S // 1024

    def din(name, shape):
        return nc.dram_tensor(name, list(shape), F32, kind="ExternalInput").ap()

    hT = din("hT", [D, S]); wcm_d = din("wcm", [D, 896]); wtm_d = din("wtm", [D, 194])
    swa_bias_d = din("swa_bias", [128, 256]); sbmask_d = din("sbmask", [8, 128, 512])
    scal_d = din("scal", [128, 8]); dnconv_d = din("dnconv", [128, 12]); gmix_d = din("gmix", [128, 16])
    cst_d = din("cst", [128, 4, 128])
    y_dn = nc.dram_tensor("y_dn", [128, S], F32, kind="ExternalOutput").ap()
    y_swa = nc.dram_tensor("y_swa", [64, S], F32, kind="ExternalOutput").ap()
    y_sb = nc.dram_tensor("y_sb", [128, NSLOT * 512], F32, kind="ExternalOutput").ap()
    cm = nc.dram_tensor("cm", [896, S], F32, kind="Internal").ap()
    tm = nc.dram_tensor("tm", [S, 194], F32, kind="Internal").ap()
    with ExitStack() as es:
        p = Prog(nc, es)
        c = Ctx(p, n_wbuf=0)
        scal_sb = p.sb("scal_sb", [128, 8], F32); dnconv_sb = p.sb("dnconv_sb", [128, 12], F32); gmix_sb = p.sb("gmix_sb", [128, 16], F32)
        cst_sb = p.sb("cst_sb", [128, 4, 128], F32)
        ident_bf = p.sb("ident_bf", [128, 128], BF16); negtri = p.sb("negtri", [128, 128], BF16)
        Bpar = Buf(); Bc = Buf()
        p.dma("sp", scal_sb[:], scal_d, writes=[Bpar])
        p.dma("sp", dnconv_sb[:], dnconv_d, writes=[Bpar])
        p.dma("sp", gmix_sb[:], gmix_d, writes=[Bpar])
        p.dma("sp", cst_sb[:], cst_d, writes=[Bc])
        p.dve.tensor_copy(out=ident_bf[:], in_=cst_sb[:, 0, :], reads=[Bc], writes=[Bc])
        p.dve.tensor_tensor(out=negtri[:], in0=cst_sb[:, 1, :], in1=cst_sb[:, 0, :], op=ALU.subtract, reads=[Bc], writes=[Bc])
        p.dve.tensor_scalar(out=negtri[:], in0=negtri[:], scalar1=-1.0, scalar2=None, op0=ALU.add, reads=[Bc], writes=[Bc])
        dB = BufGrid()
        cst = {"ident_f": cst_sb[:, 0, :], "triu": cst_sb[:, 1, :], "mneg": cst_sb[:, 2, :], "mup": cst_sb[:, 3, :], "B": Bc}
        if "p" in phases:
            with p.phase():
                la_proj(c, S, hT, wcm_d, wtm_d, gmix_sb, Bpar, cm, tm, dB)
        if "w" in phases:
            with p.phase():
                la_swa(c, S, cm, tm, dB, swa_bias_d, scal_sb, Bpar, y_swa)
        if "s" in phases:
            with p.phase():
                la_sb(c, S, cm, tm, dB, sbmask_d, scal_sb, Bpar, (negtri, ident_bf, Bc), y_sb)
        if "d" in phases:
            with p.phase():
                la_dn(c, S, cm, tm, dB, dnconv_sb, scal_sb, Bpar, cst, y_dn)
        p.finish()
    return nc


def la_consts():
    i = np.arange(128)
    ident = (i[:, None] == i[None, :]).astype(np.float32)
    triu = (i[:, None] <= i[None, :]).astype(np.float32)
    mneg = np.where(i[:, None] > i[None, :], 0.0, -NEG).astype(np.float32)
    mup = np.where(i[None, :] >= i[:, None], 0.0, NEG).astype(np.float32)
    return np.ascontiguousarray(np.stack([ident, triu, mneg, mup], axis=1))


def la_core_inputs(inp, l, cidx):
    f32 = np.float32
    w = np.asarray(inp["w_in"][l], f32)
    hd = cidx; kvh = cidx // 4; sbh = cidx // 2; pr = cidx % 2
    cols = []
    cols.append(w[:, hd * 128:(hd + 1) * 128])
    cols.append(w[:, 1024 + hd * 128:1024 + (hd + 1) * 128])
    cols.append(w[:, 2048 + hd * 128:2048 + (hd + 1) * 128])
    cols.append(w[:, 3072 + hd * 128:3072 + (hd + 1) * 128])
    cols.append(w[:, 4880 + sbh * 128:4880 + (sbh + 1) * 128])
    cols.append(w[:, 5392 + sbh * 128:5392 + (sbh + 1) * 128])
    cols.append(w[:, 4112 + hd * 64:4112 + (hd + 1) * 64])
    cols.append(w[:, 4624 + kvh * 64:4624 + (kvh + 1) * 64])
    wcm = np.ascontiguousarray(np.concatenate(cols, axis=1))
    wtm = np.ascontiguousarray(np.concatenate([w[:, 5904 + sbh * 128:5904 + (sbh + 1) * 128],
                                               w[:, 4752 + kvh * 64:4752 + (kvh + 1) * 64],
                                               w[:, 4096 + hd:4097 + hd], w[:, 4104 + hd:4105 + hd]], axis=1))
    kp = np.arange(128)[:, None]; qf = np.arange(256)[None, :]
    slope = 2.0 ** (-(cidx + 1.0))
    dist = (qf - kp).astype(f32)
    swa_bias = np.where((dist >= 0) & (dist <= 127), -slope * dist, NEG).astype(f32)
    sp = np.arange(128)[:, None]; q5 = np.arange(512)[None, :]
    sbmask = np.zeros((8, 128, 512), f32)
    for r in range(8):
        if pr == 0:
            sbmask[r] = np.where(r * 128 + sp < q5, 0.0, NEG) if r < 4 else NEG
        else:
            sbmask[r] = 0.0 if r < 4 else np.where((r - 4) * 128 + sp < q5, 0.0, NEG)
    scal = np.zeros((128, 8), f32)
    scal[:, 0] = inp["swa_sinks"][l][cidx]
    scal[:, 1] = inp["dn_a_log"][l][hd]
    scal[:, 2] = inp["dn_dt_bias"][l][hd]
    scal[:, 3] = 1.0 if pr == 0 else 0.0
    scal[:, 4] = 1.0 if pr == 1 else 0.0
    scal[:, 5] = np.asarray(inp["dn_norm"][l], f32)
    dc = np.asarray(inp["dn_conv"][l], f32)
    dnconv = np.zeros((128, 12), f32)
    for i in range(4):
        for wch in range(3):
            dnconv[:, i * 3 + wch] = dc[i, wch * 1024 + hd * 128:wch * 1024 + (hd + 1) * 128]
    return {"wcm": wcm, "wtm": wtm, "swa_bias": swa_bias, "sbmask": sbmask, "scal": scal, "dnconv": dnconv,
            "gmix": _pcol(inp["norm_mix"][l], 16), "cst": la_consts()}


SEQ = 16384
DEPTH = 4
_CACHE = {}


def _prog(key, fn):
    if key not in _CACHE:
        _CACHE[key] = fn()
    return _CACHE[key]


def run_model(inp, S, NPASS, TW):
    inp = {k: np.asarray(v) for k, v in inp.items()}
    hT = np.ascontiguousarray(inp["x"][0, :S].T.astype(np.float32))
    PT = S // NCORES // NPASS
    T = PT + 2
    for l in range(DEPTH):
        ncA = _prog(("LA", S), lambda: build_LA(S))
        maps = []
        for c in range(NCORES):
            m = la_core_inputs(inp, l, c)
            m["hT"] = hT
            maps.append(m)
        resA = run_bass_kernel_spmd(ncA, maps, core_ids=list(range(NCORES))).results
        yT = np.empty((D, S), np.float32)
        for c in range(NCORES):
            r = resA[c]
            yT[c * 128:(c + 1) * 128] = r["y_dn"]
            yT[1024 + c * 64:1024 + (c + 1) * 64] = r["y_swa"]
            pr, hh = c % 2, c // 2
            for j in range(S // 1024):
                I = 2 * j + pr
                yT[1536 + hh * 128:1536 + (hh + 1) * 128, I * 512:(I + 1) * 512] = r["y_sb"][:, j * 512:(j + 1) * 512]
        del resA, maps
        final = (l == DEPTH - 1)
        ncB = _prog(("LB", T, TW, final, NPASS), lambda: build_LB(T, TW, final, NPASS))
        w = lb_weights(inp, l)
        maps = []
        for c in range(NCORES):
            m = dict(w)
            hs = np.zeros((NPASS, D, T), np.float32)
            ys = np.zeros((NPASS, D, T), np.float32)
            fl = np.ones((NPASS, 128, 1), np.float32)
            for pi in range(NPASS):
                a = (c * NPASS + pi) * PT
                if a == 0:
                    hs[pi, :, 2:] = hT[:, 0:PT]
                    ys[pi, :, 2:] = yT[:, 0:PT]
                    fl[pi] = 0.0
                else:
                    hs[pi] = hT[:, a - 2:a + PT]
                    ys[pi] = yT[:, a - 2:a + PT]
            m["hT"] = hs; m["yT"] = ys; m["flag"] = fl
            maps.append(m)
        resB = run_bass_kernel_spmd(ncB, maps, core_ids=list(range(NCORES))).results
        hT = np.empty((D, S), np.float32)
        for c in range(NCORES):
            for pi in range(NPASS):
                a = (c * NPASS + pi) * PT
                hT[:, a:a + PT] = resB[c]["hout"][pi]
        del resB, maps
    return np.ascontiguousarray(hT.T)[None].astype(np.float32)


def kernel(**inp):
    return run_model(inp, SEQ, 2, 342)
```
